# Optimizing a Trainium2 kernel written in Bass

```python
import jax, jax.numpy as jnp
from jax import lax
import numpy as np

D_MODEL = 2048
BATCH = 4
SEQ = 2048
DEPTH = 1
DEC_BATCH = 128
DEC_SEQ = 8
PAST_LEN = 16384
PAGE_SIZE = 128

CHUNK = 128
A_WIDTH = 1024
A_GROUPS = 4
A_GROUP_DIM = A_WIDTH // A_GROUPS
B_WIDTH = 1024
CONV_W = 3
N_MEM = 256
X_HEADS = 4
X_HEAD_DIM = D_MODEL // X_HEADS
D_FF = 4 * D_MODEL
EPS = 1e-6
IN_WIDTHS = (A_WIDTH, A_WIDTH, B_WIDTH, B_WIDTH, B_WIDTH, D_MODEL, D_MODEL)
IN_WIDTH = sum(IN_WIDTHS)
IN_SPLITS = tuple(int(s) for s in np.cumsum(IN_WIDTHS)[:-1])

kernel_name = "gated_chunkmlp_shortconv_memxattn_step"


def rmsnorm(x, g):
    xf = x.astype(jnp.float32)
    r = lax.rsqrt(jnp.mean(xf * xf, axis=-1, keepdims=True) + EPS)
    return (xf * r).astype(x.dtype) * g


def layernorm(x, g, b):
    xf = x.astype(jnp.float32)
    mu = jnp.mean(xf, axis=-1, keepdims=True)
    var = jnp.mean(jnp.square(xf - mu), axis=-1, keepdims=True)
    return ((xf - mu) * lax.rsqrt(var + EPS)).astype(x.dtype) * g + b


def chunk_spatial_gate(u, v, w_s, b_s):
    n, t, _ = v.shape
    L = CHUNK if t % CHUNK == 0 else t
    nc = t // L
    mask = jnp.tril(jnp.ones((L, L), dtype=bool))
    ws = jnp.where(mask, w_s[:, :L, :L], 0).astype(v.dtype)
    vc = v.reshape(n, nc, L, A_GROUPS, A_GROUP_DIM)
    z = jnp.einsum("gts,bcsgd->bctgd", ws, vc) + b_s[:, :L].T[None, None, :, :, None]
    return u * z.reshape(n, t, A_WIDTH)


def causal_dwconv(p, prev, w):
    t = p.shape[1]
    xp = jnp.concatenate([prev, p], axis=1)
    y = sum(w[k] * xp[:, k:k + t] for k in range(CONV_W))
    return y, xp[:, t:]


def mem_kv(mem, g_mem, w_k, w_v):
    n = mem.shape[0]
    mn = rmsnorm(mem, g_mem)
    k = (mn @ w_k).reshape(n, N_MEM, X_HEADS, X_HEAD_DIM)
    v = (mn @ w_v).reshape(n, N_MEM, X_HEADS, X_HEAD_DIM)
    return k, v


def cross_attn(hn, k, v, w_q, w_xo):
    n, t, _ = hn.shape
    q = (hn @ w_q).reshape(n, t, X_HEADS, X_HEAD_DIM)
    s = jnp.einsum("bthd,bmhd->bhtm", q, k).astype(jnp.float32) * (X_HEAD_DIM ** -0.5)
    p = jax.nn.softmax(s, axis=-1).astype(v.dtype)
    o = jnp.einsum("bhtm,bmhd->bthd", p, v).reshape(n, t, D_MODEL)
    return o @ w_xo


def layer(x, conv_prev, k_mem, v_mem, norm_mix_g, w_in, ln_v_g, ln_v_b, w_spatial, b_spatial,
          conv_w, w_branch_a, w_branch_b, w_mix_out, norm_x_g, w_q, w_x_out, norm_mlp_g, w_up, w_down):
    xn = rmsnorm(x, norm_mix_g)
    u, v, bg, cg, xin, ga, gb = jnp.split(xn @ w_in, IN_SPLITS, axis=-1)
    v = layernorm(v, ln_v_g, ln_v_b)
    y_a = chunk_spatial_gate(u, v, w_spatial, b_spatial)
    conv, conv_state = causal_dwconv(cg * xin, conv_prev, conv_w)
    y_b = bg * conv
    merged = jax.nn.sigmoid(ga) * (y_a @ w_branch_a) + jax.nn.sigmoid(gb) * (y_b @ w_branch_b)
    h = x + merged @ w_mix_out
    h = h + cross_attn(rmsnorm(h, norm_x_g), k_mem, v_mem, w_q, w_x_out)
    h = h + jnp.square(jax.nn.relu(rmsnorm(h, norm_mlp_g) @ w_up)) @ w_down
    return h, v, conv_state


def setup_inputs(seed: int = 0) -> dict:
    key = jax.random.key(seed)
    ks = jax.random.split(key, 32)
    f32 = jnp.float32

    def nrm(k, shape, scale=1.0):
        return jax.random.normal(k, shape, f32) * scale

    def gain(k, shape):
        return 1.0 + 0.1 * jax.random.normal(k, shape, f32)

    return {
        "x_prompt": nrm(ks[0], (BATCH, SEQ, D_MODEL)),
        "x_sample": nrm(ks[1], (DEC_BATCH, DEC_SEQ, D_MODEL)),
        "state_conv": nrm(ks[2], (DEPTH, DEC_BATCH, CONV_W - 1, B_WIDTH)),
        "cache_mem_k": nrm(ks[3], (DEPTH, DEC_BATCH, N_MEM, X_HEADS, X_HEAD_DIM)),
        "cache_mem_v": nrm(ks[4], (DEPTH, DEC_BATCH, N_MEM, X_HEADS, X_HEAD_DIM)),
        "mem_prompt": nrm(ks[5], (BATCH, N_MEM, D_MODEL)),
        "norm_mix_g": gain(ks[6], (DEPTH, D_MODEL)),
        "w_in": nrm(ks[7], (DEPTH, D_MODEL, IN_WIDTH), D_MODEL ** -0.5),
        "ln_v_g": gain(ks[8], (DEPTH, A_WIDTH)),
        "ln_v_b": nrm(ks[9], (DEPTH, A_WIDTH), 0.02),
        "w_spatial": nrm(ks[10], (DEPTH, A_GROUPS, CHUNK, CHUNK), CHUNK ** -0.5),
        "b_spatial": gain(ks[11], (DEPTH, A_GROUPS, CHUNK)),
        "conv_w": nrm(ks[12], (DEPTH, CONV_W, B_WIDTH), CONV_W ** -0.5),
        "w_branch_a": nrm(ks[13], (DEPTH, A_WIDTH, D_MODEL), A_WIDTH ** -0.5),
        "w_branch_b": nrm(ks[14], (DEPTH, B_WIDTH, D_MODEL), B_WIDTH ** -0.5),
        "w_mix_out": nrm(ks[15], (DEPTH, D_MODEL, D_MODEL), D_MODEL ** -0.5),
        "norm_x_g": gain(ks[16], (DEPTH, D_MODEL)),
        "norm_mem_g": gain(ks[17], (DEPTH, D_MODEL)),
        "w_q": nrm(ks[18], (DEPTH, D_MODEL, D_MODEL), D_MODEL ** -0.5),
        "w_k": nrm(ks[19], (DEPTH, D_MODEL, D_MODEL), D_MODEL ** -0.5),
        "w_v": nrm(ks[20], (DEPTH, D_MODEL, D_MODEL), D_MODEL ** -0.5),
        "w_x_out": nrm(ks[21], (DEPTH, D_MODEL, D_MODEL), D_MODEL ** -0.5),
        "norm_mlp_g": gain(ks[22], (DEPTH, D_MODEL)),
        "w_up": nrm(ks[23], (DEPTH, D_MODEL, D_FF), D_MODEL ** -0.5),
        "w_down": nrm(ks[24], (DEPTH, D_FF, D_MODEL), D_FF ** -0.5),
        "norm_final_g": gain(ks[25], (D_MODEL,)),
    }


def reference(x_prompt, x_sample, state_conv, cache_mem_k, cache_mem_v, mem_prompt,
              norm_mix_g, w_in, ln_v_g, ln_v_b, w_spatial, b_spatial, conv_w,
              w_branch_a, w_branch_b, w_mix_out, norm_x_g, norm_mem_g, w_q, w_k, w_v,
              w_x_out, norm_mlp_g, w_up, w_down, norm_final_g):
    hp, hs = x_prompt, x_sample
    mem_k_list, mem_v_list, conv_p_list, conv_s_list, chunk_v_list = [], [], [], [], []
    for l in range(DEPTH):
        lp = (norm_mix_g[l], w_in[l], ln_v_g[l], ln_v_b[l], w_spatial[l], b_spatial[l], conv_w[l],
              w_branch_a[l], w_branch_b[l], w_mix_out[l], norm_x_g[l], w_q[l], w_x_out[l],
              norm_mlp_g[l], w_up[l], w_down[l])
        k_p, v_p = mem_kv(mem_prompt, norm_mem_g[l], w_k[l], w_v[l])
        zero_prev = jnp.zeros((hp.shape[0], CONV_W - 1, B_WIDTH), hp.dtype)
        hp, _, conv_p = layer(hp, zero_prev, k_p, v_p, *lp)
        hs, v_s, conv_s = layer(hs, state_conv[l], cache_mem_k[l], cache_mem_v[l], *lp)
        mem_k_list.append(k_p)
        mem_v_list.append(v_p)
        conv_p_list.append(conv_p)
        conv_s_list.append(conv_s)
        chunk_v_list.append(v_s)
    y_prompt = rmsnorm(hp, norm_final_g)
    y_sample = rmsnorm(hs, norm_final_g)
    mem_k_prompt = jnp.stack(mem_k_list, axis=0)
    mem_v_prompt = jnp.stack(mem_v_list, axis=0)
    conv_prompt = jnp.stack(conv_p_list, axis=0)
    conv_sample = jnp.stack(conv_s_list, axis=0)
    chunk_v_sample = jnp.stack(chunk_v_list, axis=0)
    return (y_prompt, y_sample, mem_k_prompt, mem_v_prompt, conv_prompt, conv_sample, chunk_v_sample)
```

```python
import numpy as np
from contextlib import ExitStack
import concourse.bass as bass
import concourse.mybir as mybir
from concourse.bass_utils import run_bass_kernel_spmd

F32 = mybir.dt.float32
BF16 = mybir.dt.bfloat16
AF = mybir.ActivationFunctionType
ALU = mybir.AluOpType

NCORES = 8
P = 128
D = 2048
KC = 16
NPB = 512
NS = 64
NSB = 66
T = NPB + NSB
NGROUPS = 2
AW = 1024
DFF = 8192
EPS = 1e-6
INW = 9216
OFF_U, OFF_V, OFF_B, OFF_C, OFF_XIN, OFF_GA, OFF_GB = 0, 1024, 2048, 3072, 4096, 5120, 7168
NBUF = 3
SCALE = 512 ** -0.5

DEBUG_DUMP = None


class Sched:
    def __init__(self, engines, ndma_slots):
        self.engines = list(engines)
        self.streams = {e: [] for e in engines}
        self.count = {e: 0 for e in engines}
        self.seen = {e: {} for e in engines}
        self.lastw = {}
        self.readers = {}
        self.dma_n = {e: 0 for e in ndma_slots}
        self.dma_slots = dict(ndma_slots)
        self.dma_uses = {}

    def _deps(self, reads, writes):
        deps = []
        for k in reads:
            t = self.lastw.get(k)
            if t is not None:
                deps.append(t)
        for k in writes:
            t = self.lastw.get(k)
            if t is not None:
                deps.append(t)
            deps.extend(self.readers.get(k, ()))
        return deps

    def _waits(self, eng, deps):
        need = {}
        for (s, v) in deps:
            if self.seen[eng].get(s, 0) < v and need.get(s, 0) < v:
                need[s] = v
        for s, v in need.items():
            self.seen[eng][s] = v
        return list(need.items())

    def _record(self, tok, reads, writes):
        for k in reads:
            self.readers.setdefault(k, []).append(tok)
        for k in writes:
            self.lastw[k] = tok
            self.readers[k] = []

    @staticmethod
    def _excl(reads, writes):
        r2, w2 = [], list(writes)
        for k in reads:
            if isinstance(k, tuple) and k[0] in ("accP", "accS", "misc") or k == "miscb":
                w2.append(k)
            else:
                r2.append(k)
        return r2, w2

    dry = False

    def op(self, eng, fn, reads=(), writes=()):
        if self.dry:
            return None
        reads, writes = self._excl(reads, writes)
        waits = self._waits(eng, self._deps(reads, writes))
        self.count[eng] += 1
        tok = (eng, self.count[eng])
        self.streams[eng].append((waits, fn, eng, 1))
        self._record(tok, reads, writes)
        return tok

    def dma(self, eng, fn, reads=(), writes=()):
        if self.dry:
            return None
        n = self.dma_n[eng]
        self.dma_n[eng] = n + 1
        slot = "%s_d%d" % (eng, n % self.dma_slots[eng])
        uses = self.dma_uses.get(slot, 0)
        deps = self._deps(reads, writes)
        if uses > 0:
            deps.append((slot, 16 * uses))
        waits = self._waits(eng, deps)
        self.dma_uses[slot] = uses + 1
        tok = (slot, 16 * (uses + 1))
        self.streams[eng].append((waits, fn, slot, 16))
        self._record(tok, reads, writes)
        return tok

    def alias(self, new_keys, old_keys):
        if self.dry:
            return
        toks = []
        for k in old_keys:
            t = self.lastw.get(k)
            if t is not None:
                toks.append(t)
            toks.extend(self.readers.get(k, ()))
        for k in new_keys:
            self.readers.setdefault(k, []).extend(toks)

    def sem_names(self):
        names = list(self.engines)
        for e, n in self.dma_slots.items():
            names += ["%s_d%d" % (e, i) for i in range(n)]
        return names


def build_program():
    nc = bass.Bass("TRN2", target_bir_lowering=False)

    def din(name, shape):
        return nc.dram_tensor(name, list(shape), F32, kind="ExternalInput").ap()

    def dout(name, shape):
        return nc.dram_tensor(name, list(shape), F32, kind="ExternalOutput").ap()

    xg = din("xg", [NGROUPS, T, D])
    sconv = din("sconv", [32, AW])
    ck = din("ck", [16, 256, D])
    cv = din("cv", [16, 256, D])
    mem = din("mem", [256, D])
    w_in = din("w_in", [D, INW])
    w_a = din("w_a", [AW, D])
    w_b = din("w_b", [AW, D])
    w_mix = din("w_mix", [D, D])
    w_q = din("w_q", [D, D])
    w_k = din("w_k", [D, D])
    w_v = din("w_v", [D, D])
    w_xo = din("w_xo", [D, D])
    w_up = din("w_up", [D, DFF])
    w_down = din("w_down", [DFF, D])
    gains = din("gains", [P, 5, KC])
    cwt = din("cwt", [P, 8, 3])
    lnv = din("lnv", [2, AW])
    wsp = din("wsp", [P, 4, P])
    wsb = din("wsb", [NS, 4, NS])
    bsp = din("bsp", [1, 4 * P])
    ident = din("ident", [P, P])
    tril = din("tril", [P, P])

    y_o = dout("y", [NGROUPS, NPB + NS, D])
    mk_o = dout("mk", [P, D])
    mv_o = dout("mv", [P, D])
    convp_o = dout("convp", [2, AW])
    convs_o = dout("convs", [32, AW])
    chunkv_o = dout("chunkv", [NGROUPS, NS, AW])
    dbg_o = dout("dbg", [KC, P, T]) if DEBUG_DUMP else None

    S = Sched(["pe", "act", "dve", "pool", "sp"], {"sp": 8, "pool": 10})

    es = ExitStack()
    with es:
        def sb(name, shape, dt=F32):
            return es.enter_context(nc.sbuf_tensor(name, list(shape), dt))

        def pst(name, shape, dt=F32):
            return es.enter_context(nc.psum_tensor(name, list(shape), dt))

        R = sb("R", [P, KC, T])
        X = sb("X", [P, KC, T], BF16)
        Y = sb("Y", [P, KC, T], BF16)
        M = sb("M", [P, KC, T], BF16)
        wbuf = sb("wbuf", [P, NBUF, KC, 512], BF16)
        KT = sb("KT", [P, KC, 256], BF16)
        Vbf = sb("Vbf", [P, 2, D], BF16)
        xt = sb("xt", [P, 2, 1024])
        sq = sb("sq", [P, 2, T], BF16)
        stat = sb("stat", [P, 32])
        scrA = sb("scrA", [P, 4, T])
        scrB = sb("scrB", [P, 3072])
        vraw = scrB[:, 0:AW]
        scrC = sb("scrC", [P, 2, T])
        pT = sb("pT", [P, 2, NPB], BF16)
        pT2 = sb("pT2", [P, 2, NPB], BF16)
        pTs = sb("pTs", [P, 2, 4, NS], BF16)
        rinvs = sb("rinvs", [P, 4, NS])
        vbf = M[:, :, :].rearrange("p k t -> p (k t)")[:, 0:5 * AW].rearrange("p (a b) -> p a b", a=5)
        G = scrA
        scrA_flat = scrA[:, :, :].rearrange("p a t -> p (a t)")
        lng_bc = scrA_flat[:, 0:AW]
        lnb_bc = scrA_flat[:, AW:2 * AW]
        rinv = scrA[:, 3, 0:NPB]
        Csb = scrB[:, 0:2 * T].rearrange("p (a t) -> p a t", a=2)
        convb = scrB[:, 2 * T:2 * T + 2 * (NPB + NS)].rearrange("p (a t) -> p a t", a=2)
        pbuf = scrB[:, 2308:2308 + 2 + NPB + 80]
        kb = scrB[:, 0:1024].bitcast(BF16).rearrange("p (a b c) -> p a b c", a=2, b=2)
        vb = scrB[:, 1024:2048].bitcast(BF16).rearrange("p (a b c) -> p a b c", a=2, b=2)
        kbT = scrB[:, 2048:3072].bitcast(BF16).rearrange("p (a b c) -> p a b c", a=2, b=4)
        usb = scrC[:, 0, :]
        rbc = scrC[:, 0, :]
        tz = scrC[:, 1, :]
        kst = scrC[:, :, 0:512]
        KSTKEY = ["usb", "tz"]
        VRAW = [scrB[:, 0:AW], scrB[:, AW:2 * AW]]
        VR_KEYS = ["vraw", ("vraw", 1)]
        STATV = [stat[:, 0:16], stat[:, 16:32]]
        STG = [xt[:, 0, :], xt[:, 1, :], scrB[:, 0:1024], scrB[:, 1024:2048], scrB[:, 2048:3072]]
        STG_KEYS = [("xt", 0), ("xt", 1), ("stg", 2), ("stg", 3), ("stg", 4)]
        XKEYS = [("X", k) for k in range(KC)]
        pend = {"fn": None}
        LNG_KEYS = [("G", 0), ("G", 1)]
        LNB_KEYS = [("G", 1), ("G", 2), ("G", 3)]
        PHA_KEYS = [("Csb", 0), ("Csb", 1), ("convb", 0), ("convb", 1), "pbuf"]
        ATT_KEYS = [("kb", 0), ("kb", 1), ("vb", 0), ("vb", 1), ("kbT", 0), ("kbT", 1), ("kbX", 0)]
        ident_f = sb("ident_f", [P, P])
        ident_b = sb("ident_b", [P, P], BF16)
        ones_b = sb("ones_b", [P, P], BF16)
        epsc = sb("epsc", [P, 1])
        gn = sb("gn", [P, 5, KC])
        cw = sb("cw", [P, 8, 3])
        bsp_bc = sb("bsp_bc", [P, 4, P])
        WsT = sb("WsT", [P, 4, P], BF16)
        WsTs = sb("WsTs", [NS, 4, NS], BF16)
        wst = vraw[:, 0:512].rearrange("p (g t) -> p g t", g=4)
        wsts = vraw[0:NS, 512:768].rearrange("p (g t) -> p g t", g=4)
        trilm = sb("trilm", [P, P])
        sconv_sb = xt[0:32, 0, :]
        sconvT = sb("sconvT", [P, 8, 32])
        pl = sb("pl", [P, 8, 2])
        pss = sb("pss", [P, 8, 16, 2])
        cst = xt[0:32, 1, :]

        accP = [pst("accP%d" % i, [P, NPB]) for i in range(4)]
        accS2 = [pst("accS%d" % i, [P, NPB]) for i in range(2)]
        misc = [pst("misc0", [P, 512])]

        def aS(i):
            return accS2[i % 2]

        miscb = pst("miscb", [P, 1024], BF16)
        stat_pb = misc[0][:, 0:NPB]
        stat_sb = miscb[:, 0:2 * NSB].bitcast(F32)
        STATKEYS = [("misc", 0), "miscb"]

        sem_names = S.sem_names()
        sems = {n: es.enter_context(nc.semaphore("s_" + n)) for n in sem_names}
        block = es.enter_context(nc.Block())

        st = {"acc": 0, "misc": 0, "ev": 0}

        def next_acc():
            i = st["acc"] % 4
            st["acc"] += 1
            return i

        def next_misc():
            return 0

        TRB = [(accP[i], ("accP", i)) for i in range(4)]

        def next_tr():
            st["tr"] = st.get("tr", 0) + 1
            return TRB[st["tr"] % len(TRB)]

        def ev_eng():
            st["ev"] += 1
            return "act" if st["ev"] % 2 else "dve"

        def load_const(dst, src, key, eng="sp"):
            S.dma(eng, lambda e, d=dst, s=src: e.dma_start(out=d, in_=s), writes=[key])

        def mm_group(i, terms_pb, terms_sb, reads, nsb=NSB):
            def fn(e):
                ins = None
                n = len(terms_pb)
                for j in range(n):
                    if terms_pb:
                        l, r = terms_pb[j]
                        ins = e.matmul(accP[i][:, :], lhsT=l, rhs=r, start=(j == 0), stop=(j == n - 1))
                    if terms_sb:
                        l, r = terms_sb[j]
                        ins = e.matmul(aS(i)[:, 0:nsb], lhsT=l, rhs=r, start=(j == 0), stop=(j == n - 1))
                return ins
            S.op("pe", fn, reads=reads, writes=acc_reads(i))

        def lin_chunk(src, srckey, wb, kcn, colsl):
            i = next_acc()
            tp = [(wb[:, k, colsl], src[:, k, 0:NPB]) for k in range(kcn)]
            ts = [(wb[:, k, colsl], src[:, k, NPB:T]) for k in range(kcn)]
            mm_group(i, tp, ts, reads=[srckey, "wbuf"])
            return i

        def acc_reads(i):
            return [("accP", i), ("accS", i % 2)]

        slab_state = {"n": 0}
        stages = []

        def wview(w, r0, nk, c0, ncol):
            return w[r0:r0 + nk * P, c0:c0 + ncol].rearrange("(k p) n -> p k n", p=P)

        all_stages = []
        mode = {"si": 0, "issued": 0, "half0": set()}

        def issue_load(idx, only_half=None):
            parts = all_stages[idx][0]
            b = idx % NBUF
            for pi, (kofs, nk, cofs, ncol, src) in enumerate(parts):
                if ncol == 512:
                    pieces = [(0, 256, 0), (256, 256, 1)]
                else:
                    pieces = [(0, ncol, pi if len(parts) == 2 else cofs // 256)]
                for (c0, nc_, half) in pieces:
                    if only_half is not None and half != only_half:
                        continue
                    if only_half is None and half == 0 and idx in mode["half0"]:
                        continue
                    S.dma("pool",
                          lambda e, b=b, kofs=kofs, nk=nk, cofs=cofs, c0=c0, nc_=nc_, src=src:
                          e.dma_start(out=wbuf[:, b, kofs:kofs + nk, cofs + c0:cofs + c0 + nc_],
                                      in_=src[:, :, c0:c0 + nc_]),
                          writes=[("wbuf", b, half)])

        def wk(wkey, colsl=None):
            if colsl is None:
                return [(wkey[0], wkey[1], 0), (wkey[0], wkey[1], 1)]
            return [(wkey[0], wkey[1], colsl.start // 256)]

        def ensure_loaded(upto):
            while mode["issued"] <= upto and mode["issued"] < len(all_stages):
                issue_load(mode["issued"])
                mode["issued"] += 1

        def add_stage(parts, fn, *flags):
            if S.dry:
                all_stages.append((parts, flags))
                return
            i = mode["si"]
            if i >= 1 and "defer" in all_stages[i - 1][1]:
                ensure_loaded(i)
            else:
                ensure_loaded(i + NBUF - 1)
            fn(wbuf[:, i % NBUF], ("wbuf", i % NBUF))
            ensure_loaded(i + NBUF - 1)
            j = i + NBUF
            flags_i = all_stages[i][1]
            if "defer" not in flags_i and "noahead" not in flags_i and j < len(all_stages) \
                    and j not in mode["half0"]:
                issue_load(j, only_half=0)
                mode["half0"].add(j)
            mode["si"] += 1

        def run_stages():
            pass

        def lin_chunk_b(src, srckey, wb, wkey, kcn, colsl, koff=0):
            i = next_acc()
            if st.get("split") and srckey is XKEYS:
                st["split"] = False
                for k in range(kcn):
                    def fk(e, k=k, i=i):
                        e.matmul(accP[i][:, :], lhsT=wb[:, koff + k, colsl], rhs=src[:, k, 0:NPB], start=(k == 0),
                                 stop=(k == kcn - 1))
                        return e.matmul(aS(i)[:, 0:NSB], lhsT=wb[:, koff + k, colsl], rhs=src[:, k, NPB:T],
                                        start=(k == 0), stop=(k == kcn - 1))
                    S.op("pe", fk, reads=[("X", k)] + wk(wkey, colsl), writes=acc_reads(i))
                return i
            tp = [(wb[:, koff + k, colsl], src[:, k, 0:NPB]) for k in range(kcn)]
            ts = [(wb[:, koff + k, colsl], src[:, k, NPB:T]) for k in range(kcn)]
            mm_group(i, tp, ts, reads=list(srckey) + wk(wkey, colsl))
            return i

        def ew(eng, fn, reads, writes):
            S.op(eng, fn, reads=reads, writes=writes)

        ENG = {}

        load_const(ident_f[:, :], ident[:, :], "ident_f")
        load_const(trilm[:, :], tril[:, :], "trilm")
        load_const(gn[:, :, :], gains[:, :, :], "gn")
        load_const(cw[:, :, :], cwt[:, :, :], "cw")
        load_const(wst, wsp[:, :, :], "vraw")
        load_const(wsts, wsb[:, :, :], "vraw")
        load_const(sconv_sb, sconv[:, :], ("xt", 0))
        S.dma("sp", lambda e: e.dma_start(out=bsp_bc[:, :, :].rearrange("p g t -> p (g t)").unsqueeze(1),
                                          in_=bsp[0:1, :].partition_broadcast(P)),
              writes=["bsp_bc"])
        ew("dve", lambda e: e.tensor_copy(out=ident_b[:, :], in_=ident_f[:, :]), ["ident_f"], ["ident_b"])
        ew("dve", lambda e: e.memset(ones_b[:, :], 1.0), [], ["ones_b"])
        ew("dve", lambda e: e.memset(epsc[:, :], EPS), [], ["epsc"])
        ew("dve", lambda e: e.memset(Y[:, :, :], 0.0), [], ["Y"])
        ew("dve", lambda e: e.tensor_tensor(out=wst[:, :, :], in0=wst[:, :, :],
                                            in1=trilm[:, None, :].to_broadcast([P, 4, P]), op=ALU.mult),
           ["vraw", "trilm"], ["vraw"])
        ew("dve", lambda e: e.tensor_tensor(out=wsts[:, :, :], in0=wsts[:, :, :],
                                            in1=trilm[0:NS, None, 0:NS].to_broadcast([NS, 4, NS]), op=ALU.mult),
           ["vraw", "trilm"], ["vraw"])
        mi = next_misc()

        def f_wst(e, mi=mi):
            ins = None
            for g in range(4):
                ins = e.transpose(out=misc[mi][:, g * P:(g + 1) * P], in_=wst[:, g, :], identity=ident_f[:, :])
            return ins
        S.op("pe", f_wst, reads=["vraw", "ident_f"], writes=[("misc", mi)])
        ew("dve", lambda e, mi=mi: e.tensor_copy(out=WsT[:, :, :].rearrange("p g t -> p (g t)"), in_=misc[mi][:, :]),
           [("misc", mi)], ["WsT"])
        mi = next_misc()

        def f_wsts(e, mi=mi):
            ins = None
            for g in range(4):
                ins = e.transpose(out=misc[mi][0:NS, g * NS:(g + 1) * NS], in_=wsts[:, g, :],
                                  identity=ident_f[0:NS, 0:NS])
            return ins
        S.op("pe", f_wsts, reads=["vraw", "ident_f"], writes=[("misc", mi)])
        ew("dve", lambda e, mi=mi: e.tensor_copy(out=WsTs[:, :, :].rearrange("p g t -> p (g t)"),
                                                  in_=misc[mi][0:NS, 0:4 * NS]),
           [("misc", mi)], ["WsTs"])
        mi = next_misc()

        def f_sc(e, mi=mi):
            ins = None
            for c in range(8):
                ins = e.transpose(out=misc[mi][:, c * 32:(c + 1) * 32], in_=sconv_sb[:, c * P:(c + 1) * P],
                                  identity=ident_f[0:32, 0:32])
            return ins
        S.op("pe", f_sc, reads=[("xt", 0), "ident_f"], writes=[("misc", mi)])
        ew("dve", lambda e, mi=mi: e.tensor_copy(out=sconvT[:, :, :].rearrange("p c b -> p (c b)"),
                                                  in_=misc[mi][:, 0:256]),
           [("misc", mi)], ["sconvT"])

        def to_feature_major(src_tile, rows, dst, dstkey, col0, k0, nk, srckey):
            for r0 in range(0, nk, 4):
                tb, tkey = next_tr()

                def f(e, tb=tb, r0=r0):
                    ins = None
                    for kk in range(4):
                        ins = e.transpose(out=tb[:, kk * P:kk * P + rows],
                                          in_=src_tile[0:rows, (r0 + kk) * P:(r0 + kk + 1) * P],
                                          identity=ident_f[0:rows, 0:rows])
                    return ins
                S.op("pe", f, reads=[srckey, "ident_f"], writes=[tkey])
                eng = ev_eng()
                src = tb[:, :].rearrange("p (k t) -> p k t", k=4)[:, :, 0:rows]
                out = dst[:, k0 + r0:k0 + r0 + 4, col0:col0 + rows]
                if eng == "act":
                    ew("act", lambda e, o=out, s=src: e.copy(out=o, in_=s), [tkey], [dstkey])
                else:
                    ew("dve", lambda e, o=out, s=src: e.tensor_copy(out=o, in_=s), [tkey], [dstkey])

        def stats_push(k):
            sl = st.get("sqn", 0) % 2
            st["sqn"] = st.get("sqn", 0) + 1
            ew("act", lambda e, k=k, sl=sl: e.activation(out=sq[:, sl, :], in_=R[:, k, :], func=AF.Square),
               ["R"], [("sq", sl)])

            def f(e, k=k, sl=sl):
                e.matmul(stat_pb, lhsT=ones_b[:, :], rhs=sq[:, sl, 0:NPB], start=(k == 0), stop=(k == KC - 1))
                return e.matmul(stat_sb, lhsT=ones_b[:, :], rhs=sq[:, sl, NPB:T], start=(k == 0), stop=(k == KC - 1))
            pend["fn"] = lambda: S.op("pe", f, reads=[("sq", sl), "ones_b"], writes=STATKEYS)

        def stats_flush():
            if pend["fn"] is not None:
                fn = pend["fn"]
                pend["fn"] = None
                fn()

        def rms_rbc():
            stats_flush()
            ew("act", lambda e: e.activation(out=rbc[:, 0:NPB], in_=stat_pb, func=AF.Sqrt,
                                             scale=1.0 / D, bias=epsc[:, 0:1]), STATKEYS + ["epsc"], ["usb"])
            ew("act", lambda e: e.activation(out=rbc[:, NPB:T], in_=stat_sb, func=AF.Sqrt,
                                             scale=1.0 / D, bias=epsc[:, 0:1]), STATKEYS + ["epsc"], ["usb"])
            ew("dve", lambda e: e.reciprocal(out=rbc[:, :], in_=rbc[:, :]), ["usb"], ["usb"])

        def rmsnorm_fm(gidx, dst, dstkey, fused=False):
            if not fused:
                for k in range(KC):
                    stats_flush()
                    stats_push(k)
            rms_rbc()
            for k in range(KC):
                ew("dve", lambda e, k=k: e.scalar_tensor_tensor(out=dst[:, k, :], in0=R[:, k, :],
                                                                 scalar=gn[:, gidx, k:k + 1], in1=rbc[:, :],
                                                                 op0=ALU.mult, op1=ALU.mult),
                   ["R", "usb", "gn"], [("X", k)])
            st["split"] = True

        def resid_add(i, k):
            ew("dve", lambda e, i=i, k=k: e.tensor_tensor(out=R[:, k, 0:NPB], in0=accP[i][:, :], in1=R[:, k, 0:NPB],
                                                           op=ALU.add),
               acc_reads(i) + ["R"], ["R"])
            ew("dve", lambda e, i=i, k=k: e.tensor_tensor(out=R[:, k, NPB:T], in0=aS(i)[:, 0:NSB],
                                                           in1=R[:, k, NPB:T], op=ALU.add),
               acc_reads(i) + ["R"], ["R"])

        def evac_copy(i, dst3, k, dstkey):
            eng = ev_eng()
            if eng == "act":
                ew("act", lambda e, i=i, k=k: e.copy(out=dst3[:, k, 0:NPB], in_=accP[i][:, :]), acc_reads(i), [dstkey])
                ew("act", lambda e, i=i, k=k: e.copy(out=dst3[:, k, NPB:T], in_=aS(i)[:, 0:NSB]), acc_reads(i),
                   [dstkey])
            else:
                ew("dve", lambda e, i=i, k=k: e.tensor_copy(out=dst3[:, k, 0:NPB], in_=accP[i][:, :]), acc_reads(i),
                   [dstkey])
                ew("dve", lambda e, i=i, k=k: e.tensor_copy(out=dst3[:, k, NPB:T], in_=aS(i)[:, 0:NSB]),
                   acc_reads(i), [dstkey])

        def group(g):
            S.alias(["R"], [("Rf", k) for k in range(KC)])
            S.alias(STG_KEYS[2:], ATT_KEYS + PHA_KEYS + VR_KEYS)
            if g == 0:
                mnT = M
                for mt in range(2):
                    for h in range(2):
                        s = (mt * 2 + h) % 2
                        S.dma("sp", lambda e, mt=mt, h=h, s=s: e.dma_start(
                            out=xt[:, s, :], in_=mem[mt * P:(mt + 1) * P, h * 1024:(h + 1) * 1024]),
                            writes=[("xt", s)])
                        ew("act", lambda e, s=s, mt=mt, h=h: e.activation(out=tz[:, 0:512], in_=xt[:, s, 0:512],
                                                                           func=AF.Square,
                                                                           accum_out=stat[:, 10 + 2 * h:11 + 2 * h]),
                           [("xt", s)], ["tz", ("stat", 0)])
                        ew("act", lambda e, s=s, mt=mt, h=h: e.activation(out=tz[:, 0:512], in_=xt[:, s, 512:1024],
                                                                           func=AF.Square,
                                                                           accum_out=stat[:, 11 + 2 * h:12 + 2 * h]),
                           [("xt", s)], ["tz", ("stat", 0)])
                    ew("dve", lambda e: e.tensor_reduce(out=stat[:, 14:15], in_=stat[:, 10:14],
                                                        axis=mybir.AxisListType.X, op=ALU.add), [("stat", 0)], [("stat", 0)])
                    ew("act", lambda e: e.activation(out=stat[:, 15:16], in_=stat[:, 14:15], func=AF.Sqrt,
                                                      scale=1.0 / D, bias=epsc[:, 0:1]), [("stat", 0), "epsc"], [("stat", 0)])
                    ew("dve", lambda e: e.reciprocal(out=stat[:, 15:16], in_=stat[:, 15:16]), [("stat", 0)], [("stat", 0)])
                    for h in range(2):
                        ew("act", lambda e, h=h: e.activation(out=xt[:, h, :], in_=xt[:, h, :], func=AF.Copy,
                                                               scale=stat[:, 15:16]), [("xt", h), ("stat", 0)], [("xt", h)])
                        for r0 in range(0, 8, 4):
                            mi = next_misc()

                            def f(e, mi=mi, r0=r0, h=h):
                                ins = None
                                for kk in range(4):
                                    ins = e.transpose(out=misc[mi][:, kk * P:(kk + 1) * P],
                                                      in_=xt[:, h, (r0 + kk) * P:(r0 + kk + 1) * P],
                                                      identity=ident_f[:, :])
                                return ins
                            S.op("pe", f, reads=[("xt", h), "ident_f"], writes=[("misc", mi)])
                            k0 = h * 8 + r0
                            ew("dve", lambda e, mi=mi, k0=k0, mt=mt: e.tensor_tensor(
                                out=mnT[:, k0:k0 + 4, mt * P:(mt + 1) * P],
                                in0=misc[mi][:, :].rearrange("p (k t) -> p k t", k=4),
                                in1=gn[:, 4, k0:k0 + 4].unsqueeze(2).to_broadcast([P, 4, P]), op=ALU.mult),
                               [("misc", mi), "gn"], ["M"])


            tiles = [(t * P, P) for t in range(4)] + [(NPB, NSB)]
            hn = 0
            for (r0, rows) in tiles:
                for h in range(2):
                    sl = hn % len(STG)
                    hn += 1
                    S.dma("sp", lambda e, r0=r0, rows=rows, h=h, sl=sl:
                          e.dma_start(out=STG[sl][0:rows, :], in_=xg[g, r0:r0 + rows, h * 1024:(h + 1) * 1024]),
                          writes=[STG_KEYS[sl]])
                    to_feature_major(STG[sl], rows, R, "R", r0, h * 8, 8, STG_KEYS[sl])
                sdst = stat_pb[:, r0:r0 + rows] if r0 < NPB else stat_sb[:, 0:rows]
                for k in range(KC):
                    sl2 = st.get("sqn", 0) % 2
                    st["sqn"] = st.get("sqn", 0) + 1
                    ew("act", lambda e, k=k, sl2=sl2, r0=r0, rows=rows: e.activation(
                        out=sq[:, sl2, 0:rows], in_=R[:, k, r0:r0 + rows], func=AF.Square), ["R"], [("sq", sl2)])
                    S.op("pe", lambda e, k=k, sl2=sl2, rows=rows, sdst=sdst: e.matmul(
                        sdst, lhsT=ones_b[:, :], rhs=sq[:, sl2, 0:rows], start=(k == 0), stop=(k == KC - 1)),
                        reads=[("sq", sl2), "ones_b"], writes=STATKEYS)
            S.alias(PHA_KEYS + VR_KEYS, STG_KEYS[2:] + ATT_KEYS)
            if g == 0:
                mnT = M
                def st_kv(which, s):
                    def fn(wb, wkey):
                        outd = mk_o if which == "k" else mv_o
                        for mt in range(2):
                            i = next_acc()

                            def f(e, i=i, mt=mt):
                                ins = None
                                for k in range(KC):
                                    ins = e.matmul(accP[i][:, :], lhsT=mnT[:, k, mt * P:(mt + 1) * P], rhs=wb[:, k, :],
                                                   start=(k == 0), stop=(k == KC - 1))
                                return ins
                            S.op("pe", f, reads=["M"] + wk(wkey), writes=acc_reads(i))
                            if mt == 0:
                                ks = st["ev"] % 2
                                ew("act", lambda e, i=i, ks=ks: e.copy(out=kst[:, ks, :], in_=accP[i][:, :]),
                                   acc_reads(i), [KSTKEY[ks]])
                                S.dma("sp", lambda e, ks=ks, s=s: e.dma_start(out=outd[:, s * 512:(s + 1) * 512],
                                                                              in_=kst[:, ks, :]),
                                      reads=[KSTKEY[ks]])
                                st["ev"] += 1
                            if which == "v":
                                ew("dve", lambda e, i=i, mt=mt, s=s: e.tensor_copy(
                                    out=Vbf[:, mt, s * 512:(s + 1) * 512], in_=accP[i][:, :]), acc_reads(i), ["Vbf"])
                        if which == "k":
                            for cc in range(4):
                                i = next_acc()

                                def f2(e, i=i, cc=cc):
                                    ins = None
                                    for k in range(KC):
                                        ins = e.matmul(accP[i][:, 0:256], lhsT=wb[:, k, cc * P:(cc + 1) * P],
                                                       rhs=mnT[:, k, 0:256], start=(k == 0), stop=(k == KC - 1))
                                    return ins
                                S.op("pe", f2, reads=["M"] + wk(wkey, slice(cc * P, (cc + 1) * P)), writes=acc_reads(i))
                                ew("dve", lambda e, i=i, cc=cc, s=s: e.tensor_copy(out=KT[:, 4 * s + cc, :],
                                                                                    in_=accP[i][:, 0:256]),
                                   acc_reads(i), ["KT"])
                    return fn
                for s in range(4):
                    add_stage(*([(0, KC, 0, 512, wview(w_k, 0, KC, s * 512, 512))], st_kv("k", s)))
                for s in range(4):
                    add_stage(*([(0, KC, 0, 512, wview(w_v, 0, KC, s * 512, 512))], st_kv("v", s)))

            rmsnorm_fm(0, X, "X", fused=True)
            S.dma("sp", lambda e: e.dma_start(out=lng_bc.unsqueeze(1), in_=lnv[0:1, :].partition_broadcast(P)),
                  writes=LNG_KEYS)
            S.dma("sp", lambda e: e.dma_start(out=lnb_bc.unsqueeze(1), in_=lnv[1:2, :].partition_broadcast(P)),
                  writes=LNB_KEYS)

            vstate = {}

            def st_v0(wb, wkey):
                vstate["wb0"] = wb
                vstate["k0"] = wkey

            def st_v1(wb, wkey):
                wbs = [vstate["wb0"], wb]
                wkeys = [vstate["k0"], wkey]
                for ti, (r0, rows) in enumerate(tiles):
                    vpar = ti % 2
                    vraw = VRAW[vpar]
                    vkey = VR_KEYS[vpar]
                    stat = STATV[vpar]
                    skey = (("stat", 0), vpar)
                    banks = []
                    for h in range(2):
                        i = next_acc()
                        banks.append(i)

                        def f(e, i=i, h=h, r0=r0, rows=rows, ks=range(KC)):
                            ins = None
                            for k in ks:
                                ins = e.matmul(accP[i][0:rows, :], lhsT=X[:, k, r0:r0 + rows], rhs=wbs[h][:, k, :],
                                               start=(k == 0), stop=(k == KC - 1))
                            return ins
                        if st.get("split"):
                            st["split"] = False
                            for k in range(KC):
                                S.op("pe", lambda e, f=f, k=k: f(e, ks=[k]), reads=[("X", k)] + wk(wkeys[h]),
                                     writes=acc_reads(i))
                        else:
                            S.op("pe", f, reads=XKEYS + wk(wkeys[h]), writes=acc_reads(i))
                    for h in range(2):
                        i = banks[h]
                        ew("act", lambda e, i=i, h=h, rows=rows, vraw=vraw, stat=stat: e.activation(
                            out=vraw[0:rows, h * 512:(h + 1) * 512], in_=accP[i][0:rows, :], func=AF.Copy,
                            accum_out=stat[0:rows, h:h + 1]),
                           acc_reads(i), [vkey, skey])
                        ew("act", lambda e, i=i, h=h, rows=rows, vraw=vraw, stat=stat: e.activation(
                            out=tz[0:rows, 0:512], in_=accP[i][0:rows, :], func=AF.Square,
                            accum_out=stat[0:rows, 2 + h:3 + h]),
                           acc_reads(i), ["tz", skey])
                    rs = slice(0, rows)
                    ew("dve", lambda e, rs=rs, vraw=vraw, stat=stat: e.tensor_tensor(out=stat[rs, 4:5], in0=stat[rs, 0:1], in1=stat[rs, 1:2],
                                                                op=ALU.add), [skey], [skey])
                    ew("dve", lambda e, rs=rs, vraw=vraw, stat=stat: e.tensor_tensor(out=stat[rs, 5:6], in0=stat[rs, 2:3], in1=stat[rs, 3:4],
                                                                op=ALU.add), [skey], [skey])
                    ew("dve", lambda e, rs=rs, vraw=vraw, stat=stat: e.tensor_scalar(out=stat[rs, 4:6], in0=stat[rs, 4:6], scalar1=1.0 / AW,
                                                                scalar2=None, op0=ALU.mult), [skey], [skey])
                    ew("dve", lambda e, rs=rs, vraw=vraw, stat=stat: e.tensor_tensor(out=stat[rs, 6:7], in0=stat[rs, 4:5], in1=stat[rs, 4:5],
                                                                op=ALU.mult), [skey], [skey])
                    ew("dve", lambda e, rs=rs, vraw=vraw, stat=stat: e.tensor_tensor(out=stat[rs, 7:8], in0=stat[rs, 5:6], in1=stat[rs, 6:7],
                                                                op=ALU.subtract), [skey], [skey])
                    ew("act", lambda e, rs=rs, vraw=vraw, stat=stat: e.activation(out=stat[rs, 8:9], in_=stat[rs, 7:8], func=AF.Sqrt,
                                                             scale=1.0, bias=epsc[rs, 0:1]),
                       [skey, "epsc"], [skey])
                    ew("dve", lambda e, rs=rs, vraw=vraw, stat=stat: e.reciprocal(out=stat[rs, 8:9], in_=stat[rs, 8:9]), [skey], [skey])
                    ew("dve", lambda e, rs=rs, vraw=vraw, stat=stat: e.scalar_tensor_tensor(out=stat[rs, 9:10], in0=stat[rs, 4:5],
                                                                       scalar=-1.0, in1=stat[rs, 8:9], op0=ALU.mult,
                                                                       op1=ALU.mult), [skey], [skey])
                    ew("dve", lambda e, rs=rs, vraw=vraw, stat=stat: e.tensor_scalar(
                        out=vraw[rs, :], in0=vraw[rs, :], scalar1=stat[rs, 8:9], scalar2=stat[rs, 9:10],
                        op0=ALU.mult, op1=ALU.add), [vkey, skey], [vkey])
                    ew("dve", lambda e, rs=rs, vraw=vraw: e.tensor_tensor(out=vraw[rs, :], in0=vraw[rs, :],
                                                                           in1=lng_bc[rs, :], op=ALU.mult),
                       [vkey] + LNG_KEYS, [vkey])
                    if ti != 4:
                        ew("dve", lambda e, rs=rs, ti=ti, vraw=vraw: e.tensor_tensor(
                            out=vbf[rs, ti, :], in0=vraw[rs, :], in1=lnb_bc[rs, :], op=ALU.add),
                           [vkey] + LNB_KEYS, ["M"])
                    else:
                        ew("dve", lambda e, rs=rs, vraw=vraw: e.tensor_tensor(out=vraw[rs, :], in0=vraw[rs, :],
                                                                               in1=lnb_bc[rs, :], op=ALU.add),
                           [vkey] + LNB_KEYS, [vkey])
                        ew("act", lambda e, rs=rs, ti=ti, vraw=vraw: e.copy(out=vbf[rs, ti, :], in_=vraw[rs, :]),
                           [vkey], ["M"])
                    if ti == 4:
                        S.dma("sp", lambda e, vraw=vraw: e.dma_start(out=chunkv_o[g, :, :], in_=vraw[0:NS, :]), reads=[vkey])

            add_stage(*([(0, KC, 0, 512, wview(w_in, 0, KC, OFF_V, 512))], st_v0, "defer"))
            add_stage(*([(0, KC, 0, 512, wview(w_in, 0, KC, OFF_V + 512, 512))], st_v1))

            def st_alpha(q):
                def fn(wb, wkey):
                    if q == 0:
                        S.alias(PHA_KEYS, VR_KEYS)
                    for cc in range(2):
                        c = 2 * q + cc
                        i = lin_chunk_b(X, XKEYS, wb, wkey, KC, slice(cc * P, (cc + 1) * P))
                        evac_copy(i, Csb, cc, ("Csb", cc))
                        i = lin_chunk_b(X, XKEYS, wb, wkey, KC, slice(256 + cc * P, 256 + (cc + 1) * P))
                        rd = acc_reads(i) + [("Csb", cc)]
                        samp = pbuf[:, 2 + NPB:2 + NPB + 80].rearrange("p (b t) -> p b t", t=10)
                        ew("dve", lambda e, i=i, cc=cc: e.tensor_tensor(out=pbuf[:, 2:2 + NPB], in0=accP[i][:, :],
                                                                         in1=Csb[:, cc, 0:NPB], op=ALU.mult),
                           rd, ["pbuf"])
                        ew("dve", lambda e, i=i, cc=cc: e.tensor_tensor(out=pbuf[:, 0:2], in0=aS(i)[:, NS:NSB],
                                                                         in1=Csb[:, cc, NPB + NS:T], op=ALU.mult),
                           rd, ["pbuf"])
                        ew("dve", lambda e, i=i, cc=cc, samp=samp: e.tensor_tensor(
                            out=samp[:, :, 2:10], in0=aS(i)[:, 0:NS].rearrange("p (b t) -> p b t", t=8),
                            in1=Csb[:, cc, NPB:NPB + NS].rearrange("p (b t) -> p b t", t=8), op=ALU.mult),
                           rd, ["pbuf"])
                        ew("act", lambda e, c=c, samp=samp: e.copy(
                            out=samp[:, :, 0:2],
                            in_=sconvT[:, c, :].rearrange("p (b k) -> p b k", k=2)[:, 8 * g:8 * g + 8, :]),
                           ["sconvT"], ["pbuf"])
                        cvp = convb[:, cc, 0:NPB]
                        cvs = convb[:, cc, NPB:NPB + NS].rearrange("p (b t) -> p b t", t=8)
                        ew("act", lambda e, c=c, cvp=cvp: e.activation(out=cvp, in_=pbuf[:, 0:NPB], func=AF.Copy,
                                                                        scale=cw[:, c, 0:1]),
                           ["pbuf", "cw"], [("convb", cc)])
                        ew("act", lambda e, c=c, cvs=cvs, samp=samp: e.activation(out=cvs, in_=samp[:, :, 0:8],
                                                                                   func=AF.Copy, scale=cw[:, c, 0:1]),
                           ["pbuf", "cw"], [("convb", cc)])
                        for kk in (1, 2):
                            ew("dve", lambda e, c=c, cvp=cvp, kk=kk: e.scalar_tensor_tensor(
                                out=cvp, in0=pbuf[:, kk:kk + NPB], scalar=cw[:, c, kk:kk + 1], in1=cvp,
                                op0=ALU.mult, op1=ALU.add), ["pbuf", "cw", ("convb", cc)], [("convb", cc)])
                            ew("dve", lambda e, c=c, cvs=cvs, kk=kk, samp=samp: e.scalar_tensor_tensor(
                                out=cvs, in0=samp[:, :, kk:kk + 8], scalar=cw[:, c, kk:kk + 1], in1=cvs,
                                op0=ALU.mult, op1=ALU.add), ["pbuf", "cw", ("convb", cc)], [("convb", cc)])
                        if g == NGROUPS - 1:
                            ew("act", lambda e, c=c: e.copy(out=pl[:, c, :], in_=pbuf[:, NPB:NPB + 2]),
                               ["pbuf"], ["pl"])
                        ew("act", lambda e, c=c, samp=samp: e.copy(out=pss[:, c, 8 * g:8 * g + 8, :],
                                                                    in_=samp[:, :, 8:10]), ["pbuf"], ["pss"])
                return fn

            def st_beta(q):
                def fn(wb, wkey):
                    for cc in range(2):
                        c = 2 * q + cc
                        gi = c // 2
                        i = lin_chunk_b(X, XKEYS, wb, wkey, KC, slice(cc * P, (cc + 1) * P))
                        ew("dve", lambda e, i=i, cc=cc, c=c: e.tensor_tensor(out=Y[:, 8 + c, 0:NPB], in0=accP[i][:, :],
                                                                              in1=convb[:, cc, 0:NPB], op=ALU.mult),
                           acc_reads(i) + [("convb", cc)], ["Y"])
                        ew("dve", lambda e, i=i, cc=cc, c=c: e.tensor_tensor(out=Y[:, 8 + c, NPB:NPB + NS],
                                                                              in0=aS(i)[:, 0:NS],
                                                                              in1=convb[:, cc, NPB:NPB + NS],
                                                                              op=ALU.mult),
                           acc_reads(i) + [("convb", cc)], ["Y"])
                        i = lin_chunk_b(X, XKEYS, wb, wkey, KC, slice(256 + cc * P, 256 + (cc + 1) * P))
                        ew("act", lambda e, i=i: e.copy(out=usb[:, 0:NPB], in_=accP[i][:, :]), acc_reads(i), ["usb"])
                        ew("act", lambda e, i=i: e.copy(out=usb[:, NPB:T], in_=aS(i)[:, 0:NSB]), acc_reads(i),
                           ["usb"])
                        i = next_acc()

                        def fz(e, i=i, c=c, gi=gi):
                            for t4 in range(4):
                                e.matmul(accP[i][:, t4 * P:(t4 + 1) * P], lhsT=vbf[:, t4, c * P:(c + 1) * P],
                                         rhs=WsT[:, gi, :], start=True, stop=True)
                            return e.matmul(aS(i)[:, 0:NS], lhsT=vbf[0:NS, 4, c * P:(c + 1) * P],
                                            rhs=WsTs[:, gi, :], start=True, stop=True)
                        S.op("pe", fz, reads=["M", "WsT", "WsTs"], writes=acc_reads(i))
                        ew("dve", lambda e, i=i, gi=gi: e.tensor_tensor(
                            out=tz[:, 0:NPB].rearrange("p (a t) -> p a t", t=P),
                            in0=accP[i][:, :].rearrange("p (a t) -> p a t", t=P),
                            in1=bsp_bc[:, gi:gi + 1, :].to_broadcast([P, 4, P]), op=ALU.add),
                           acc_reads(i) + ["bsp_bc"], ["tz"])
                        ew("dve", lambda e, i=i, gi=gi: e.tensor_tensor(
                            out=tz[:, NPB:NPB + NS].rearrange("p (b t) -> p b t", t=8),
                            in0=aS(i)[:, 0:NS].rearrange("p (b t) -> p b t", t=8),
                            in1=bsp_bc[:, gi:gi + 1, 0:8].to_broadcast([P, 8, 8]), op=ALU.add),
                           acc_reads(i) + ["bsp_bc"], ["tz"])
                        ew("dve", lambda e, c=c: e.tensor_tensor(out=Y[:, c, 0:NPB + NS], in0=tz[:, 0:NPB + NS],
                                                                  in1=usb[:, 0:NPB + NS], op=ALU.mult),
                           ["tz", "usb"], ["Y"])
                return fn

            for q in range(4):
                add_stage(*([(0, KC, 0, 256, wview(w_in, 0, KC, OFF_C + q * 256, 256)),
                                (0, KC, 256, 256, wview(w_in, 0, KC, OFF_XIN + q * 256, 256))], st_alpha(q)))
                add_stage(*([(0, KC, 0, 256, wview(w_in, 0, KC, OFF_B + q * 256, 256)),
                                (0, KC, 256, 256, wview(w_in, 0, KC, OFF_U + q * 256, 256))], st_beta(q)))

            def st_gamma(jj):
                def fn(wb, wkey):
                    for s4 in range(4):
                        i = lin_chunk_b(X, XKEYS, wb, wkey, KC, slice(s4 * P, (s4 + 1) * P))
                        ew("act", lambda e, i=i, s4=s4: e.activation(out=G[:, s4, 0:NPB], in_=accP[i][:, :],
                                                                      func=AF.Sigmoid), acc_reads(i), [("G", s4)])
                        ew("act", lambda e, i=i, s4=s4: e.activation(out=G[:, s4, NPB:T], in_=aS(i)[:, 0:NSB],
                                                                      func=AF.Sigmoid), acc_reads(i), [("G", s4)])
                return fn

            def st_delta(jj):
                def fn(wb, wkey):
                    for cc in range(2):
                        j = 2 * jj + cc
                        i = next_acc()
                        tp = [(wb[:, k, cc * P:(cc + 1) * P], Y[:, k, 0:NPB]) for k in range(8)]
                        ts = [(wb[:, k, cc * P:(cc + 1) * P], Y[:, k, NPB:T]) for k in range(8)]
                        mm_group(i, tp, ts, reads=["Y"] + wk(wkey))
                        ew("dve", lambda e, i=i, cc=cc: e.tensor_tensor(out=G[:, cc, 0:NPB], in0=accP[i][:, :],
                                                                         in1=G[:, cc, 0:NPB], op=ALU.mult),
                           acc_reads(i) + [("G", cc)], [("G", cc)])
                        ew("dve", lambda e, i=i, cc=cc: e.tensor_tensor(out=G[:, cc, NPB:T], in0=aS(i)[:, 0:NSB],
                                                                         in1=G[:, cc, NPB:T], op=ALU.mult),
                           acc_reads(i) + [("G", cc)], [("G", cc)])
                        i = next_acc()
                        tp = [(wb[:, k, 256 + cc * P:256 + (cc + 1) * P], Y[:, 8 + k, 0:NPB]) for k in range(8)]
                        ts = [(wb[:, k, 256 + cc * P:256 + (cc + 1) * P], Y[:, 8 + k, NPB:T]) for k in range(8)]
                        mm_group(i, tp, ts, reads=["Y"] + wk(wkey))
                        ew("dve", lambda e, i=i, cc=cc: e.tensor_tensor(out=G[:, 2 + cc, 0:NPB], in0=accP[i][:, :],
                                                                         in1=G[:, 2 + cc, 0:NPB], op=ALU.mult),
                           acc_reads(i) + [("G", 2 + cc)], [("G", 2 + cc)])
                        ew("dve", lambda e, i=i, cc=cc: e.tensor_tensor(out=G[:, 2 + cc, NPB:T],
                                                                         in0=aS(i)[:, 0:NSB],
                                                                         in1=G[:, 2 + cc, NPB:T], op=ALU.mult),
                           acc_reads(i) + [("G", 2 + cc)], [("G", 2 + cc)])
                        ew("dve", lambda e, j=j, cc=cc: e.tensor_tensor(out=M[:, j, :], in0=G[:, cc, :],
                                                                          in1=G[:, 2 + cc, :], op=ALU.add),
                           [("G", cc), ("G", 2 + cc)], ["M"])
                return fn

            for jj in range(8):
                add_stage(*([(0, KC, 0, 256, wview(w_in, 0, KC, OFF_GA + jj * 256, 256)),
                                (0, KC, 256, 256, wview(w_in, 0, KC, OFF_GB + jj * 256, 256))], st_gamma(jj)))
                add_stage(*([(0, 8, 0, 256, wview(w_a, 0, 8, jj * 256, 256)),
                                (0, 8, 256, 256, wview(w_b, 0, 8, jj * 256, 256))], st_delta(jj)))

            def st_resid(src, srckey, stats=False):
                def mk(s):
                    def fn(wb, wkey):
                        for cc in range(4):
                            i = lin_chunk_b(src, [srckey], wb, wkey, KC, slice(cc * P, (cc + 1) * P))
                            stats_flush()
                            resid_add(i, 4 * s + cc)
                            if stats:
                                stats_push(4 * s + cc)
                    return fn
                return mk

            mkfn = st_resid(M, "M", stats=True)
            for s in range(4):
                add_stage(*([(0, KC, 0, 512, wview(w_mix, 0, KC, s * 512, 512))], mkfn(s)))
            run_stages()
            if DEBUG_DUMP == "h1" and g == 0:
                dump_R()

            rmsnorm_fm(1, X, "X", fused=True)
            Xflat = X[:, :, :].rearrange("p k t -> p (k t)")
            kvr = [scrB[:, 0:2048].bitcast(BF16).rearrange("p (m d) -> p m d", m=2)] + \
                  [Xflat[:, r * 4096:(r + 1) * 4096].rearrange("p (m d) -> p m d", m=2) for r in range(2)]
            S.alias([("kbX", 0)], PHA_KEYS + VR_KEYS + STG_KEYS[2:])
            S.dma("pool", lambda e: e.dma_start(out=kvr[0], in_=ck[8 * g, :, :].rearrange("(m p) d -> p m d", p=P)),
                  writes=[("kbX", 0)])

            def st_q(s):
                def fn(wb, wkey):
                    for cc in range(4):
                        i = lin_chunk_b(X, XKEYS, wb, wkey, KC, slice(cc * P, (cc + 1) * P))
                        evac_copy(i, Y, 4 * s + cc, "Y")
                return fn
            for s in range(4):
                add_stage(*([(0, KC, 0, 512, wview(w_q, 0, KC, s * 512, 512))], st_q(s)),
                          *(["noahead"] if s == 3 else []))

            run_stages()

            S.alias(ATT_KEYS, PHA_KEYS + VR_KEYS)
            S.alias([("kbX", 1), ("kbX", 2)], XKEYS)
            S.dma("pool", lambda e: e.dma_start(out=kvr[1],
                                                in_=ck[8 * g + 1, :, :].rearrange("(m p) d -> p m d", p=P)),
                  writes=[("kbX", 1)])
            S.dma("pool", lambda e: e.dma_start(out=kvr[2],
                                                in_=ck[8 * g + 2, :, :].rearrange("(m p) d -> p m d", p=P)),
                  writes=[("kbX", 2)])
            ABK = [(accP[j], [("accP", j)]) for j in range(4)] + [(accS2[j], [("accS", j)]) for j in range(2)]
            pTd = [pT, pT2]
            rinvd = [(scrA[:, 2, 0:NPB], ("G", 2)), (scrA[:, 3, 0:NPB], ("G", 3))]

            def nbk():
                st["abk"] = st.get("abk", 0) + 1
                return ABK[st["abk"] % len(ABK)]

            def att_scores(h):
                par = h % 2
                for mt in range(2):
                    bank, bkeys = nbk()

                    def f(e, bank=bank, mt=mt, h=h):
                        ins = None
                        for c in range(4):
                            ins = e.matmul(bank[:, :], lhsT=KT[:, 4 * h + c, mt * P:(mt + 1) * P],
                                           rhs=Y[:, 4 * h + c, 0:NPB], start=(c == 0), stop=(c == 3))
                        return ins
                    S.op("pe", f, reads=["KT", "Y"], writes=bkeys)
                    ew("act", lambda e, bank=bank, mt=mt, par=par: e.activation(
                        out=pTd[par][:, mt, :], in_=bank[:, :], func=AF.Exp, scale=SCALE), bkeys, [("pT", par)])

            def att_rest(h):
                par = h % 2
                pTh = pTd[par]
                rv, rkey = rinvd[par]
                bank, bkeys = nbk()

                def fden(e, bank=bank, pTh=pTh):
                    e.matmul(bank[:, :], lhsT=ones_b[:, :], rhs=pTh[:, 0, :], start=True, stop=False)
                    return e.matmul(bank[:, :], lhsT=ones_b[:, :], rhs=pTh[:, 1, :], start=False, stop=True)
                S.op("pe", fden, reads=[("pT", par), "ones_b"], writes=bkeys)
                ew("dve", lambda e, bank=bank, rv=rv: e.reciprocal(out=rv, in_=bank[:, :]), bkeys, [rkey])
                for c in range(4):
                    bank, bkeys = nbk()

                    def fo(e, bank=bank, c=c, h=h, pTh=pTh):
                        e.matmul(bank[:, :], lhsT=Vbf[:, 0, (4 * h + c) * P:(4 * h + c + 1) * P], rhs=pTh[:, 0, :],
                                 start=True, stop=False)
                        return e.matmul(bank[:, :], lhsT=Vbf[:, 1, (4 * h + c) * P:(4 * h + c + 1) * P],
                                        rhs=pTh[:, 1, :], start=False, stop=True)
                    S.op("pe", fo, reads=[("pT", par), "Vbf"], writes=bkeys)
                    ew("dve", lambda e, bank=bank, c=c, h=h, rv=rv: e.tensor_tensor(
                        out=M[:, 4 * h + c, 0:NPB], in0=bank[:, :], in1=rv, op=ALU.mult), bkeys + [rkey], ["M"])

            att_scores(0)
            for h in range(4):
                if h + 1 < 4:
                    att_scores(h + 1)
                att_rest(h)

            Sall = misc[0][:, :].rearrange("p (m h t) -> p m h t", m=2, h=4)
            trb = [(miscb[:, :], "miscb"), (accP[3][:, :].bitcast(BF16), ("accP", 3))]
            Oall = [accP[0][:, :].rearrange("p (j t) -> p j t", j=8), accP[1][:, :].rearrange("p (j t) -> p j t", j=8)]
            OKEYS = [("accP", 0), ("accP", 1)]

            def ld_kv(src, seq, r):
                S.dma("pool", lambda e, seq=seq, r=r: e.dma_start(
                    out=kvr[r], in_=src[seq, :, :].rearrange("(m p) d -> p m d", p=P)), writes=[("kbX", r)])

            bfree = (mode["si"] + NBUF - 1) % NBUF
            wfl = wbuf[:, bfree, :, :].rearrange("p k c -> p (k c)")
            vvr = [wfl[:, r * 4096:(r + 1) * 4096].rearrange("p (m d) -> p m d", m=2) for r in range(2)]
            VKEYS = [("vring", 0), ("vring", 1)]
            S.alias(VKEYS, [("wbuf", bfree, 0), ("wbuf", bfree, 1)])

            def ld_v(seq, r):
                S.dma("pool", lambda e, seq=seq, r=r: e.dma_start(
                    out=vvr[r], in_=cv[seq, :, :].rearrange("(m p) d -> p m d", p=P)), writes=[VKEYS[r]])

            items = [(b, h) for b in range(8) for h in range(4)]
            SB2 = [(misc[0], ("misc", 0)), (accP[2], ("accP", 2))]
            OB2 = [(accP[0], ("accP", 0)), (accP[1], ("accP", 1))]

            def Sreg(b):
                return SB2[b % 2][0][:, 0:64].rearrange("p (m h t) -> p m h t", m=2, h=4)

            def emit_T(n):
                b, h = items[n]
                r = b % 3
                tb, tkey = trb[n % 2]

                def ftr(e, r=r, h=h, tb=tb):
                    ins = None
                    for c in range(4):
                        for mt in range(2):
                            ins = e.transpose(out=tb[:, c * 256 + mt * P:c * 256 + (mt + 1) * P],
                                              in_=kvr[r][:, mt, (4 * h + c) * P:(4 * h + c + 1) * P],
                                              identity=ident_b[:, :])
                    return ins
                S.op("pe", ftr, reads=[("kbX", r), "ident_b"], writes=[tkey])
                tr = n % 2
                eng = ev_eng()
                if eng == "act":
                    ew("act", lambda e, tr=tr, tb=tb: e.copy(out=kbT[:, tr, :, :].rearrange("p c m -> p (c m)"),
                                                             in_=tb), [tkey], [("kbT", tr)])
                else:
                    ew("dve", lambda e, tr=tr, tb=tb: e.tensor_copy(
                        out=kbT[:, tr, :, :].rearrange("p c m -> p (c m)"), in_=tb), [tkey], [("kbT", tr)])

            def emit_S(n):
                b, h = items[n]
                tr = n % 2
                sreg = Sreg(b)

                def fsc(e, tr=tr, b=b, h=h, sreg=sreg):
                    ins = None
                    for mt in range(2):
                        for c in range(4):
                            ins = e.matmul(sreg[:, mt, h, :], lhsT=kbT[:, tr, c, mt * P:(mt + 1) * P],
                                           rhs=Y[:, 4 * h + c, NPB + 8 * b:NPB + 8 * b + 8], start=(c == 0),
                                           stop=(c == 3))
                    return ins
                S.op("pe", fsc, reads=[("kbT", tr), "Y"], writes=[SB2[b % 2][1]])

            def seq_exp(b):
                skey = SB2[b % 2][1]
                ew("act", lambda e, b=b: e.activation(out=pTs[:, :, :, 8 * b:8 * b + 8], in_=Sreg(b), func=AF.Exp,
                                                      scale=SCALE), [skey], [("pTs", b)])

            def seq_rest(b):
                r = b % 2
                sbank, skey = SB2[b % 2]
                obank, okey = OB2[b % 2]
                den = sbank[:, 64:96]
                oreg = obank[:, 0:128].rearrange("p (j t) -> p j t", j=16)

                def fden(e, b=b, den=den):
                    e.matmul(den, lhsT=ones_b[:, :], rhs=pTs[:, 0, :, 8 * b:8 * b + 8], start=True, stop=False)
                    return e.matmul(den, lhsT=ones_b[:, :], rhs=pTs[:, 1, :, 8 * b:8 * b + 8], start=False, stop=True)
                S.op("pe", fden, reads=[("pTs", b), "ones_b"], writes=[skey])
                ew("dve", lambda e, b=b, den=den: e.reciprocal(out=rinvs[:, :, 8 * b:8 * b + 8],
                                                               in_=den.rearrange("p (h t) -> p h t", h=4)),
                   [skey], [("rinvs", b)])

                def fpv(e, r=r, b=b, oreg=oreg):
                    ins = None
                    for j in range(16):
                        for mt in range(2):
                            ins = e.matmul(oreg[:, j, :], lhsT=vvr[r][:, mt, j * P:(j + 1) * P],
                                           rhs=pTs[:, mt, j // 4, 8 * b:8 * b + 8], start=(mt == 0), stop=(mt == 1))
                    return ins
                S.op("pe", fpv, reads=[VKEYS[r], ("pTs", b)], writes=[okey])
                if b + 2 < 8:
                    ld_v(8 * g + b + 2, r)
                ew("dve", lambda e, b=b, oreg=oreg: e.tensor_tensor(
                    out=M[:, :, NPB + 8 * b:NPB + 8 * b + 8].rearrange("p (h c) t -> p h c t", h=4),
                    in0=oreg.rearrange("p (h c) t -> p h c t", h=4),
                    in1=rinvs[:, :, 8 * b:8 * b + 8].unsqueeze(2).to_broadcast([P, 4, 4, 8]), op=ALU.mult),
                   [okey, ("rinvs", b)], ["M"])

            emit_T(0)
            ld_v(8 * g + 0, 0)
            ld_v(8 * g + 1, 1)
            pend_seq = []
            for n in range(len(items)):
                b0, h0 = items[n]
                if n + 1 < len(items):
                    b1, h1 = items[n + 1]
                    emit_T(n + 1)
                    if h1 == 3 and b1 + 3 < 8:
                        ld_kv(ck, 8 * g + b1 + 3, b1 % 3)
                emit_S(n)
                if h0 == 0 and pend_seq:
                    seq_rest(pend_seq.pop(0))
                if h0 == 3:
                    seq_exp(b0)
                    pend_seq.append(b0)
            while pend_seq:
                seq_rest(pend_seq.pop(0))

            S.alias(XKEYS, [("kbX", 1), ("kbX", 2)])
            S.alias([("wbuf", bfree, 0), ("wbuf", bfree, 1)], VKEYS)


            mkfn = st_resid(M, "M", stats=True)
            for s in range(4):
                add_stage(*([(0, KC, 0, 512, wview(w_xo, 0, KC, s * 512, 512))], mkfn(s)))
            run_stages()
            if DEBUG_DUMP == "h2" and g == 0:
                dump_R()

            rmsnorm_fm(2, X, "X", fused=True)

            def st_up(fg, s):
                def fn(wb, wkey):
                    for cc in range(4):
                        j = 4 * s + cc
                        i = lin_chunk_b(X, XKEYS, wb, wkey, KC, slice(cc * P, (cc + 1) * P))
                        ew("act", lambda e, i=i: e.activation(out=usb[:, 0:NPB], in_=accP[i][:, :], func=AF.Square),
                           acc_reads(i), ["usb"])
                        ew("act", lambda e, i=i: e.activation(out=usb[:, NPB:T], in_=aS(i)[:, 0:NSB],
                                                               func=AF.Square), acc_reads(i), ["usb"])
                        ew("dve", lambda e, i=i, j=j: e.scalar_tensor_tensor(out=Y[:, j, 0:NPB], in0=accP[i][:, :],
                                                                              scalar=0.0, in1=usb[:, 0:NPB],
                                                                              op0=ALU.is_gt, op1=ALU.mult),
                           acc_reads(i) + ["usb"], ["Y"])
                        ew("dve", lambda e, i=i, j=j: e.scalar_tensor_tensor(out=Y[:, j, NPB:T],
                                                                              in0=aS(i)[:, 0:NSB], scalar=0.0,
                                                                              in1=usb[:, NPB:T], op0=ALU.is_gt,
                                                                              op1=ALU.mult),
                           acc_reads(i) + ["usb"], ["Y"])
                return fn

            for fg in range(4):
                mkfn = st_resid(Y, "Y", stats=(fg == 3))
                for s in range(4):
                    add_stage(*([(0, KC, 0, 512, wview(w_up, 0, KC, fg * 2048 + s * 512, 512))], st_up(fg, s)))
                for s in range(4):
                    add_stage(*([(0, KC, 0, 512, wview(w_down, fg * 2048, KC, s * 512, 512))], mkfn(s)))
            run_stages()
            if DEBUG_DUMP == "h3" and g == 0:
                dump_R()

            rms_rbc()
            for k in range(KC):
                ew("dve", lambda e, k=k: e.scalar_tensor_tensor(out=R[:, k, :], in0=R[:, k, :],
                                                                 scalar=gn[:, 3, k:k + 1], in1=rbc[:, :],
                                                                 op0=ALU.mult, op1=ALU.mult),
                   ["R", "usb", "gn"], ["R", ("Rf", k)])
            S.alias(STG_KEYS[2:], ATT_KEYS + PHA_KEYS + VR_KEYS)
            otiles = [(t * P, P) for t in range(4)] + [(NPB, NS)]
            hn2 = 0
            for (r0, rows) in otiles:
                for h in range(2):
                    s = hn2 % len(STG)
                    hn2 += 1
                    for r4 in range(0, 8, 4):
                        tb, tkey = next_tr()

                        def f(e, tb=tb, r0=r0, rows=rows, h=h, r4=r4):
                            ins = None
                            for kk in range(4):
                                ins = e.transpose(out=tb[0:rows, kk * P:(kk + 1) * P],
                                                  in_=R[:, h * 8 + r4 + kk, r0:r0 + rows], identity=ident_f[:, :])
                            return ins
                        S.op("pe", f, reads=[("Rf", h * 8 + r4 + kk) for kk in range(4)] + ["ident_f"], writes=[tkey])
                        eng = ev_eng()
                        o_ = STG[s][0:rows, r4 * P:(r4 + 4) * P]
                        i_ = tb[0:rows, :]
                        if eng == "act":
                            ew("act", lambda e, o_=o_, i_=i_: e.copy(out=o_, in_=i_), [tkey], [STG_KEYS[s]])
                        else:
                            ew("dve", lambda e, o_=o_, i_=i_: e.tensor_copy(out=o_, in_=i_), [tkey],
                               [STG_KEYS[s]])
                    S.dma("sp", lambda e, r0=r0, rows=rows, h=h, s=s: e.dma_start(
                        out=y_o[g, r0:r0 + rows, h * 1024:(h + 1) * 1024], in_=STG[s][0:rows, :]),
                        reads=[STG_KEYS[s]])

        def dump_R():
            for k in range(KC):
                S.dma("sp", lambda e, k=k: e.dma_start(out=dbg_o[k, :, :], in_=R[:, k, :]), reads=["R"])

        S.dry = True
        for g in range(NGROUPS):
            group(g)
        S.dry = False
        st.clear()
        st.update({"acc": 0, "misc": 0, "ev": 0})
        for g in range(NGROUPS):
            group(g)

        for hh in range(2):
            def fcs(e, hh=hh):
                ins = None
                for c4 in range(4):
                    c = hh * 4 + c4
                    ins = e.transpose(out=misc[0][0:32, c4 * P:(c4 + 1) * P],
                                      in_=pss[:, c, :, :].rearrange("p b k -> p (b k)"), identity=ident_f[:, :])
                return ins
            S.op("pe", fcs, reads=["pss", "ident_f"], writes=[("misc", 0)])
            ew("dve", lambda e, hh=hh: e.tensor_copy(out=cst[:, hh * 512:(hh + 1) * 512], in_=misc[0][0:32, :]),
               [("misc", 0)], [("xt", 1)])
        S.dma("sp", lambda e: e.dma_start(out=convs_o[:, :], in_=cst), reads=[("xt", 1)])
        for hh in range(2):
            def fcp(e, hh=hh):
                ins = None
                for c4 in range(4):
                    c = hh * 4 + c4
                    ins = e.transpose(out=misc[0][0:2, c4 * P:(c4 + 1) * P], in_=pl[:, c, :], identity=ident_f[:, :])
                return ins
            S.op("pe", fcp, reads=["pl", "ident_f"], writes=[("misc", 0)])
            ew("dve", lambda e, hh=hh: e.tensor_copy(out=kst[0:2, hh, :], in_=misc[0][0:2, :]), [("misc", 0)],
               [KSTKEY[hh]])
        S.dma("sp", lambda e: e.dma_start(out=convp_o[:, :].rearrange("r (h n) -> r h n", h=2), in_=kst[0:2, :, :]),
              reads=list(KSTKEY))

        final_waits = {}
        for slot, uses in S.dma_uses.items():
            final_waits[slot] = 16 * uses

        def emit(engname, e):
            for (waits, fn, semname, inc) in S.streams[engname]:
                for (sn, v) in waits:
                    e.wait_ge(sems[sn], v)
                ins = fn(e)
                ins.then_inc(sems[semname], inc)

        @block.tensor
        def _(e):
            emit("pe", e)

        @block.scalar
        def _(e):
            emit("act", e)

        @block.vector
        def _(e):
            emit("dve", e)

        @block.gpsimd
        def _(e):
            emit("pool", e)
            for slot, v in final_waits.items():
                if slot.startswith("pool"):
                    e.wait_ge(sems[slot], v)

        @block.sync
        def _(e):
            emit("sp", e)
            for slot, v in final_waits.items():
                if slot.startswith("sp"):
                    e.wait_ge(sems[slot], v)
    return nc


_CACHE = {}


def _program():
    if "nc" not in _CACHE:
        _CACHE["nc"] = build_program()
    return _CACHE["nc"]


def _make_in_maps(x_prompt, x_sample, state_conv, cache_mem_k, cache_mem_v, mem_prompt,
           norm_mix_g, w_in, ln_v_g, ln_v_b, w_spatial, b_spatial, conv_w,
           w_branch_a, w_branch_b, w_mix_out, norm_x_g, norm_mem_g, w_q, w_k, w_v,
           w_x_out, norm_mlp_g, w_up, w_down, norm_final_g):
    f = np.float32
    A = lambda a: np.ascontiguousarray(np.asarray(a, dtype=f))
    x_prompt, x_sample = A(x_prompt), A(x_sample)
    state_conv, cache_mem_k, cache_mem_v, mem_prompt = A(state_conv), A(cache_mem_k), A(cache_mem_v), A(mem_prompt)

    def fm(gv):
        return np.asarray(gv, dtype=f).reshape(KC, P).T

    gains = np.ascontiguousarray(np.stack([fm(norm_mix_g[0]), fm(norm_x_g[0]), fm(norm_mlp_g[0]),
                                           fm(norm_final_g), fm(norm_mem_g[0])], axis=1))
    cwt = np.ascontiguousarray(np.asarray(conv_w[0], dtype=f).reshape(3, 8, P).transpose(2, 1, 0))
    lnv = np.ascontiguousarray(np.stack([np.asarray(ln_v_g[0], dtype=f), np.asarray(ln_v_b[0], dtype=f)]))
    ws = np.asarray(w_spatial[0], dtype=f)
    wsp = np.ascontiguousarray(ws.transpose(1, 0, 2))
    wsb = np.zeros((NS, 4, NS), dtype=f)
    for b in range(8):
        wsb[8 * b:8 * b + 8, :, 8 * b:8 * b + 8] = ws[:, :8, :8].transpose(1, 0, 2)
    bsp = np.ascontiguousarray(np.asarray(b_spatial[0], dtype=f).reshape(1, 4 * P))
    ident = np.eye(P, dtype=f)
    tril = np.tril(np.ones((P, P), dtype=f))
    shared = dict(w_in=A(w_in[0]), w_a=A(w_branch_a[0]), w_b=A(w_branch_b[0]), w_mix=A(w_mix_out[0]),
                  w_q=A(w_q[0]), w_k=A(w_k[0]), w_v=A(w_v[0]), w_xo=A(w_x_out[0]), w_up=A(w_up[0]),
                  w_down=A(w_down[0]), gains=gains, cwt=cwt, lnv=lnv, wsp=wsp, wsb=wsb, bsp=bsp, ident=ident,
                  tril=tril)
    in_maps = []
    for c in range(NCORES):
        b, half = c // 2, c % 2
        xg = np.zeros((NGROUPS, T, D), dtype=f)
        for g in range(NGROUPS):
            p0 = half * 1024 + g * 512
            xg[g, 0:NPB] = x_prompt[b, p0:p0 + NPB]
            xg[g, NPB:NPB + NS] = x_sample[16 * c + 8 * g:16 * c + 8 * g + 8].reshape(NS, D)
            if p0 >= 2:
                xg[g, NPB + NS:T] = x_prompt[b, p0 - 2:p0]
        m = dict(shared)
        m["xg"] = xg
        m["sconv"] = np.ascontiguousarray(state_conv[0, 16 * c:16 * c + 16].reshape(32, AW))
        m["ck"] = np.ascontiguousarray(cache_mem_k[0, 16 * c:16 * c + 16].reshape(16, 256, D))
        m["cv"] = np.ascontiguousarray(cache_mem_v[0, 16 * c:16 * c + 16].reshape(16, 256, D))
        m["mem"] = np.ascontiguousarray(np.concatenate(
            [mem_prompt[b, half * P:(half + 1) * P], mem_prompt[b, (1 - half) * P:(2 - half) * P]], axis=0))
        in_maps.append(m)

    return in_maps


def _assemble(outs):
    f = np.float32

    y_prompt = np.zeros((4, 2048, D), dtype=f)
    y_sample = np.zeros((128, 8, D), dtype=f)
    mem_k = np.zeros((1, 4, 256, 4, 512), dtype=f)
    mem_v = np.zeros((1, 4, 256, 4, 512), dtype=f)
    conv_p = np.zeros((1, 4, 2, AW), dtype=f)
    conv_s = np.zeros((1, 128, 2, AW), dtype=f)
    chunk_v = np.zeros((1, 128, 8, AW), dtype=f)
    for c in range(NCORES):
        b, half = c // 2, c % 2
        o = outs[c]
        for g in range(NGROUPS):
            p0 = half * 1024 + g * 512
            y_prompt[b, p0:p0 + NPB] = o["y"][g, 0:NPB]
            y_sample[16 * c + 8 * g:16 * c + 8 * g + 8] = o["y"][g, NPB:NPB + NS].reshape(8, 8, D)
            chunk_v[0, 16 * c + 8 * g:16 * c + 8 * g + 8] = o["chunkv"][g].reshape(8, 8, AW)
        mem_k[0, b, half * P:(half + 1) * P] = o["mk"].reshape(P, 4, 512)
        mem_v[0, b, half * P:(half + 1) * P] = o["mv"].reshape(P, 4, 512)
        if half == 1:
            conv_p[0, b] = o["convp"]
        conv_s[0, 16 * c:16 * c + 16] = o["convs"].reshape(16, 2, AW)
    return (y_prompt, y_sample, mem_k, mem_v, conv_p, conv_s, chunk_v)


def kernel(**inputs):
    in_maps = _make_in_maps(**inputs)
    nc = _program()
    res = run_bass_kernel_spmd(nc, in_maps, core_ids=list(range(NCORES)))
    return _assemble(res.results)
```

```python
import numpy as np
from contextlib import ExitStack
import concourse.bass as bass
import concourse.mybir as mybir
from concourse.bass_utils import run_bass_kernel_spmd

F32 = mybir.dt.float32
BF16 = mybir.dt.bfloat16
AF = mybir.ActivationFunctionType
ALU = mybir.AluOpType

NCORES = 8
P = 128
D = 2048
KC = 16
NPB = 512
NS = 64
NSB = 66
T = NPB + NSB
NGROUPS = 2
AW = 1024
DFF = 8192
EPS = 1e-6
INW = 9216
OFF_U, OFF_V, OFF_B, OFF_C, OFF_XIN, OFF_GA, OFF_GB = 0, 1024, 2048, 3072, 4096, 5120, 7168
NBUF = 3
SCALE = 512 ** -0.5

DEBUG_DUMP = None


class Sched:
    def __init__(self, engines, ndma_slots):
        self.engines = list(engines)
        self.streams = {e: [] for e in engines}
        self.count = {e: 0 for e in engines}
        self.seen = {e: {} for e in engines}
        self.lastw = {}
        self.readers = {}
        self.dma_n = {e: 0 for e in ndma_slots}
        self.dma_slots = dict(ndma_slots)
        self.dma_uses = {}

    def _deps(self, reads, writes):
        deps = []
        for k in reads:
            t = self.lastw.get(k)
            if t is not None:
                deps.append(t)
        for k in writes:
            t = self.lastw.get(k)
            if t is not None:
                deps.append(t)
            deps.extend(self.readers.get(k, ()))
        return deps

    def _waits(self, eng, deps):
        need = {}
        for (s, v) in deps:
            if self.seen[eng].get(s, 0) < v and need.get(s, 0) < v:
                need[s] = v
        for s, v in need.items():
            self.seen[eng][s] = v
        return list(need.items())

    def _record(self, tok, reads, writes):
        for k in reads:
            self.readers.setdefault(k, []).append(tok)
        for k in writes:
            self.lastw[k] = tok
            self.readers[k] = []

    @staticmethod
    def _excl(reads, writes):
        r2, w2 = [], list(writes)
        for k in reads:
            if isinstance(k, tuple) and k[0] in ("accP", "accS", "misc") or k == "miscb":
                w2.append(k)
            else:
                r2.append(k)
        return r2, w2

    dry = False

    def op(self, eng, fn, reads=(), writes=()):
        if self.dry:
            return None
        reads, writes = self._excl(reads, writes)
        waits = self._waits(eng, self._deps(reads, writes))
        self.count[eng] += 1
        tok = (eng, self.count[eng])
        self.streams[eng].append((waits, fn, eng, 1))
        self._record(tok, reads, writes)
        return tok

    def dma(self, eng, fn, reads=(), writes=()):
        if self.dry:
            return None
        n = self.dma_n[eng]
        self.dma_n[eng] = n + 1
        slot = "%s_d%d" % (eng, n % self.dma_slots[eng])
        uses = self.dma_uses.get(slot, 0)
        deps = self._deps(reads, writes)
        if uses > 0:
            deps.append((slot, 16 * uses))
        waits = self._waits(eng, deps)
        self.dma_uses[slot] = uses + 1
        tok = (slot, 16 * (uses + 1))
        self.streams[eng].append((waits, fn, slot, 16))
        self._record(tok, reads, writes)
        return tok

    def alias(self, new_keys, old_keys):
        if self.dry:
            return
        toks = []
        for k in old_keys:
            t = self.lastw.get(k)
            if t is not None:
                toks.append(t)
            toks.extend(self.readers.get(k, ()))
        for k in new_keys:
            self.readers.setdefault(k, []).extend(toks)

    def sem_names(self):
        names = list(self.engines)
        for e, n in self.dma_slots.items():
            names += ["%s_d%d" % (e, i) for i in range(n)]
        return names


def build_program():
    nc = bass.Bass("TRN2", target_bir_lowering=False)

    def din(name, shape):
        return nc.dram_tensor(name, list(shape), F32, kind="ExternalInput").ap()

    def dout(name, shape):
        return nc.dram_tensor(name, list(shape), F32, kind="ExternalOutput").ap()

    xg = din("xg", [NGROUPS, T, D])
    sconv = din("sconv", [32, AW])
    ck = din("ck", [16, 256, D])
    cv = din("cv", [16, 256, D])
    mem = din("mem", [256, D])
    w_in = din("w_in", [D, INW])
    w_a = din("w_a", [AW, D])
    w_b = din("w_b", [AW, D])
    w_mix = din("w_mix", [D, D])
    w_q = din("w_q", [D, D])
    w_k = din("w_k", [D, D])
    w_v = din("w_v", [D, D])
    w_xo = din("w_xo", [D, D])
    w_up = din("w_up", [D, DFF])
    w_down = din("w_down", [DFF, D])
    gains = din("gains", [P, 5, KC])
    cwt = din("cwt", [P, 8, 3])
    lnv = din("lnv", [2, AW])
    wsp = din("wsp", [P, 4, P])
    wsb = din("wsb", [NS, 4, NS])
    bsp = din("bsp", [1, 4 * P])
    ident = din("ident", [P, P])
    tril = din("tril", [P, P])

    y_o = dout("y", [NGROUPS, NPB + NS, D])
    mk_o = dout("mk", [P, D])
    mv_o = dout("mv", [P, D])
    convp_o = dout("convp", [2, AW])
    convs_o = dout("convs", [32, AW])
    chunkv_o = dout("chunkv", [NGROUPS, NS, AW])
    dbg_o = dout("dbg", [KC, P, T]) if DEBUG_DUMP else None

    S = Sched(["pe", "act", "dve", "pool", "sp"], {"sp": 8, "pool": 10})

    es = ExitStack()
    with es:
        def sb(name, shape, dt=F32):
            return es.enter_context(nc.sbuf_tensor(name, list(shape), dt))

        def pst(name, shape, dt=F32):
            return es.enter_context(nc.psum_tensor(name, list(shape), dt))

        R = sb("R", [P, KC, T])
        X = sb("X", [P, KC, T], BF16)
        Y = sb("Y", [P, KC, T], BF16)
        M = sb("M", [P, KC, T], BF16)
        wbuf = sb("wbuf", [P, NBUF, KC, 512], BF16)
        KT = sb("KT", [P, KC, 256], BF16)
        Vbf = sb("Vbf", [P, 2, D], BF16)
        xt = sb("xt", [P, 2, 1024])
        sq = sb("sq", [P, 2, T], BF16)
        stat = sb("stat", [P, 32])
        scrA = sb("scrA", [P, 4, T])
        scrB = sb("scrB", [P, 3072])
        vraw = scrB[:, 0:AW]
        scrC = sb("scrC", [P, 2, T])
        pT = sb("pT", [P, 2, NPB], BF16)
        pT2 = sb("pT2", [P, 2, NPB], BF16)
        pTs = sb("pTs", [P, 2, 4, NS], BF16)
        rinvs = sb("rinvs", [P, 4, NS])
        vbf = M[:, :, :].rearrange("p k t -> p (k t)")[:, 0:5 * AW].rearrange("p (a b) -> p a b", a=5)
        G = scrA
        scrA_flat = scrA[:, :, :].rearrange("p a t -> p (a t)")
        lng_bc = scrA_flat[:, 0:AW]
        lnb_bc = scrA_flat[:, AW:2 * AW]
        rinv = scrA[:, 3, 0:NPB]
        Csb = scrB[:, 0:2 * T].rearrange("p (a t) -> p a t", a=2)
        convb = scrB[:, 2 * T:2 * T + 2 * (NPB + NS)].rearrange("p (a t) -> p a t", a=2)
        pbuf = scrB[:, 2308:2308 + 2 + NPB + 80]
        kb = scrB[:, 0:1024].bitcast(BF16).rearrange("p (a b c) -> p a b c", a=2, b=2)
        vb = scrB[:, 1024:2048].bitcast(BF16).rearrange("p (a b c) -> p a b c", a=2, b=2)
        kbT = scrB[:, 2048:3072].bitcast(BF16).rearrange("p (a b c) -> p a b c", a=2, b=4)
        usb = scrC[:, 0, :]
        rbc = scrC[:, 0, :]
        tz = scrC[:, 1, :]
        kst = scrC[:, :, 0:512]
        KSTKEY = ["usb", "tz"]
        VRAW = [scrB[:, 0:AW], scrB[:, AW:2 * AW]]
        VR_KEYS = ["vraw", ("vraw", 1)]
        STATV = [stat[:, 0:16], stat[:, 16:32]]
        STG = [xt[:, 0, :], xt[:, 1, :], scrB[:, 0:1024], scrB[:, 1024:2048], scrB[:, 2048:3072]]
        STG_KEYS = [("xt", 0), ("xt", 1), ("stg", 2), ("stg", 3), ("stg", 4)]
        XKEYS = [("X", k) for k in range(KC)]
        pend = {"fn": None}
        LNG_KEYS = [("G", 0), ("G", 1)]
        LNB_KEYS = [("G", 1), ("G", 2), ("G", 3)]
        PHA_KEYS = [("Csb", 0), ("Csb", 1), ("convb", 0), ("convb", 1), "pbuf"]
        ATT_KEYS = [("kb", 0), ("kb", 1), ("vb", 0), ("vb", 1), ("kbT", 0), ("kbT", 1)]
        ident_f = sb("ident_f", [P, P])
        ident_b = sb("ident_b", [P, P], BF16)
        ones_b = sb("ones_b", [P, P], BF16)
        epsc = sb("epsc", [P, 1])
        gn = sb("gn", [P, 5, KC])
        cw = sb("cw", [P, 8, 3])
        bsp_bc = sb("bsp_bc", [P, 4, P])
        WsT = sb("WsT", [P, 4, P], BF16)
        WsTs = sb("WsTs", [NS, 4, NS], BF16)
        wst = vraw[:, 0:512].rearrange("p (g t) -> p g t", g=4)
        wsts = vraw[0:NS, 512:768].rearrange("p (g t) -> p g t", g=4)
        trilm = sb("trilm", [P, P])
        sconv_sb = xt[0:32, 0, :]
        sconvT = sb("sconvT", [P, 8, 32])
        pl = sb("pl", [P, 8, 2])
        pss = sb("pss", [P, 8, 16, 2])
        cst = xt[0:32, 1, :]

        accP = [pst("accP%d" % i, [P, NPB]) for i in range(4)]
        accS2 = [pst("accS%d" % i, [P, NPB]) for i in range(2)]
        misc = [pst("misc0", [P, 512])]

        def aS(i):
            return accS2[i % 2]

        miscb = pst("miscb", [P, 1024], BF16)
        stat_pb = misc[0][:, 0:NPB]
        stat_sb = miscb[:, 0:2 * NSB].bitcast(F32)
        STATKEYS = [("misc", 0), "miscb"]

        sem_names = S.sem_names()
        sems = {n: es.enter_context(nc.semaphore("s_" + n)) for n in sem_names}
        block = es.enter_context(nc.Block())

        st = {"acc": 0, "misc": 0, "ev": 0}

        def next_acc():
            i = st["acc"] % 4
            st["acc"] += 1
            return i

        def next_misc():
            return 0

        TRB = [(accP[i], ("accP", i)) for i in range(4)]

        def next_tr():
            st["tr"] = st.get("tr", 0) + 1
            return TRB[st["tr"] % len(TRB)]

        def ev_eng():
            st["ev"] += 1
            return "act" if st["ev"] % 2 else "dve"

        def load_const(dst, src, key, eng="sp"):
            S.dma(eng, lambda e, d=dst, s=src: e.dma_start(out=d, in_=s), writes=[key])

        def mm_group(i, terms_pb, terms_sb, reads, nsb=NSB):
            def fn(e):
                ins = None
                n = len(terms_pb)
                for j in range(n):
                    if terms_pb:
                        l, r = terms_pb[j]
                        ins = e.matmul(accP[i][:, :], lhsT=l, rhs=r, start=(j == 0), stop=(j == n - 1))
                    if terms_sb:
                        l, r = terms_sb[j]
                        ins = e.matmul(aS(i)[:, 0:nsb], lhsT=l, rhs=r, start=(j == 0), stop=(j == n - 1))
                return ins
            S.op("pe", fn, reads=reads, writes=acc_reads(i))

        def lin_chunk(src, srckey, wb, kcn, colsl):
            i = next_acc()
            tp = [(wb[:, k, colsl], src[:, k, 0:NPB]) for k in range(kcn)]
            ts = [(wb[:, k, colsl], src[:, k, NPB:T]) for k in range(kcn)]
            mm_group(i, tp, ts, reads=[srckey, "wbuf"])
            return i

        def acc_reads(i):
            return [("accP", i), ("accS", i % 2)]

        slab_state = {"n": 0}
        stages = []

        def wview(w, r0, nk, c0, ncol):
            return w[r0:r0 + nk * P, c0:c0 + ncol].rearrange("(k p) n -> p k n", p=P)

        all_stages = []
        mode = {"si": 0, "issued": 0, "half0": set()}

        def issue_load(idx, only_half=None):
            parts = all_stages[idx][0]
            b = idx % NBUF
            for pi, (kofs, nk, cofs, ncol, src) in enumerate(parts):
                if ncol == 512:
                    pieces = [(0, 256, 0), (256, 256, 1)]
                else:
                    pieces = [(0, ncol, pi if len(parts) == 2 else cofs // 256)]
                for (c0, nc_, half) in pieces:
                    if only_half is not None and half != only_half:
                        continue
                    if only_half is None and half == 0 and idx in mode["half0"]:
                        continue
                    S.dma("pool",
                          lambda e, b=b, kofs=kofs, nk=nk, cofs=cofs, c0=c0, nc_=nc_, src=src:
                          e.dma_start(out=wbuf[:, b, kofs:kofs + nk, cofs + c0:cofs + c0 + nc_],
                                      in_=src[:, :, c0:c0 + nc_]),
                          writes=[("wbuf", b, half)])

        def wk(wkey, colsl=None):
            if colsl is None:
                return [(wkey[0], wkey[1], 0), (wkey[0], wkey[1], 1)]
            return [(wkey[0], wkey[1], colsl.start // 256)]

        def ensure_loaded(upto):
            while mode["issued"] <= upto and mode["issued"] < len(all_stages):
                issue_load(mode["issued"])
                mode["issued"] += 1

        def add_stage(parts, fn, *flags):
            if S.dry:
                all_stages.append((parts, flags))
                return
            i = mode["si"]
            if i >= 1 and "defer" in all_stages[i - 1][1]:
                ensure_loaded(i)
            else:
                ensure_loaded(i + NBUF - 1)
            fn(wbuf[:, i % NBUF], ("wbuf", i % NBUF))
            ensure_loaded(i + NBUF - 1)
            j = i + NBUF
            flags_i = all_stages[i][1]
            if "defer" not in flags_i and "noahead" not in flags_i and j < len(all_stages) \
                    and j not in mode["half0"]:
                issue_load(j, only_half=0)
                mode["half0"].add(j)
            mode["si"] += 1

        def run_stages():
            pass

        def lin_chunk_b(src, srckey, wb, wkey, kcn, colsl, koff=0):
            i = next_acc()
            if st.get("split") and srckey is XKEYS:
                st["split"] = False
                for k in range(kcn):
                    def fk(e, k=k, i=i):
                        e.matmul(accP[i][:, :], lhsT=wb[:, koff + k, colsl], rhs=src[:, k, 0:NPB], start=(k == 0),
                                 stop=(k == kcn - 1))
                        return e.matmul(aS(i)[:, 0:NSB], lhsT=wb[:, koff + k, colsl], rhs=src[:, k, NPB:T],
                                        start=(k == 0), stop=(k == kcn - 1))
                    S.op("pe", fk, reads=[("X", k)] + wk(wkey, colsl), writes=acc_reads(i))
                return i
            tp = [(wb[:, koff + k, colsl], src[:, k, 0:NPB]) for k in range(kcn)]
            ts = [(wb[:, koff + k, colsl], src[:, k, NPB:T]) for k in range(kcn)]
            mm_group(i, tp, ts, reads=list(srckey) + wk(wkey, colsl))
            return i

        def ew(eng, fn, reads, writes):
            S.op(eng, fn, reads=reads, writes=writes)

        ENG = {}

        load_const(ident_f[:, :], ident[:, :], "ident_f")
        load_const(trilm[:, :], tril[:, :], "trilm")
        load_const(gn[:, :, :], gains[:, :, :], "gn")
        load_const(cw[:, :, :], cwt[:, :, :], "cw")
        load_const(wst, wsp[:, :, :], "vraw")
        load_const(wsts, wsb[:, :, :], "vraw")
        load_const(sconv_sb, sconv[:, :], ("xt", 0))
        S.dma("sp", lambda e: e.dma_start(out=bsp_bc[:, :, :].rearrange("p g t -> p (g t)").unsqueeze(1),
                                          in_=bsp[0:1, :].partition_broadcast(P)),
              writes=["bsp_bc"])
        ew("dve", lambda e: e.tensor_copy(out=ident_b[:, :], in_=ident_f[:, :]), ["ident_f"], ["ident_b"])
        ew("dve", lambda e: e.memset(ones_b[:, :], 1.0), [], ["ones_b"])
        ew("dve", lambda e: e.memset(epsc[:, :], EPS), [], ["epsc"])
        ew("dve", lambda e: e.memset(Y[:, :, :], 0.0), [], ["Y"])
        ew("dve", lambda e: e.tensor_tensor(out=wst[:, :, :], in0=wst[:, :, :],
                                            in1=trilm[:, None, :].to_broadcast([P, 4, P]), op=ALU.mult),
           ["vraw", "trilm"], ["vraw"])
        ew("dve", lambda e: e.tensor_tensor(out=wsts[:, :, :], in0=wsts[:, :, :],
                                            in1=trilm[0:NS, None, 0:NS].to_broadcast([NS, 4, NS]), op=ALU.mult),
           ["vraw", "trilm"], ["vraw"])
        mi = next_misc()

        def f_wst(e, mi=mi):
            ins = None
            for g in range(4):
                ins = e.transpose(out=misc[mi][:, g * P:(g + 1) * P], in_=wst[:, g, :], identity=ident_f[:, :])
            return ins
        S.op("pe", f_wst, reads=["vraw", "ident_f"], writes=[("misc", mi)])
        ew("dve", lambda e, mi=mi: e.tensor_copy(out=WsT[:, :, :].rearrange("p g t -> p (g t)"), in_=misc[mi][:, :]),
           [("misc", mi)], ["WsT"])
        mi = next_misc()

        def f_wsts(e, mi=mi):
            ins = None
            for g in range(4):
                ins = e.transpose(out=misc[mi][0:NS, g * NS:(g + 1) * NS], in_=wsts[:, g, :],
                                  identity=ident_f[0:NS, 0:NS])
            return ins
        S.op("pe", f_wsts, reads=["vraw", "ident_f"], writes=[("misc", mi)])
        ew("dve", lambda e, mi=mi: e.tensor_copy(out=WsTs[:, :, :].rearrange("p g t -> p (g t)"),
                                                  in_=misc[mi][0:NS, 0:4 * NS]),
           [("misc", mi)], ["WsTs"])
        mi = next_misc()

        def f_sc(e, mi=mi):
            ins = None
            for c in range(8):
                ins = e.transpose(out=misc[mi][:, c * 32:(c + 1) * 32], in_=sconv_sb[:, c * P:(c + 1) * P],
                                  identity=ident_f[0:32, 0:32])
            return ins
        S.op("pe", f_sc, reads=[("xt", 0), "ident_f"], writes=[("misc", mi)])
        ew("dve", lambda e, mi=mi: e.tensor_copy(out=sconvT[:, :, :].rearrange("p c b -> p (c b)"),
                                                  in_=misc[mi][:, 0:256]),
           [("misc", mi)], ["sconvT"])

        def to_feature_major(src_tile, rows, dst, dstkey, col0, k0, nk, srckey):
            for r0 in range(0, nk, 4):
                tb, tkey = next_tr()

                def f(e, tb=tb, r0=r0):
                    ins = None
                    for kk in range(4):
                        ins = e.transpose(out=tb[:, kk * P:kk * P + rows],
                                          in_=src_tile[0:rows, (r0 + kk) * P:(r0 + kk + 1) * P],
                                          identity=ident_f[0:rows, 0:rows])
                    return ins
                S.op("pe", f, reads=[srckey, "ident_f"], writes=[tkey])
                eng = ev_eng()
                src = tb[:, :].rearrange("p (k t) -> p k t", k=4)[:, :, 0:rows]
                out = dst[:, k0 + r0:k0 + r0 + 4, col0:col0 + rows]
                if eng == "act":
                    ew("act", lambda e, o=out, s=src: e.copy(out=o, in_=s), [tkey], [dstkey])
                else:
                    ew("dve", lambda e, o=out, s=src: e.tensor_copy(out=o, in_=s), [tkey], [dstkey])

        def stats_push(k):
            sl = st.get("sqn", 0) % 2
            st["sqn"] = st.get("sqn", 0) + 1
            ew("act", lambda e, k=k, sl=sl: e.activation(out=sq[:, sl, :], in_=R[:, k, :], func=AF.Square),
               ["R"], [("sq", sl)])

            def f(e, k=k, sl=sl):
                e.matmul(stat_pb, lhsT=ones_b[:, :], rhs=sq[:, sl, 0:NPB], start=(k == 0), stop=(k == KC - 1))
                return e.matmul(stat_sb, lhsT=ones_b[:, :], rhs=sq[:, sl, NPB:T], start=(k == 0), stop=(k == KC - 1))
            pend["fn"] = lambda: S.op("pe", f, reads=[("sq", sl), "ones_b"], writes=STATKEYS)

        def stats_flush():
            if pend["fn"] is not None:
                fn = pend["fn"]
                pend["fn"] = None
                fn()

        def rms_rbc():
            stats_flush()
            ew("act", lambda e: e.activation(out=rbc[:, 0:NPB], in_=stat_pb, func=AF.Sqrt,
                                             scale=1.0 / D, bias=epsc[:, 0:1]), STATKEYS + ["epsc"], ["usb"])
            ew("act", lambda e: e.activation(out=rbc[:, NPB:T], in_=stat_sb, func=AF.Sqrt,
                                             scale=1.0 / D, bias=epsc[:, 0:1]), STATKEYS + ["epsc"], ["usb"])
            ew("dve", lambda e: e.reciprocal(out=rbc[:, :], in_=rbc[:, :]), ["usb"], ["usb"])

        def rmsnorm_fm(gidx, dst, dstkey, fused=False):
            if not fused:
                for k in range(KC):
                    stats_flush()
                    stats_push(k)
            rms_rbc()
            for k in range(KC):
                ew("dve", lambda e, k=k: e.scalar_tensor_tensor(out=dst[:, k, :], in0=R[:, k, :],
                                                                 scalar=gn[:, gidx, k:k + 1], in1=rbc[:, :],
                                                                 op0=ALU.mult, op1=ALU.mult),
                   ["R", "usb", "gn"], [("X", k)])
            st["split"] = True

        def resid_add(i, k):
            ew("dve", lambda e, i=i, k=k: e.tensor_tensor(out=R[:, k, 0:NPB], in0=accP[i][:, :], in1=R[:, k, 0:NPB],
                                                           op=ALU.add),
               acc_reads(i) + ["R"], ["R"])
            ew("dve", lambda e, i=i, k=k: e.tensor_tensor(out=R[:, k, NPB:T], in0=aS(i)[:, 0:NSB],
                                                           in1=R[:, k, NPB:T], op=ALU.add),
               acc_reads(i) + ["R"], ["R"])

        def evac_copy(i, dst3, k, dstkey):
            eng = ev_eng()
            if eng == "act":
                ew("act", lambda e, i=i, k=k: e.copy(out=dst3[:, k, 0:NPB], in_=accP[i][:, :]), acc_reads(i), [dstkey])
                ew("act", lambda e, i=i, k=k: e.copy(out=dst3[:, k, NPB:T], in_=aS(i)[:, 0:NSB]), acc_reads(i),
                   [dstkey])
            else:
                ew("dve", lambda e, i=i, k=k: e.tensor_copy(out=dst3[:, k, 0:NPB], in_=accP[i][:, :]), acc_reads(i),
                   [dstkey])
                ew("dve", lambda e, i=i, k=k: e.tensor_copy(out=dst3[:, k, NPB:T], in_=aS(i)[:, 0:NSB]),
                   acc_reads(i), [dstkey])

        def group(g):
            S.alias(["R"], [("Rf", k) for k in range(KC)])
            S.alias(STG_KEYS[2:], ATT_KEYS + PHA_KEYS + VR_KEYS)
            if g == 0:
                mnT = M
                for mt in range(2):
                    for h in range(2):
                        s = (mt * 2 + h) % 2
                        S.dma("sp", lambda e, mt=mt, h=h, s=s: e.dma_start(
                            out=xt[:, s, :], in_=mem[mt * P:(mt + 1) * P, h * 1024:(h + 1) * 1024]),
                            writes=[("xt", s)])
                        ew("act", lambda e, s=s, mt=mt, h=h: e.activation(out=tz[:, 0:512], in_=xt[:, s, 0:512],
                                                                           func=AF.Square,
                                                                           accum_out=stat[:, 10 + 2 * h:11 + 2 * h]),
                           [("xt", s)], ["tz", ("stat", 0)])
                        ew("act", lambda e, s=s, mt=mt, h=h: e.activation(out=tz[:, 0:512], in_=xt[:, s, 512:1024],
                                                                           func=AF.Square,
                                                                           accum_out=stat[:, 11 + 2 * h:12 + 2 * h]),
                           [("xt", s)], ["tz", ("stat", 0)])
                    ew("dve", lambda e: e.tensor_reduce(out=stat[:, 14:15], in_=stat[:, 10:14],
                                                        axis=mybir.AxisListType.X, op=ALU.add), [("stat", 0)], [("stat", 0)])
                    ew("act", lambda e: e.activation(out=stat[:, 15:16], in_=stat[:, 14:15], func=AF.Sqrt,
                                                      scale=1.0 / D, bias=epsc[:, 0:1]), [("stat", 0), "epsc"], [("stat", 0)])
                    ew("dve", lambda e: e.reciprocal(out=stat[:, 15:16], in_=stat[:, 15:16]), [("stat", 0)], [("stat", 0)])
                    for h in range(2):
                        ew("act", lambda e, h=h: e.activation(out=xt[:, h, :], in_=xt[:, h, :], func=AF.Copy,
                                                               scale=stat[:, 15:16]), [("xt", h), ("stat", 0)], [("xt", h)])
                        for r0 in range(0, 8, 4):
                            mi = next_misc()

                            def f(e, mi=mi, r0=r0, h=h):
                                ins = None
                                for kk in range(4):
                                    ins = e.transpose(out=misc[mi][:, kk * P:(kk + 1) * P],
                                                      in_=xt[:, h, (r0 + kk) * P:(r0 + kk + 1) * P],
                                                      identity=ident_f[:, :])
                                return ins
                            S.op("pe", f, reads=[("xt", h), "ident_f"], writes=[("misc", mi)])
                            k0 = h * 8 + r0
                            ew("dve", lambda e, mi=mi, k0=k0, mt=mt: e.tensor_tensor(
                                out=mnT[:, k0:k0 + 4, mt * P:(mt + 1) * P],
                                in0=misc[mi][:, :].rearrange("p (k t) -> p k t", k=4),
                                in1=gn[:, 4, k0:k0 + 4].unsqueeze(2).to_broadcast([P, 4, P]), op=ALU.mult),
                               [("misc", mi), "gn"], ["M"])


            tiles = [(t * P, P) for t in range(4)] + [(NPB, NSB)]
            hn = 0
            for (r0, rows) in tiles:
                for h in range(2):
                    sl = hn % len(STG)
                    hn += 1
                    S.dma("sp", lambda e, r0=r0, rows=rows, h=h, sl=sl:
                          e.dma_start(out=STG[sl][0:rows, :], in_=xg[g, r0:r0 + rows, h * 1024:(h + 1) * 1024]),
                          writes=[STG_KEYS[sl]])
                    to_feature_major(STG[sl], rows, R, "R", r0, h * 8, 8, STG_KEYS[sl])
                sdst = stat_pb[:, r0:r0 + rows] if r0 < NPB else stat_sb[:, 0:rows]
                for k in range(KC):
                    sl2 = st.get("sqn", 0) % 2
                    st["sqn"] = st.get("sqn", 0) + 1
                    ew("act", lambda e, k=k, sl2=sl2, r0=r0, rows=rows: e.activation(
                        out=sq[:, sl2, 0:rows], in_=R[:, k, r0:r0 + rows], func=AF.Square), ["R"], [("sq", sl2)])
                    S.op("pe", lambda e, k=k, sl2=sl2, rows=rows, sdst=sdst: e.matmul(
                        sdst, lhsT=ones_b[:, :], rhs=sq[:, sl2, 0:rows], start=(k == 0), stop=(k == KC - 1)),
                        reads=[("sq", sl2), "ones_b"], writes=STATKEYS)
            S.alias(PHA_KEYS + VR_KEYS, STG_KEYS[2:] + ATT_KEYS)
            if g == 0:
                mnT = M
                def st_kv(which, s):
                    def fn(wb, wkey):
                        outd = mk_o if which == "k" else mv_o
                        for mt in range(2):
                            i = next_acc()

                            def f(e, i=i, mt=mt):
                                ins = None
                                for k in range(KC):
                                    ins = e.matmul(accP[i][:, :], lhsT=mnT[:, k, mt * P:(mt + 1) * P], rhs=wb[:, k, :],
                                                   start=(k == 0), stop=(k == KC - 1))
                                return ins
                            S.op("pe", f, reads=["M"] + wk(wkey), writes=acc_reads(i))
                            if mt == 0:
                                ks = st["ev"] % 2
                                ew("act", lambda e, i=i, ks=ks: e.copy(out=kst[:, ks, :], in_=accP[i][:, :]),
                                   acc_reads(i), [KSTKEY[ks]])
                                S.dma("sp", lambda e, ks=ks, s=s: e.dma_start(out=outd[:, s * 512:(s + 1) * 512],
                                                                              in_=kst[:, ks, :]),
                                      reads=[KSTKEY[ks]])
                                st["ev"] += 1
                            if which == "v":
                                ew("dve", lambda e, i=i, mt=mt, s=s: e.tensor_copy(
                                    out=Vbf[:, mt, s * 512:(s + 1) * 512], in_=accP[i][:, :]), acc_reads(i), ["Vbf"])
                        if which == "k":
                            for cc in range(4):
                                i = next_acc()

                                def f2(e, i=i, cc=cc):
                                    ins = None
                                    for k in range(KC):
                                        ins = e.matmul(accP[i][:, 0:256], lhsT=wb[:, k, cc * P:(cc + 1) * P],
                                                       rhs=mnT[:, k, 0:256], start=(k == 0), stop=(k == KC - 1))
                                    return ins
                                S.op("pe", f2, reads=["M"] + wk(wkey, slice(cc * P, (cc + 1) * P)), writes=acc_reads(i))
                                ew("dve", lambda e, i=i, cc=cc, s=s: e.tensor_copy(out=KT[:, 4 * s + cc, :],
                                                                                    in_=accP[i][:, 0:256]),
                                   acc_reads(i), ["KT"])
                    return fn
                for s in range(4):
                    add_stage(*([(0, KC, 0, 512, wview(w_k, 0, KC, s * 512, 512))], st_kv("k", s)))
                for s in range(4):
                    add_stage(*([(0, KC, 0, 512, wview(w_v, 0, KC, s * 512, 512))], st_kv("v", s)))

            rmsnorm_fm(0, X, "X", fused=True)
            S.dma("sp", lambda e: e.dma_start(out=lng_bc.unsqueeze(1), in_=lnv[0:1, :].partition_broadcast(P)),
                  writes=LNG_KEYS)
            S.dma("sp", lambda e: e.dma_start(out=lnb_bc.unsqueeze(1), in_=lnv[1:2, :].partition_broadcast(P)),
                  writes=LNB_KEYS)

            vstate = {}

            def st_v0(wb, wkey):
                vstate["wb0"] = wb
                vstate["k0"] = wkey

            def st_v1(wb, wkey):
                wbs = [vstate["wb0"], wb]
                wkeys = [vstate["k0"], wkey]
                for ti, (r0, rows) in enumerate(tiles):
                    vpar = ti % 2
                    vraw = VRAW[vpar]
                    vkey = VR_KEYS[vpar]
                    stat = STATV[vpar]
                    skey = (("stat", 0), vpar)
                    banks = []
                    for h in range(2):
                        i = next_acc()
                        banks.append(i)

                        def f(e, i=i, h=h, r0=r0, rows=rows, ks=range(KC)):
                            ins = None
                            for k in ks:
                                ins = e.matmul(accP[i][0:rows, :], lhsT=X[:, k, r0:r0 + rows], rhs=wbs[h][:, k, :],
                                               start=(k == 0), stop=(k == KC - 1))
                            return ins
                        if st.get("split"):
                            st["split"] = False
                            for k in range(KC):
                                S.op("pe", lambda e, f=f, k=k: f(e, ks=[k]), reads=[("X", k)] + wk(wkeys[h]),
                                     writes=acc_reads(i))
                        else:
                            S.op("pe", f, reads=XKEYS + wk(wkeys[h]), writes=acc_reads(i))
                    for h in range(2):
                        i = banks[h]
                        ew("act", lambda e, i=i, h=h, rows=rows, vraw=vraw, stat=stat: e.activation(
                            out=vraw[0:rows, h * 512:(h + 1) * 512], in_=accP[i][0:rows, :], func=AF.Copy,
                            accum_out=stat[0:rows, h:h + 1]),
                           acc_reads(i), [vkey, skey])
                        ew("act", lambda e, i=i, h=h, rows=rows, vraw=vraw, stat=stat: e.activation(
                            out=tz[0:rows, 0:512], in_=accP[i][0:rows, :], func=AF.Square,
                            accum_out=stat[0:rows, 2 + h:3 + h]),
                           acc_reads(i), ["tz", skey])
                    rs = slice(0, rows)
                    ew("dve", lambda e, rs=rs, vraw=vraw, stat=stat: e.tensor_tensor(out=stat[rs, 4:5], in0=stat[rs, 0:1], in1=stat[rs, 1:2],
                                                                op=ALU.add), [skey], [skey])
                    ew("dve", lambda e, rs=rs, vraw=vraw, stat=stat: e.tensor_tensor(out=stat[rs, 5:6], in0=stat[rs, 2:3], in1=stat[rs, 3:4],
                                                                op=ALU.add), [skey], [skey])
                    ew("dve", lambda e, rs=rs, vraw=vraw, stat=stat: e.tensor_scalar(out=stat[rs, 4:6], in0=stat[rs, 4:6], scalar1=1.0 / AW,
                                                                scalar2=None, op0=ALU.mult), [skey], [skey])
                    ew("dve", lambda e, rs=rs, vraw=vraw, stat=stat: e.tensor_tensor(out=stat[rs, 6:7], in0=stat[rs, 4:5], in1=stat[rs, 4:5],
                                                                op=ALU.mult), [skey], [skey])
                    ew("dve", lambda e, rs=rs, vraw=vraw, stat=stat: e.tensor_tensor(out=stat[rs, 7:8], in0=stat[rs, 5:6], in1=stat[rs, 6:7],
                                                                op=ALU.subtract), [skey], [skey])
                    ew("act", lambda e, rs=rs, vraw=vraw, stat=stat: e.activation(out=stat[rs, 8:9], in_=stat[rs, 7:8], func=AF.Sqrt,
                                                             scale=1.0, bias=epsc[rs, 0:1]),
                       [skey, "epsc"], [skey])
                    ew("dve", lambda e, rs=rs, vraw=vraw, stat=stat: e.reciprocal(out=stat[rs, 8:9], in_=stat[rs, 8:9]), [skey], [skey])
                    ew("dve", lambda e, rs=rs, vraw=vraw, stat=stat: e.scalar_tensor_tensor(out=stat[rs, 9:10], in0=stat[rs, 4:5],
                                                                       scalar=-1.0, in1=stat[rs, 8:9], op0=ALU.mult,
                                                                       op1=ALU.mult), [skey], [skey])
                    ew("dve", lambda e, rs=rs, vraw=vraw, stat=stat: e.tensor_scalar(
                        out=vraw[rs, :], in0=vraw[rs, :], scalar1=stat[rs, 8:9], scalar2=stat[rs, 9:10],
                        op0=ALU.mult, op1=ALU.add), [vkey, skey], [vkey])
                    ew("dve", lambda e, rs=rs, vraw=vraw: e.tensor_tensor(out=vraw[rs, :], in0=vraw[rs, :],
                                                                           in1=lng_bc[rs, :], op=ALU.mult),
                       [vkey] + LNG_KEYS, [vkey])
                    if ti != 4:
                        ew("dve", lambda e, rs=rs, ti=ti, vraw=vraw: e.tensor_tensor(
                            out=vbf[rs, ti, :], in0=vraw[rs, :], in1=lnb_bc[rs, :], op=ALU.add),
                           [vkey] + LNB_KEYS, ["M"])
                    else:
                        ew("dve", lambda e, rs=rs, vraw=vraw: e.tensor_tensor(out=vraw[rs, :], in0=vraw[rs, :],
                                                                               in1=lnb_bc[rs, :], op=ALU.add),
                           [vkey] + LNB_KEYS, [vkey])
                        ew("act", lambda e, rs=rs, ti=ti, vraw=vraw: e.copy(out=vbf[rs, ti, :], in_=vraw[rs, :]),
                           [vkey], ["M"])
                    if ti == 4:
                        S.dma("sp", lambda e, vraw=vraw: e.dma_start(out=chunkv_o[g, :, :], in_=vraw[0:NS, :]), reads=[vkey])

            add_stage(*([(0, KC, 0, 512, wview(w_in, 0, KC, OFF_V, 512))], st_v0, "defer"))
            add_stage(*([(0, KC, 0, 512, wview(w_in, 0, KC, OFF_V + 512, 512))], st_v1))

            def st_alpha(q):
                def fn(wb, wkey):
                    if q == 0:
                        S.alias(PHA_KEYS, VR_KEYS)
                    for cc in range(2):
                        c = 2 * q + cc
                        i = lin_chunk_b(X, XKEYS, wb, wkey, KC, slice(cc * P, (cc + 1) * P))
                        evac_copy(i, Csb, cc, ("Csb", cc))
                        i = lin_chunk_b(X, XKEYS, wb, wkey, KC, slice(256 + cc * P, 256 + (cc + 1) * P))
                        rd = acc_reads(i) + [("Csb", cc)]
                        samp = pbuf[:, 2 + NPB:2 + NPB + 80].rearrange("p (b t) -> p b t", t=10)
                        ew("dve", lambda e, i=i, cc=cc: e.tensor_tensor(out=pbuf[:, 2:2 + NPB], in0=accP[i][:, :],
                                                                         in1=Csb[:, cc, 0:NPB], op=ALU.mult),
                           rd, ["pbuf"])
                        ew("dve", lambda e, i=i, cc=cc: e.tensor_tensor(out=pbuf[:, 0:2], in0=aS(i)[:, NS:NSB],
                                                                         in1=Csb[:, cc, NPB + NS:T], op=ALU.mult),
                           rd, ["pbuf"])
                        ew("dve", lambda e, i=i, cc=cc, samp=samp: e.tensor_tensor(
                            out=samp[:, :, 2:10], in0=aS(i)[:, 0:NS].rearrange("p (b t) -> p b t", t=8),
                            in1=Csb[:, cc, NPB:NPB + NS].rearrange("p (b t) -> p b t", t=8), op=ALU.mult),
                           rd, ["pbuf"])
                        ew("act", lambda e, c=c, samp=samp: e.copy(
                            out=samp[:, :, 0:2],
                            in_=sconvT[:, c, :].rearrange("p (b k) -> p b k", k=2)[:, 8 * g:8 * g + 8, :]),
                           ["sconvT"], ["pbuf"])
                        cvp = convb[:, cc, 0:NPB]
                        cvs = convb[:, cc, NPB:NPB + NS].rearrange("p (b t) -> p b t", t=8)
                        ew("act", lambda e, c=c, cvp=cvp: e.activation(out=cvp, in_=pbuf[:, 0:NPB], func=AF.Copy,
                                                                        scale=cw[:, c, 0:1]),
                           ["pbuf", "cw"], [("convb", cc)])
                        ew("act", lambda e, c=c, cvs=cvs, samp=samp: e.activation(out=cvs, in_=samp[:, :, 0:8],
                                                                                   func=AF.Copy, scale=cw[:, c, 0:1]),
                           ["pbuf", "cw"], [("convb", cc)])
                        for kk in (1, 2):
                            ew("dve", lambda e, c=c, cvp=cvp, kk=kk: e.scalar_tensor_tensor(
                                out=cvp, in0=pbuf[:, kk:kk + NPB], scalar=cw[:, c, kk:kk + 1], in1=cvp,
                                op0=ALU.mult, op1=ALU.add), ["pbuf", "cw", ("convb", cc)], [("convb", cc)])
                            ew("dve", lambda e, c=c, cvs=cvs, kk=kk, samp=samp: e.scalar_tensor_tensor(
                                out=cvs, in0=samp[:, :, kk:kk + 8], scalar=cw[:, c, kk:kk + 1], in1=cvs,
                                op0=ALU.mult, op1=ALU.add), ["pbuf", "cw", ("convb", cc)], [("convb", cc)])
                        if g == NGROUPS - 1:
                            ew("act", lambda e, c=c: e.copy(out=pl[:, c, :], in_=pbuf[:, NPB:NPB + 2]),
                               ["pbuf"], ["pl"])
                        ew("act", lambda e, c=c, samp=samp: e.copy(out=pss[:, c, 8 * g:8 * g + 8, :],
                                                                    in_=samp[:, :, 8:10]), ["pbuf"], ["pss"])
                return fn

            def st_beta(q):
                def fn(wb, wkey):
                    for cc in range(2):
                        c = 2 * q + cc
                        gi = c // 2
                        i = lin_chunk_b(X, XKEYS, wb, wkey, KC, slice(cc * P, (cc + 1) * P))
                        ew("dve", lambda e, i=i, cc=cc, c=c: e.tensor_tensor(out=Y[:, 8 + c, 0:NPB], in0=accP[i][:, :],
                                                                              in1=convb[:, cc, 0:NPB], op=ALU.mult),
                           acc_reads(i) + [("convb", cc)], ["Y"])
                        ew("dve", lambda e, i=i, cc=cc, c=c: e.tensor_tensor(out=Y[:, 8 + c, NPB:NPB + NS],
                                                                              in0=aS(i)[:, 0:NS],
                                                                              in1=convb[:, cc, NPB:NPB + NS],
                                                                              op=ALU.mult),
                           acc_reads(i) + [("convb", cc)], ["Y"])
                        i = lin_chunk_b(X, XKEYS, wb, wkey, KC, slice(256 + cc * P, 256 + (cc + 1) * P))
                        ew("act", lambda e, i=i: e.copy(out=usb[:, 0:NPB], in_=accP[i][:, :]), acc_reads(i), ["usb"])
                        ew("act", lambda e, i=i: e.copy(out=usb[:, NPB:T], in_=aS(i)[:, 0:NSB]), acc_reads(i),
                           ["usb"])
                        i = next_acc()

                        def fz(e, i=i, c=c, gi=gi):
                            for t4 in range(4):
                                e.matmul(accP[i][:, t4 * P:(t4 + 1) * P], lhsT=vbf[:, t4, c * P:(c + 1) * P],
                                         rhs=WsT[:, gi, :], start=True, stop=True)
                            return e.matmul(aS(i)[:, 0:NS], lhsT=vbf[0:NS, 4, c * P:(c + 1) * P],
                                            rhs=WsTs[:, gi, :], start=True, stop=True)
                        S.op("pe", fz, reads=["M", "WsT", "WsTs"], writes=acc_reads(i))
                        ew("dve", lambda e, i=i, gi=gi: e.tensor_tensor(
                            out=tz[:, 0:NPB].rearrange("p (a t) -> p a t", t=P),
                            in0=accP[i][:, :].rearrange("p (a t) -> p a t", t=P),
                            in1=bsp_bc[:, gi:gi + 1, :].to_broadcast([P, 4, P]), op=ALU.add),
                           acc_reads(i) + ["bsp_bc"], ["tz"])
                        ew("dve", lambda e, i=i, gi=gi: e.tensor_tensor(
                            out=tz[:, NPB:NPB + NS].rearrange("p (b t) -> p b t", t=8),
                            in0=aS(i)[:, 0:NS].rearrange("p (b t) -> p b t", t=8),
                            in1=bsp_bc[:, gi:gi + 1, 0:8].to_broadcast([P, 8, 8]), op=ALU.add),
                           acc_reads(i) + ["bsp_bc"], ["tz"])
                        ew("dve", lambda e, c=c: e.tensor_tensor(out=Y[:, c, 0:NPB + NS], in0=tz[:, 0:NPB + NS],
                                                                  in1=usb[:, 0:NPB + NS], op=ALU.mult),
                           ["tz", "usb"], ["Y"])
                return fn

            for q in range(4):
                add_stage(*([(0, KC, 0, 256, wview(w_in, 0, KC, OFF_C + q * 256, 256)),
                                (0, KC, 256, 256, wview(w_in, 0, KC, OFF_XIN + q * 256, 256))], st_alpha(q)))
                add_stage(*([(0, KC, 0, 256, wview(w_in, 0, KC, OFF_B + q * 256, 256)),
                                (0, KC, 256, 256, wview(w_in, 0, KC, OFF_U + q * 256, 256))], st_beta(q)))

            def st_gamma(jj):
                def fn(wb, wkey):
                    for s4 in range(4):
                        i = lin_chunk_b(X, XKEYS, wb, wkey, KC, slice(s4 * P, (s4 + 1) * P))
                        ew("act", lambda e, i=i, s4=s4: e.activation(out=G[:, s4, 0:NPB], in_=accP[i][:, :],
                                                                      func=AF.Sigmoid), acc_reads(i), [("G", s4)])
                        ew("act", lambda e, i=i, s4=s4: e.activation(out=G[:, s4, NPB:T], in_=aS(i)[:, 0:NSB],
                                                                      func=AF.Sigmoid), acc_reads(i), [("G", s4)])
                return fn

            def st_delta(jj):
                def fn(wb, wkey):
                    for cc in range(2):
                        j = 2 * jj + cc
                        i = next_acc()
                        tp = [(wb[:, k, cc * P:(cc + 1) * P], Y[:, k, 0:NPB]) for k in range(8)]
                        ts = [(wb[:, k, cc * P:(cc + 1) * P], Y[:, k, NPB:T]) for k in range(8)]
                        mm_group(i, tp, ts, reads=["Y"] + wk(wkey))
                        ew("dve", lambda e, i=i, cc=cc: e.tensor_tensor(out=G[:, cc, 0:NPB], in0=accP[i][:, :],
                                                                         in1=G[:, cc, 0:NPB], op=ALU.mult),
                           acc_reads(i) + [("G", cc)], [("G", cc)])
                        ew("dve", lambda e, i=i, cc=cc: e.tensor_tensor(out=G[:, cc, NPB:T], in0=aS(i)[:, 0:NSB],
                                                                         in1=G[:, cc, NPB:T], op=ALU.mult),
                           acc_reads(i) + [("G", cc)], [("G", cc)])
                        i = next_acc()
                        tp = [(wb[:, k, 256 + cc * P:256 + (cc + 1) * P], Y[:, 8 + k, 0:NPB]) for k in range(8)]
                        ts = [(wb[:, k, 256 + cc * P:256 + (cc + 1) * P], Y[:, 8 + k, NPB:T]) for k in range(8)]
                        mm_group(i, tp, ts, reads=["Y"] + wk(wkey))
                        ew("dve", lambda e, i=i, cc=cc: e.tensor_tensor(out=G[:, 2 + cc, 0:NPB], in0=accP[i][:, :],
                                                                         in1=G[:, 2 + cc, 0:NPB], op=ALU.mult),
                           acc_reads(i) + [("G", 2 + cc)], [("G", 2 + cc)])
                        ew("dve", lambda e, i=i, cc=cc: e.tensor_tensor(out=G[:, 2 + cc, NPB:T],
                                                                         in0=aS(i)[:, 0:NSB],
                                                                         in1=G[:, 2 + cc, NPB:T], op=ALU.mult),
                           acc_reads(i) + [("G", 2 + cc)], [("G", 2 + cc)])
                        ew("dve", lambda e, j=j, cc=cc: e.tensor_tensor(out=M[:, j, :], in0=G[:, cc, :],
                                                                          in1=G[:, 2 + cc, :], op=ALU.add),
                           [("G", cc), ("G", 2 + cc)], ["M"])
                return fn

            for jj in range(8):
                add_stage(*([(0, KC, 0, 256, wview(w_in, 0, KC, OFF_GA + jj * 256, 256)),
                                (0, KC, 256, 256, wview(w_in, 0, KC, OFF_GB + jj * 256, 256))], st_gamma(jj)))
                add_stage(*([(0, 8, 0, 256, wview(w_a, 0, 8, jj * 256, 256)),
                                (0, 8, 256, 256, wview(w_b, 0, 8, jj * 256, 256))], st_delta(jj)))

            def st_resid(src, srckey, stats=False):
                def mk(s):
                    def fn(wb, wkey):
                        for cc in range(4):
                            if srckey == "Y" and s == 0 and cc == 0:
                                i = next_acc()
                                colsl = slice(0, P)
                                for k in range(KC):
                                    def fk(e, k=k, i=i, colsl=colsl):
                                        e.matmul(accP[i][:, :], lhsT=wb[:, k, colsl], rhs=src[:, k, 0:NPB],
                                                 start=(k == 0), stop=(k == KC - 1))
                                        return e.matmul(aS(i)[:, 0:NSB], lhsT=wb[:, k, colsl], rhs=src[:, k, NPB:T],
                                                        start=(k == 0), stop=(k == KC - 1))
                                    S.op("pe", fk, reads=[("Ya", k)] + wk(wkey, colsl), writes=acc_reads(i))
                            else:
                                i = lin_chunk_b(src, [srckey], wb, wkey, KC, slice(cc * P, (cc + 1) * P))
                            stats_flush()
                            resid_add(i, 4 * s + cc)
                            if stats:
                                stats_push(4 * s + cc)
                    return fn
                return mk

            mkfn = st_resid(M, "M", stats=True)
            for s in range(4):
                add_stage(*([(0, KC, 0, 512, wview(w_mix, 0, KC, s * 512, 512))], mkfn(s)))
            run_stages()
            if DEBUG_DUMP == "h1" and g == 0:
                dump_R()

            rmsnorm_fm(1, X, "X", fused=True)

            def st_q(s):
                def fn(wb, wkey):
                    for cc in range(4):
                        i = lin_chunk_b(X, XKEYS, wb, wkey, KC, slice(cc * P, (cc + 1) * P))
                        evac_copy(i, Y, 4 * s + cc, "Y")
                return fn
            for s in range(4):
                add_stage(*([(0, KC, 0, 512, wview(w_q, 0, KC, s * 512, 512))], st_q(s)),
                          *(["noahead"] if s == 3 else []))

            run_stages()

            S.alias(ATT_KEYS, PHA_KEYS + VR_KEYS)
            S.alias([("kbX", 0), ("kbX", 1)], XKEYS)
            Xflat = X[:, :, :].rearrange("p k t -> p (k t)")
            kvr = [Xflat[:, r * 4096:(r + 1) * 4096].rearrange("p (m d) -> p m d", m=2) for r in range(2)]
            S.dma("pool", lambda e: e.dma_start(out=kvr[0], in_=ck[8 * g, :, :].rearrange("(m p) d -> p m d", p=P)),
                  writes=[("kbX", 0)])
            S.dma("pool", lambda e: e.dma_start(out=kvr[1],
                                                in_=ck[8 * g + 1, :, :].rearrange("(m p) d -> p m d", p=P)),
                  writes=[("kbX", 1)])
            ABK = [(accP[j], [("accP", j)]) for j in range(4)] + [(accS2[j], [("accS", j)]) for j in range(2)]
            pTd = [pT, pT2]
            rinvd = [(scrA[:, 2, 0:NPB], ("G", 2)), (scrA[:, 3, 0:NPB], ("G", 3))]

            def nbk():
                st["abk"] = st.get("abk", 0) + 1
                return ABK[st["abk"] % len(ABK)]

            def att_scores(h):
                par = h % 2
                for mt in range(2):
                    bank, bkeys = nbk()

                    def f(e, bank=bank, mt=mt, h=h):
                        ins = None
                        for c in range(4):
                            ins = e.matmul(bank[:, :], lhsT=KT[:, 4 * h + c, mt * P:(mt + 1) * P],
                                           rhs=Y[:, 4 * h + c, 0:NPB], start=(c == 0), stop=(c == 3))
                        return ins
                    S.op("pe", f, reads=["KT", "Y"], writes=bkeys)
                    ew("act", lambda e, bank=bank, mt=mt, par=par: e.activation(
                        out=pTd[par][:, mt, :], in_=bank[:, :], func=AF.Exp, scale=SCALE), bkeys, [("pT", par)])

            def att_rest(h):
                par = h % 2
                pTh = pTd[par]
                rv, rkey = rinvd[par]
                bank, bkeys = nbk()

                def fden(e, bank=bank, pTh=pTh):
                    e.matmul(bank[:, :], lhsT=ones_b[:, :], rhs=pTh[:, 0, :], start=True, stop=False)
                    return e.matmul(bank[:, :], lhsT=ones_b[:, :], rhs=pTh[:, 1, :], start=False, stop=True)
                S.op("pe", fden, reads=[("pT", par), "ones_b"], writes=bkeys)
                ew("dve", lambda e, bank=bank, rv=rv: e.reciprocal(out=rv, in_=bank[:, :]), bkeys, [rkey])
                for c in range(4):
                    bank, bkeys = nbk()

                    def fo(e, bank=bank, c=c, h=h, pTh=pTh):
                        e.matmul(bank[:, :], lhsT=Vbf[:, 0, (4 * h + c) * P:(4 * h + c + 1) * P], rhs=pTh[:, 0, :],
                                 start=True, stop=False)
                        return e.matmul(bank[:, :], lhsT=Vbf[:, 1, (4 * h + c) * P:(4 * h + c + 1) * P],
                                        rhs=pTh[:, 1, :], start=False, stop=True)
                    S.op("pe", fo, reads=[("pT", par), "Vbf"], writes=bkeys)
                    ew("dve", lambda e, bank=bank, c=c, h=h, rv=rv: e.tensor_tensor(
                        out=M[:, 4 * h + c, 0:NPB], in0=bank[:, :], in1=rv, op=ALU.mult), bkeys + [rkey], ["M"])

            att_scores(0)
            for h in range(4):
                if h + 1 < 4:
                    att_scores(h + 1)
                att_rest(h)

            Sall = misc[0][:, :].rearrange("p (m h t) -> p m h t", m=2, h=4)
            trb = [(miscb[:, :], "miscb"), (accP[3][:, :].bitcast(BF16), ("accP", 3))]
            Oall = [accP[0][:, :].rearrange("p (j t) -> p j t", j=8), accP[1][:, :].rearrange("p (j t) -> p j t", j=8)]
            OKEYS = [("accP", 0), ("accP", 1)]

            def ld_kv(src, seq, r):
                S.dma("pool", lambda e, seq=seq, r=r: e.dma_start(
                    out=kvr[r], in_=src[seq, :, :].rearrange("(m p) d -> p m d", p=P)), writes=[("kbX", r)])

            bfree = (mode["si"] + NBUF - 1) % NBUF
            wfl = wbuf[:, bfree, :, :].rearrange("p k c -> p (k c)")
            vvr = [wfl[:, r * 4096:(r + 1) * 4096].rearrange("p (m d) -> p m d", m=2) for r in range(2)]
            VKEYS = [("vring", 0), ("vring", 1)]
            S.alias(VKEYS, [("wbuf", bfree, 0), ("wbuf", bfree, 1)])

            def ld_v(seq, r):
                S.dma("pool", lambda e, seq=seq, r=r: e.dma_start(
                    out=vvr[r], in_=cv[seq, :, :].rearrange("(m p) d -> p m d", p=P)), writes=[VKEYS[r]])

            items = [(b, h) for b in range(8) for h in range(4)]
            SB2 = [(misc[0], ("misc", 0)), (accP[2], ("accP", 2))]
            OB2 = [(accP[0], ("accP", 0)), (accP[1], ("accP", 1))]

            def Sreg(b):
                return SB2[b % 2][0][:, 0:64].rearrange("p (m h t) -> p m h t", m=2, h=4)

            def emit_T(n):
                b, h = items[n]
                r = b % 2
                tb, tkey = trb[n % 2]

                def ftr(e, r=r, h=h, tb=tb):
                    ins = None
                    for c in range(4):
                        for mt in range(2):
                            ins = e.transpose(out=tb[:, c * 256 + mt * P:c * 256 + (mt + 1) * P],
                                              in_=kvr[r][:, mt, (4 * h + c) * P:(4 * h + c + 1) * P],
                                              identity=ident_b[:, :])
                    return ins
                S.op("pe", ftr, reads=[("kbX", r), "ident_b"], writes=[tkey])
                tr = n % 2
                eng = ev_eng()
                if eng == "act":
                    ew("act", lambda e, tr=tr, tb=tb: e.copy(out=kbT[:, tr, :, :].rearrange("p c m -> p (c m)"),
                                                             in_=tb), [tkey], [("kbT", tr)])
                else:
                    ew("dve", lambda e, tr=tr, tb=tb: e.tensor_copy(
                        out=kbT[:, tr, :, :].rearrange("p c m -> p (c m)"), in_=tb), [tkey], [("kbT", tr)])

            def emit_S(n):
                b, h = items[n]
                tr = n % 2
                sreg = Sreg(b)

                def fsc(e, tr=tr, b=b, h=h, sreg=sreg):
                    ins = None
                    for mt in range(2):
                        for c in range(4):
                            ins = e.matmul(sreg[:, mt, h, :], lhsT=kbT[:, tr, c, mt * P:(mt + 1) * P],
                                           rhs=Y[:, 4 * h + c, NPB + 8 * b:NPB + 8 * b + 8], start=(c == 0),
                                           stop=(c == 3))
                    return ins
                S.op("pe", fsc, reads=[("kbT", tr), "Y"], writes=[SB2[b % 2][1]])

            def seq_exp(b):
                skey = SB2[b % 2][1]
                ew("act", lambda e, b=b: e.activation(out=pTs[:, :, :, 8 * b:8 * b + 8], in_=Sreg(b), func=AF.Exp,
                                                      scale=SCALE), [skey], [("pTs", b)])

            def seq_rest(b):
                r = b % 2
                sbank, skey = SB2[b % 2]
                obank, okey = OB2[b % 2]
                den = sbank[:, 64:96]
                oreg = obank[:, 0:128].rearrange("p (j t) -> p j t", j=16)

                def fden(e, b=b, den=den):
                    e.matmul(den, lhsT=ones_b[:, :], rhs=pTs[:, 0, :, 8 * b:8 * b + 8], start=True, stop=False)
                    return e.matmul(den, lhsT=ones_b[:, :], rhs=pTs[:, 1, :, 8 * b:8 * b + 8], start=False, stop=True)
                S.op("pe", fden, reads=[("pTs", b), "ones_b"], writes=[skey])
                ew("dve", lambda e, b=b, den=den: e.reciprocal(out=rinvs[:, :, 8 * b:8 * b + 8],
                                                               in_=den.rearrange("p (h t) -> p h t", h=4)),
                   [skey], [("rinvs", b)])

                def fpv(e, r=r, b=b, oreg=oreg):
                    ins = None
                    for j in range(16):
                        for mt in range(2):
                            ins = e.matmul(oreg[:, j, :], lhsT=vvr[r][:, mt, j * P:(j + 1) * P],
                                           rhs=pTs[:, mt, j // 4, 8 * b:8 * b + 8], start=(mt == 0), stop=(mt == 1))
                    return ins
                S.op("pe", fpv, reads=[VKEYS[r], ("pTs", b)], writes=[okey])
                if b + 2 < 8:
                    ld_v(8 * g + b + 2, r)
                ew("dve", lambda e, b=b, oreg=oreg: e.tensor_tensor(
                    out=M[:, :, NPB + 8 * b:NPB + 8 * b + 8].rearrange("p (h c) t -> p h c t", h=4),
                    in0=oreg.rearrange("p (h c) t -> p h c t", h=4),
                    in1=rinvs[:, :, 8 * b:8 * b + 8].unsqueeze(2).to_broadcast([P, 4, 4, 8]), op=ALU.mult),
                   [okey, ("rinvs", b)], ["M"])

            emit_T(0)
            ld_v(8 * g + 0, 0)
            ld_v(8 * g + 1, 1)
            pend_seq = []
            for n in range(len(items)):
                b0, h0 = items[n]
                if n + 1 < len(items):
                    b1, h1 = items[n + 1]
                    emit_T(n + 1)
                    if h1 == 3 and b1 + 2 < 8:
                        ld_kv(ck, 8 * g + b1 + 2, b1 % 2)
                emit_S(n)
                if h0 == 0 and pend_seq:
                    seq_rest(pend_seq.pop(0))
                if h0 == 3:
                    seq_exp(b0)
                    pend_seq.append(b0)
            while pend_seq:
                seq_rest(pend_seq.pop(0))

            S.alias(XKEYS, [("kbX", 0), ("kbX", 1)])
            S.alias([("wbuf", bfree, 0), ("wbuf", bfree, 1)], VKEYS)


            mkfn = st_resid(M, "M", stats=True)
            for s in range(4):
                add_stage(*([(0, KC, 0, 512, wview(w_xo, 0, KC, s * 512, 512))], mkfn(s)))
            run_stages()
            if DEBUG_DUMP == "h2" and g == 0:
                dump_R()

            rmsnorm_fm(2, X, "X", fused=True)

            def st_up(fg, s):
                def fn(wb, wkey):
                    for cc in range(4):
                        j = 4 * s + cc
                        i = lin_chunk_b(X, XKEYS, wb, wkey, KC, slice(cc * P, (cc + 1) * P))
                        ew("act", lambda e, i=i: e.activation(out=usb[:, 0:NPB], in_=accP[i][:, :], func=AF.Square),
                           acc_reads(i), ["usb"])
                        ew("act", lambda e, i=i: e.activation(out=usb[:, NPB:T], in_=aS(i)[:, 0:NSB],
                                                               func=AF.Square), acc_reads(i), ["usb"])
                        ew("dve", lambda e, i=i, j=j: e.scalar_tensor_tensor(out=Y[:, j, 0:NPB], in0=accP[i][:, :],
                                                                              scalar=0.0, in1=usb[:, 0:NPB],
                                                                              op0=ALU.is_gt, op1=ALU.mult),
                           acc_reads(i) + ["usb"], ["Y", ("Ya", j)])
                        ew("dve", lambda e, i=i, j=j: e.scalar_tensor_tensor(out=Y[:, j, NPB:T],
                                                                              in0=aS(i)[:, 0:NSB], scalar=0.0,
                                                                              in1=usb[:, NPB:T], op0=ALU.is_gt,
                                                                              op1=ALU.mult),
                           acc_reads(i) + ["usb"], ["Y", ("Ya", j)])
                return fn

            for fg in range(4):
                mkfn = st_resid(Y, "Y", stats=(fg == 3))
                for s in range(4):
                    add_stage(*([(0, KC, 0, 512, wview(w_up, 0, KC, fg * 2048 + s * 512, 512))], st_up(fg, s)))
                for s in range(4):
                    add_stage(*([(0, KC, 0, 512, wview(w_down, fg * 2048, KC, s * 512, 512))], mkfn(s)))
            run_stages()
            if DEBUG_DUMP == "h3" and g == 0:
                dump_R()

            rms_rbc()
            for k in range(KC):
                ew("dve", lambda e, k=k: e.scalar_tensor_tensor(out=R[:, k, :], in0=R[:, k, :],
                                                                 scalar=gn[:, 3, k:k + 1], in1=rbc[:, :],
                                                                 op0=ALU.mult, op1=ALU.mult),
                   ["R", "usb", "gn"], ["R", ("Rf", k)])
            S.alias(STG_KEYS[2:], ATT_KEYS + PHA_KEYS + VR_KEYS)
            otiles = [(t * P, P) for t in range(4)] + [(NPB, NS)]
            hn2 = 0
            for (r0, rows) in otiles:
                for h in range(2):
                    s = hn2 % len(STG)
                    hn2 += 1
                    for r4 in range(0, 8, 4):
                        tb, tkey = next_tr()

                        def f(e, tb=tb, r0=r0, rows=rows, h=h, r4=r4):
                            ins = None
                            for kk in range(4):
                                ins = e.transpose(out=tb[0:rows, kk * P:(kk + 1) * P],
                                                  in_=R[:, h * 8 + r4 + kk, r0:r0 + rows], identity=ident_f[:, :])
                            return ins
                        S.op("pe", f, reads=[("Rf", h * 8 + r4 + kk) for kk in range(4)] + ["ident_f"], writes=[tkey])
                        eng = ev_eng()
                        o_ = STG[s][0:rows, r4 * P:(r4 + 4) * P]
                        i_ = tb[0:rows, :]
                        if eng == "act":
                            ew("act", lambda e, o_=o_, i_=i_: e.copy(out=o_, in_=i_), [tkey], [STG_KEYS[s]])
                        else:
                            ew("dve", lambda e, o_=o_, i_=i_: e.tensor_copy(out=o_, in_=i_), [tkey],
                               [STG_KEYS[s]])
                    S.dma("sp", lambda e, r0=r0, rows=rows, h=h, s=s: e.dma_start(
                        out=y_o[g, r0:r0 + rows, h * 1024:(h + 1) * 1024], in_=STG[s][0:rows, :]),
                        reads=[STG_KEYS[s]])

        def dump_R():
            for k in range(KC):
                S.dma("sp", lambda e, k=k: e.dma_start(out=dbg_o[k, :, :], in_=R[:, k, :]), reads=["R"])

        S.dry = True
        for g in range(NGROUPS):
            group(g)
        S.dry = False
        st.clear()
        st.update({"acc": 0, "misc": 0, "ev": 0})
        for g in range(NGROUPS):
            group(g)

        for hh in range(2):
            def fcs(e, hh=hh):
                ins = None
                for c4 in range(4):
                    c = hh * 4 + c4
                    ins = e.transpose(out=misc[0][0:32, c4 * P:(c4 + 1) * P],
                                      in_=pss[:, c, :, :].rearrange("p b k -> p (b k)"), identity=ident_f[:, :])
                return ins
            S.op("pe", fcs, reads=["pss", "ident_f"], writes=[("misc", 0)])
            ew("dve", lambda e, hh=hh: e.tensor_copy(out=cst[:, hh * 512:(hh + 1) * 512], in_=misc[0][0:32, :]),
               [("misc", 0)], [("xt", 1)])
        S.dma("sp", lambda e: e.dma_start(out=convs_o[:, :], in_=cst), reads=[("xt", 1)])
        for hh in range(2):
            def fcp(e, hh=hh):
                ins = None
                for c4 in range(4):
                    c = hh * 4 + c4
                    ins = e.transpose(out=misc[0][0:2, c4 * P:(c4 + 1) * P], in_=pl[:, c, :], identity=ident_f[:, :])
                return ins
            S.op("pe", fcp, reads=["pl", "ident_f"], writes=[("misc", 0)])
            ew("dve", lambda e, hh=hh: e.tensor_copy(out=kst[0:2, hh, :], in_=misc[0][0:2, :]), [("misc", 0)],
               [KSTKEY[hh]])
        S.dma("sp", lambda e: e.dma_start(out=convp_o[:, :].rearrange("r (h n) -> r h n", h=2), in_=kst[0:2, :, :]),
              reads=list(KSTKEY))

        final_waits = {}
        for slot, uses in S.dma_uses.items():
            final_waits[slot] = 16 * uses

        def emit(engname, e):
            for (waits, fn, semname, inc) in S.streams[engname]:
                for (sn, v) in waits:
                    e.wait_ge(sems[sn], v)
                ins = fn(e)
                ins.then_inc(sems[semname], inc)

        @block.tensor
        def _(e):
            emit("pe", e)

        @block.scalar
        def _(e):
            emit("act", e)

        @block.vector
        def _(e):
            emit("dve", e)

        @block.gpsimd
        def _(e):
            emit("pool", e)
            for slot, v in final_waits.items():
                if slot.startswith("pool"):
                    e.wait_ge(sems[slot], v)

        @block.sync
        def _(e):
            emit("sp", e)
            for slot, v in final_waits.items():
                if slot.startswith("sp"):
                    e.wait_ge(sems[slot], v)
    return nc


_CACHE = {}


def _program():
    if "nc" not in _CACHE:
        _CACHE["nc"] = build_program()
    return _CACHE["nc"]


def _make_in_maps(x_prompt, x_sample, state_conv, cache_mem_k, cache_mem_v, mem_prompt,
           norm_mix_g, w_in, ln_v_g, ln_v_b, w_spatial, b_spatial, conv_w,
           w_branch_a, w_branch_b, w_mix_out, norm_x_g, norm_mem_g, w_q, w_k, w_v,
           w_x_out, norm_mlp_g, w_up, w_down, norm_final_g):
    f = np.float32
    A = lambda a: np.ascontiguousarray(np.asarray(a, dtype=f))
    x_prompt, x_sample = A(x_prompt), A(x_sample)
    state_conv, cache_mem_k, cache_mem_v, mem_prompt = A(state_conv), A(cache_mem_k), A(cache_mem_v), A(mem_prompt)

    def fm(gv):
        return np.asarray(gv, dtype=f).reshape(KC, P).T

    gains = np.ascontiguousarray(np.stack([fm(norm_mix_g[0]), fm(norm_x_g[0]), fm(norm_mlp_g[0]),
                                           fm(norm_final_g), fm(norm_mem_g[0])], axis=1))
    cwt = np.ascontiguousarray(np.asarray(conv_w[0], dtype=f).reshape(3, 8, P).transpose(2, 1, 0))
    lnv = np.ascontiguousarray(np.stack([np.asarray(ln_v_g[0], dtype=f), np.asarray(ln_v_b[0], dtype=f)]))
    ws = np.asarray(w_spatial[0], dtype=f)
    wsp = np.ascontiguousarray(ws.transpose(1, 0, 2))
    wsb = np.zeros((NS, 4, NS), dtype=f)
    for b in range(8):
        wsb[8 * b:8 * b + 8, :, 8 * b:8 * b + 8] = ws[:, :8, :8].transpose(1, 0, 2)
    bsp = np.ascontiguousarray(np.asarray(b_spatial[0], dtype=f).reshape(1, 4 * P))
    ident = np.eye(P, dtype=f)
    tril = np.tril(np.ones((P, P), dtype=f))
    shared = dict(w_in=A(w_in[0]), w_a=A(w_branch_a[0]), w_b=A(w_branch_b[0]), w_mix=A(w_mix_out[0]),
                  w_q=A(w_q[0]), w_k=A(w_k[0]), w_v=A(w_v[0]), w_xo=A(w_x_out[0]), w_up=A(w_up[0]),
                  w_down=A(w_down[0]), gains=gains, cwt=cwt, lnv=lnv, wsp=wsp, wsb=wsb, bsp=bsp, ident=ident,
                  tril=tril)
    in_maps = []
    for c in range(NCORES):
        b, half = c // 2, c % 2
        xg = np.zeros((NGROUPS, T, D), dtype=f)
        for g in range(NGROUPS):
            p0 = half * 1024 + g * 512
            xg[g, 0:NPB] = x_prompt[b, p0:p0 + NPB]
            xg[g, NPB:NPB + NS] = x_sample[16 * c + 8 * g:16 * c + 8 * g + 8].reshape(NS, D)
            if p0 >= 2:
                xg[g, NPB + NS:T] = x_prompt[b, p0 - 2:p0]
        m = dict(shared)
        m["xg"] = xg
        m["sconv"] = np.ascontiguousarray(state_conv[0, 16 * c:16 * c + 16].reshape(32, AW))
        m["ck"] = np.ascontiguousarray(cache_mem_k[0, 16 * c:16 * c + 16].reshape(16, 256, D))
        m["cv"] = np.ascontiguousarray(cache_mem_v[0, 16 * c:16 * c + 16].reshape(16, 256, D))
        m["mem"] = np.ascontiguousarray(np.concatenate(
            [mem_prompt[b, half * P:(half + 1) * P], mem_prompt[b, (1 - half) * P:(2 - half) * P]], axis=0))
        in_maps.append(m)

    return in_maps


def _assemble(outs):
    f = np.float32

    y_prompt = np.zeros((4, 2048, D), dtype=f)
    y_sample = np.zeros((128, 8, D), dtype=f)
    mem_k = np.zeros((1, 4, 256, 4, 512), dtype=f)
    mem_v = np.zeros((1, 4, 256, 4, 512), dtype=f)
    conv_p = np.zeros((1, 4, 2, AW), dtype=f)
    conv_s = np.zeros((1, 128, 2, AW), dtype=f)
    chunk_v = np.zeros((1, 128, 8, AW), dtype=f)
    for c in range(NCORES):
        b, half = c // 2, c % 2
        o = outs[c]
        for g in range(NGROUPS):
            p0 = half * 1024 + g * 512
            y_prompt[b, p0:p0 + NPB] = o["y"][g, 0:NPB]
            y_sample[16 * c + 8 * g:16 * c + 8 * g + 8] = o["y"][g, NPB:NPB + NS].reshape(8, 8, D)
            chunk_v[0, 16 * c + 8 * g:16 * c + 8 * g + 8] = o["chunkv"][g].reshape(8, 8, AW)
        mem_k[0, b, half * P:(half + 1) * P] = o["mk"].reshape(P, 4, 512)
        mem_v[0, b, half * P:(half + 1) * P] = o["mv"].reshape(P, 4, 512)
        if half == 1:
            conv_p[0, b] = o["convp"]
        conv_s[0, 16 * c:16 * c + 16] = o["convs"].reshape(16, 2, AW)
    return (y_prompt, y_sample, mem_k, mem_v, conv_p, conv_s, chunk_v)


def kernel(**inputs):
    in_maps = _make_in_maps(**inputs)
    nc = _program()
    res = run_bass_kernel_spmd(nc, in_maps, core_ids=list(range(NCORES)))
    return _assemble(res.results)
```

```python
import numpy as np
from contextlib import ExitStack
import concourse.bass as bass
import concourse.mybir as mybir
from concourse.bass_utils import run_bass_kernel_spmd

F32 = mybir.dt.float32
BF16 = mybir.dt.bfloat16
AF = mybir.ActivationFunctionType
ALU = mybir.AluOpType

NCORES = 8
P = 128
D = 2048
KC = 16
NPB = 512
NS = 64
NSB = 66
T = NPB + NSB
NGROUPS = 2
AW = 1024
DFF = 8192
EPS = 1e-6
INW = 9216
OFF_U, OFF_V, OFF_B, OFF_C, OFF_XIN, OFF_GA, OFF_GB = 0, 1024, 2048, 3072, 4096, 5120, 7168
NBUF = 3
SCALE = 512 ** -0.5

DEBUG_DUMP = None


class Sched:
    def __init__(self, engines, ndma_slots):
        self.engines = list(engines)
        self.streams = {e: [] for e in engines}
        self.count = {e: 0 for e in engines}
        self.seen = {e: {} for e in engines}
        self.lastw = {}
        self.readers = {}
        self.dma_n = {e: 0 for e in ndma_slots}
        self.dma_slots = dict(ndma_slots)
        self.dma_uses = {}

    def _deps(self, reads, writes):
        deps = []
        for k in reads:
            t = self.lastw.get(k)
            if t is not None:
                deps.append(t)
        for k in writes:
            t = self.lastw.get(k)
            if t is not None:
                deps.append(t)
            deps.extend(self.readers.get(k, ()))
        return deps

    def _waits(self, eng, deps):
        need = {}
        for (s, v) in deps:
            if self.seen[eng].get(s, 0) < v and need.get(s, 0) < v:
                need[s] = v
        for s, v in need.items():
            self.seen[eng][s] = v
        return list(need.items())

    def _record(self, tok, reads, writes):
        for k in reads:
            self.readers.setdefault(k, []).append(tok)
        for k in writes:
            self.lastw[k] = tok
            self.readers[k] = []

    @staticmethod
    def _excl(reads, writes):
        r2, w2 = [], list(writes)
        for k in reads:
            if isinstance(k, tuple) and k[0] in ("accP", "accS", "misc") or k == "miscb":
                w2.append(k)
            else:
                r2.append(k)
        return r2, w2

    dry = False

    def op(self, eng, fn, reads=(), writes=()):
        if self.dry:
            return None
        reads, writes = self._excl(reads, writes)
        waits = self._waits(eng, self._deps(reads, writes))
        self.count[eng] += 1
        tok = (eng, self.count[eng])
        self.streams[eng].append((waits, fn, eng, 1))
        self._record(tok, reads, writes)
        return tok

    def dma(self, eng, fn, reads=(), writes=()):
        if self.dry:
            return None
        n = self.dma_n[eng]
        self.dma_n[eng] = n + 1
        slot = "%s_d%d" % (eng, n % self.dma_slots[eng])
        uses = self.dma_uses.get(slot, 0)
        deps = self._deps(reads, writes)
        if uses > 0:
            deps.append((slot, 16 * uses))
        waits = self._waits(eng, deps)
        self.dma_uses[slot] = uses + 1
        tok = (slot, 16 * (uses + 1))
        self.streams[eng].append((waits, fn, slot, 16))
        self._record(tok, reads, writes)
        return tok

    def alias(self, new_keys, old_keys):
        if self.dry:
            return
        toks = []
        for k in old_keys:
            t = self.lastw.get(k)
            if t is not None:
                toks.append(t)
            toks.extend(self.readers.get(k, ()))
        for k in new_keys:
            self.readers.setdefault(k, []).extend(toks)

    def sem_names(self):
        names = list(self.engines)
        for e, n in self.dma_slots.items():
            names += ["%s_d%d" % (e, i) for i in range(n)]
        return names


def build_program():
    nc = bass.Bass("TRN2", target_bir_lowering=False)

    def din(name, shape):
        return nc.dram_tensor(name, list(shape), F32, kind="ExternalInput").ap()

    def dout(name, shape):
        return nc.dram_tensor(name, list(shape), F32, kind="ExternalOutput").ap()

    xg = din("xg", [NGROUPS, T, D])
    sconv = din("sconv", [32, AW])
    ck = din("ck", [16, 256, D])
    cv = din("cv", [16, 256, D])
    mem = din("mem", [256, D])
    w_in = din("w_in", [D, INW])
    w_a = din("w_a", [AW, D])
    w_b = din("w_b", [AW, D])
    w_mix = din("w_mix", [D, D])
    w_q = din("w_q", [D, D])
    w_k = din("w_k", [D, D])
    w_v = din("w_v", [D, D])
    w_xo = din("w_xo", [D, D])
    w_up = din("w_up", [D, DFF])
    w_down = din("w_down", [DFF, D])
    gains = din("gains", [P, 5, KC])
    cwt = din("cwt", [P, 8, 3])
    lnv = din("lnv", [2, AW])
    wsp = din("wsp", [P, 4, P])
    wsb = din("wsb", [NS, 4, NS])
    bsp = din("bsp", [1, 4 * P])
    ident = din("ident", [P, P])
    tril = din("tril", [P, P])

    y_o = dout("y", [NGROUPS, NPB + NS, D])
    mk_o = dout("mk", [P, D])
    mv_o = dout("mv", [P, D])
    convp_o = dout("convp", [2, AW])
    convs_o = dout("convs", [32, AW])
    chunkv_o = dout("chunkv", [NGROUPS, NS, AW])
    dbg_o = dout("dbg", [KC, P, T]) if DEBUG_DUMP else None

    S = Sched(["pe", "act", "dve", "pool", "sp"], {"sp": 8, "pool": 10})

    es = ExitStack()
    with es:
        def sb(name, shape, dt=F32):
            return es.enter_context(nc.sbuf_tensor(name, list(shape), dt))

        def pst(name, shape, dt=F32):
            return es.enter_context(nc.psum_tensor(name, list(shape), dt))

        R = sb("R", [P, KC, T])
        X = sb("X", [P, KC, T], BF16)
        Y = sb("Y", [P, KC, T], BF16)
        M = sb("M", [P, KC, T], BF16)
        wbuf = sb("wbuf", [P, NBUF, KC, 512], BF16)
        KT = sb("KT", [P, KC, 256], BF16)
        Vbf = sb("Vbf", [P, 2, D], BF16)
        xt = sb("xt", [P, 2, 1024])
        sq = sb("sq", [P, 2, T], BF16)
        stat = sb("stat", [P, 32])
        scrA = sb("scrA", [P, 4, T])
        scrB = sb("scrB", [P, 3072])
        vraw = scrB[:, 0:AW]
        scrC = sb("scrC", [P, 2, T])
        pT = sb("pT", [P, 2, NPB], BF16)
        pT2 = sb("pT2", [P, 2, NPB], BF16)
        pTs = sb("pTs", [P, 2, 4, NS], BF16)
        rinvs = sb("rinvs", [P, 4, NS])
        vbf = M[:, :, :].rearrange("p k t -> p (k t)")[:, 0:5 * AW].rearrange("p (a b) -> p a b", a=5)
        G = scrA
        scrA_flat = scrA[:, :, :].rearrange("p a t -> p (a t)")
        lng_bc = scrA_flat[:, 0:AW]
        lnb_bc = scrA_flat[:, AW:2 * AW]
        rinv = scrA[:, 3, 0:NPB]
        Csb = scrB[:, 0:2 * T].rearrange("p (a t) -> p a t", a=2)
        convb = scrB[:, 2 * T:2 * T + 2 * (NPB + NS)].rearrange("p (a t) -> p a t", a=2)
        pbuf = scrB[:, 2308:2308 + 2 + NPB + 80]
        kb = scrB[:, 0:1024].bitcast(BF16).rearrange("p (a b c) -> p a b c", a=2, b=2)
        vb = scrB[:, 1024:2048].bitcast(BF16).rearrange("p (a b c) -> p a b c", a=2, b=2)
        kbT = scrB[:, 2048:3072].bitcast(BF16).rearrange("p (a b c) -> p a b c", a=2, b=4)
        usb = scrC[:, 0, :]
        rbc = scrC[:, 0, :]
        tz = scrC[:, 1, :]
        kst = scrC[:, :, 0:512]
        KSTKEY = ["usb", "tz"]
        VRAW = [scrB[:, 0:AW], scrB[:, AW:2 * AW]]
        VR_KEYS = ["vraw", ("vraw", 1)]
        STATV = [stat[:, 0:16], stat[:, 16:32]]
        STG = [xt[:, 0, :], xt[:, 1, :], scrB[:, 0:1024], scrB[:, 1024:2048], scrB[:, 2048:3072]]
        STG_KEYS = [("xt", 0), ("xt", 1), ("stg", 2), ("stg", 3), ("stg", 4)]
        XKEYS = [("X", k) for k in range(KC)]
        pend = {"fn": None}
        LNG_KEYS = [("G", 0), ("G", 1)]
        LNB_KEYS = [("G", 1), ("G", 2), ("G", 3)]
        PHA_KEYS = [("Csb", 0), ("Csb", 1), ("convb", 0), ("convb", 1), "pbuf"]
        ATT_KEYS = [("kb", 0), ("kb", 1), ("vb", 0), ("vb", 1), ("kbT", 0), ("kbT", 1), ("kbX", 0)]
        ident_f = sb("ident_f", [P, P])
        ident_b = sb("ident_b", [P, P], BF16)
        ones_b = sb("ones_b", [P, P], BF16)
        epsc = sb("epsc", [P, 1])
        gn = sb("gn", [P, 5, KC])
        cw = sb("cw", [P, 8, 3])
        bsp_bc = sb("bsp_bc", [P, 4, P])
        WsT = sb("WsT", [P, 4, P], BF16)
        WsTs = sb("WsTs", [NS, 4, NS], BF16)
        wst = vraw[:, 0:512].rearrange("p (g t) -> p g t", g=4)
        wsts = vraw[0:NS, 512:768].rearrange("p (g t) -> p g t", g=4)
        trilm = sb("trilm", [P, P])
        sconv_sb = xt[0:32, 0, :]
        sconvT = sb("sconvT", [P, 8, 32])
        pl = sb("pl", [P, 8, 2])
        pss = sb("pss", [P, 8, 16, 2])
        cst = xt[0:32, 1, :]

        accP = [pst("accP%d" % i, [P, NPB]) for i in range(4)]
        accS2 = [pst("accS%d" % i, [P, NPB]) for i in range(2)]
        misc = [pst("misc0", [P, 512])]

        def aS(i):
            return accS2[i % 2]

        miscb = pst("miscb", [P, 1024], BF16)
        stat_pb = misc[0][:, 0:NPB]
        stat_sb = miscb[:, 0:2 * NSB].bitcast(F32)
        STATKEYS = [("misc", 0), "miscb"]

        sem_names = S.sem_names()
        sems = {n: es.enter_context(nc.semaphore("s_" + n)) for n in sem_names}
        block = es.enter_context(nc.Block())

        st = {"acc": 0, "misc": 0, "ev": 0}

        def next_acc():
            i = st["acc"] % 4
            st["acc"] += 1
            return i

        def next_misc():
            return 0

        TRB = [(accP[i], ("accP", i)) for i in range(4)]

        def next_tr():
            st["tr"] = st.get("tr", 0) + 1
            return TRB[st["tr"] % len(TRB)]

        def ev_eng():
            st["ev"] += 1
            return "act" if st["ev"] % 2 else "dve"

        def load_const(dst, src, key, eng="sp"):
            S.dma(eng, lambda e, d=dst, s=src: e.dma_start(out=d, in_=s), writes=[key])

        def mm_group(i, terms_pb, terms_sb, reads, nsb=NSB):
            def fn(e):
                ins = None
                n = len(terms_pb)
                for j in range(n):
                    if terms_pb:
                        l, r = terms_pb[j]
                        ins = e.matmul(accP[i][:, :], lhsT=l, rhs=r, start=(j == 0), stop=(j == n - 1))
                    if terms_sb:
                        l, r = terms_sb[j]
                        ins = e.matmul(aS(i)[:, 0:nsb], lhsT=l, rhs=r, start=(j == 0), stop=(j == n - 1))
                return ins
            S.op("pe", fn, reads=reads, writes=acc_reads(i))

        def lin_chunk(src, srckey, wb, kcn, colsl):
            i = next_acc()
            tp = [(wb[:, k, colsl], src[:, k, 0:NPB]) for k in range(kcn)]
            ts = [(wb[:, k, colsl], src[:, k, NPB:T]) for k in range(kcn)]
            mm_group(i, tp, ts, reads=[srckey, "wbuf"])
            return i

        def acc_reads(i):
            return [("accP", i), ("accS", i % 2)]

        slab_state = {"n": 0}
        stages = []

        def wview(w, r0, nk, c0, ncol):
            return w[r0:r0 + nk * P, c0:c0 + ncol].rearrange("(k p) n -> p k n", p=P)

        all_stages = []
        mode = {"si": 0, "issued": 0, "half0": set()}

        def issue_load(idx, only_half=None):
            parts = all_stages[idx][0]
            b = idx % NBUF
            for pi, (kofs, nk, cofs, ncol, src) in enumerate(parts):
                if ncol == 512:
                    pieces = [(0, 256, 0), (256, 256, 1)]
                else:
                    pieces = [(0, ncol, pi if len(parts) == 2 else cofs // 256)]
                for (c0, nc_, half) in pieces:
                    if only_half is not None and half != only_half:
                        continue
                    if only_half is None and half == 0 and idx in mode["half0"]:
                        continue
                    S.dma("pool",
                          lambda e, b=b, kofs=kofs, nk=nk, cofs=cofs, c0=c0, nc_=nc_, src=src:
                          e.dma_start(out=wbuf[:, b, kofs:kofs + nk, cofs + c0:cofs + c0 + nc_],
                                      in_=src[:, :, c0:c0 + nc_]),
                          writes=[("wbuf", b, half)])

        def wk(wkey, colsl=None):
            if colsl is None:
                return [(wkey[0], wkey[1], 0), (wkey[0], wkey[1], 1)]
            return [(wkey[0], wkey[1], colsl.start // 256)]

        def ensure_loaded(upto):
            while mode["issued"] <= upto and mode["issued"] < len(all_stages):
                issue_load(mode["issued"])
                mode["issued"] += 1

        def add_stage(parts, fn, *flags):
            if S.dry:
                all_stages.append((parts, flags))
                return
            i = mode["si"]
            if i >= 1 and "defer" in all_stages[i - 1][1]:
                ensure_loaded(i)
            else:
                ensure_loaded(i + NBUF - 1)
            fn(wbuf[:, i % NBUF], ("wbuf", i % NBUF))
            ensure_loaded(i + NBUF - 1)
            j = i + NBUF
            flags_i = all_stages[i][1]
            if "defer" not in flags_i and "noahead" not in flags_i and j < len(all_stages) \
                    and j not in mode["half0"]:
                issue_load(j, only_half=0)
                mode["half0"].add(j)
            mode["si"] += 1

        def run_stages():
            pass

        def lin_chunk_b(src, srckey, wb, wkey, kcn, colsl, koff=0):
            i = next_acc()
            if st.get("split") and srckey is XKEYS:
                st["split"] = False
                for k in range(kcn):
                    def fk(e, k=k, i=i):
                        e.matmul(accP[i][:, :], lhsT=wb[:, koff + k, colsl], rhs=src[:, k, 0:NPB], start=(k == 0),
                                 stop=(k == kcn - 1))
                        return e.matmul(aS(i)[:, 0:NSB], lhsT=wb[:, koff + k, colsl], rhs=src[:, k, NPB:T],
                                        start=(k == 0), stop=(k == kcn - 1))
                    S.op("pe", fk, reads=[("X", k)] + wk(wkey, colsl), writes=acc_reads(i))
                return i
            tp = [(wb[:, koff + k, colsl], src[:, k, 0:NPB]) for k in range(kcn)]
            ts = [(wb[:, koff + k, colsl], src[:, k, NPB:T]) for k in range(kcn)]
            mm_group(i, tp, ts, reads=list(srckey) + wk(wkey, colsl))
            return i

        def ew(eng, fn, reads, writes):
            S.op(eng, fn, reads=reads, writes=writes)

        ENG = {}

        load_const(ident_f[:, :], ident[:, :], "ident_f")
        load_const(trilm[:, :], tril[:, :], "trilm")
        load_const(gn[:, :, :], gains[:, :, :], "gn")
        load_const(cw[:, :, :], cwt[:, :, :], "cw")
        load_const(wst, wsp[:, :, :], "vraw")
        load_const(wsts, wsb[:, :, :], "vraw")
        load_const(sconv_sb, sconv[:, :], ("xt", 0))
        S.dma("sp", lambda e: e.dma_start(out=bsp_bc[:, :, :].rearrange("p g t -> p (g t)").unsqueeze(1),
                                          in_=bsp[0:1, :].partition_broadcast(P)),
              writes=["bsp_bc"])
        ew("dve", lambda e: e.tensor_copy(out=ident_b[:, :], in_=ident_f[:, :]), ["ident_f"], ["ident_b"])
        ew("dve", lambda e: e.memset(ones_b[:, :], 1.0), [], ["ones_b"])
        ew("dve", lambda e: e.memset(epsc[:, :], EPS), [], ["epsc"])
        ew("dve", lambda e: e.memset(Y[:, :, :], 0.0), [], ["Y"])
        ew("dve", lambda e: e.tensor_tensor(out=wst[:, :, :], in0=wst[:, :, :],
                                            in1=trilm[:, None, :].to_broadcast([P, 4, P]), op=ALU.mult),
           ["vraw", "trilm"], ["vraw"])
        ew("dve", lambda e: e.tensor_tensor(out=wsts[:, :, :], in0=wsts[:, :, :],
                                            in1=trilm[0:NS, None, 0:NS].to_broadcast([NS, 4, NS]), op=ALU.mult),
           ["vraw", "trilm"], ["vraw"])
        mi = next_misc()

        def f_wst(e, mi=mi):
            ins = None
            for g in range(4):
                ins = e.transpose(out=misc[mi][:, g * P:(g + 1) * P], in_=wst[:, g, :], identity=ident_f[:, :])
            return ins
        S.op("pe", f_wst, reads=["vraw", "ident_f"], writes=[("misc", mi)])
        ew("dve", lambda e, mi=mi: e.tensor_copy(out=WsT[:, :, :].rearrange("p g t -> p (g t)"), in_=misc[mi][:, :]),
           [("misc", mi)], ["WsT"])
        mi = next_misc()

        def f_wsts(e, mi=mi):
            ins = None
            for g in range(4):
                ins = e.transpose(out=misc[mi][0:NS, g * NS:(g + 1) * NS], in_=wsts[:, g, :],
                                  identity=ident_f[0:NS, 0:NS])
            return ins
        S.op("pe", f_wsts, reads=["vraw", "ident_f"], writes=[("misc", mi)])
        ew("dve", lambda e, mi=mi: e.tensor_copy(out=WsTs[:, :, :].rearrange("p g t -> p (g t)"),
                                                  in_=misc[mi][0:NS, 0:4 * NS]),
           [("misc", mi)], ["WsTs"])
        mi = next_misc()

        def f_sc(e, mi=mi):
            ins = None
            for c in range(8):
                ins = e.transpose(out=misc[mi][:, c * 32:(c + 1) * 32], in_=sconv_sb[:, c * P:(c + 1) * P],
                                  identity=ident_f[0:32, 0:32])
            return ins
        S.op("pe", f_sc, reads=[("xt", 0), "ident_f"], writes=[("misc", mi)])
        ew("dve", lambda e, mi=mi: e.tensor_copy(out=sconvT[:, :, :].rearrange("p c b -> p (c b)"),
                                                  in_=misc[mi][:, 0:256]),
           [("misc", mi)], ["sconvT"])

        def to_feature_major(src_tile, rows, dst, dstkey, col0, k0, nk, srckey):
            for r0 in range(0, nk, 4):
                tb, tkey = next_tr()

                def f(e, tb=tb, r0=r0):
                    ins = None
                    for kk in range(4):
                        ins = e.transpose(out=tb[:, kk * P:kk * P + rows],
                                          in_=src_tile[0:rows, (r0 + kk) * P:(r0 + kk + 1) * P],
                                          identity=ident_f[0:rows, 0:rows])
                    return ins
                S.op("pe", f, reads=[srckey, "ident_f"], writes=[tkey])
                eng = ev_eng()
                src = tb[:, :].rearrange("p (k t) -> p k t", k=4)[:, :, 0:rows]
                out = dst[:, k0 + r0:k0 + r0 + 4, col0:col0 + rows]
                if eng == "act":
                    ew("act", lambda e, o=out, s=src: e.copy(out=o, in_=s), [tkey], [dstkey])
                else:
                    ew("dve", lambda e, o=out, s=src: e.tensor_copy(out=o, in_=s), [tkey], [dstkey])

        def stats_push(k):
            sl = st.get("sqn", 0) % 2
            st["sqn"] = st.get("sqn", 0) + 1
            ew("act", lambda e, k=k, sl=sl: e.activation(out=sq[:, sl, :], in_=R[:, k, :], func=AF.Square),
               ["R"], [("sq", sl)])

            def f(e, k=k, sl=sl):
                e.matmul(stat_pb, lhsT=ones_b[:, :], rhs=sq[:, sl, 0:NPB], start=(k == 0), stop=(k == KC - 1))
                return e.matmul(stat_sb, lhsT=ones_b[:, :], rhs=sq[:, sl, NPB:T], start=(k == 0), stop=(k == KC - 1))
            pend["fn"] = lambda: S.op("pe", f, reads=[("sq", sl), "ones_b"], writes=STATKEYS)

        def stats_flush():
            if pend["fn"] is not None:
                fn = pend["fn"]
                pend["fn"] = None
                fn()

        def rms_rbc():
            stats_flush()
            ew("act", lambda e: e.activation(out=rbc[:, 0:NPB], in_=stat_pb, func=AF.Sqrt,
                                             scale=1.0 / D, bias=epsc[:, 0:1]), STATKEYS + ["epsc"], ["usb"])
            ew("act", lambda e: e.activation(out=rbc[:, NPB:T], in_=stat_sb, func=AF.Sqrt,
                                             scale=1.0 / D, bias=epsc[:, 0:1]), STATKEYS + ["epsc"], ["usb"])
            ew("dve", lambda e: e.reciprocal(out=rbc[:, :], in_=rbc[:, :]), ["usb"], ["usb"])

        def rmsnorm_fm(gidx, dst, dstkey, fused=False):
            if not fused:
                for k in range(KC):
                    stats_flush()
                    stats_push(k)
            rms_rbc()
            for k in range(KC):
                ew("dve", lambda e, k=k: e.scalar_tensor_tensor(out=dst[:, k, :], in0=R[:, k, :],
                                                                 scalar=gn[:, gidx, k:k + 1], in1=rbc[:, :],
                                                                 op0=ALU.mult, op1=ALU.mult),
                   ["R", "usb", "gn"], [("X", k)])
            st["split"] = True

        def resid_add(i, k):
            ew("dve", lambda e, i=i, k=k: e.tensor_tensor(out=R[:, k, 0:NPB], in0=accP[i][:, :], in1=R[:, k, 0:NPB],
                                                           op=ALU.add),
               acc_reads(i) + ["R"], ["R"])
            ew("dve", lambda e, i=i, k=k: e.tensor_tensor(out=R[:, k, NPB:T], in0=aS(i)[:, 0:NSB],
                                                           in1=R[:, k, NPB:T], op=ALU.add),
               acc_reads(i) + ["R"], ["R"])

        def evac_copy(i, dst3, k, dstkey):
            eng = ev_eng()
            if eng == "act":
                ew("act", lambda e, i=i, k=k: e.copy(out=dst3[:, k, 0:NPB], in_=accP[i][:, :]), acc_reads(i), [dstkey])
                ew("act", lambda e, i=i, k=k: e.copy(out=dst3[:, k, NPB:T], in_=aS(i)[:, 0:NSB]), acc_reads(i),
                   [dstkey])
            else:
                ew("dve", lambda e, i=i, k=k: e.tensor_copy(out=dst3[:, k, 0:NPB], in_=accP[i][:, :]), acc_reads(i),
                   [dstkey])
                ew("dve", lambda e, i=i, k=k: e.tensor_copy(out=dst3[:, k, NPB:T], in_=aS(i)[:, 0:NSB]),
                   acc_reads(i), [dstkey])

        def group(g):
            S.alias(["R"], [("Rf", k) for k in range(KC)])
            S.alias(STG_KEYS[2:], ATT_KEYS + PHA_KEYS + VR_KEYS)
            if g == 0:
                mnT = M
                for mt in range(2):
                    for h in range(2):
                        s = (mt * 2 + h) % 2
                        S.dma("sp", lambda e, mt=mt, h=h, s=s: e.dma_start(
                            out=xt[:, s, :], in_=mem[mt * P:(mt + 1) * P, h * 1024:(h + 1) * 1024]),
                            writes=[("xt", s)])
                        ew("act", lambda e, s=s, mt=mt, h=h: e.activation(out=tz[:, 0:512], in_=xt[:, s, 0:512],
                                                                           func=AF.Square,
                                                                           accum_out=stat[:, 10 + 2 * h:11 + 2 * h]),
                           [("xt", s)], ["tz", ("stat", 0)])
                        ew("act", lambda e, s=s, mt=mt, h=h: e.activation(out=tz[:, 0:512], in_=xt[:, s, 512:1024],
                                                                           func=AF.Square,
                                                                           accum_out=stat[:, 11 + 2 * h:12 + 2 * h]),
                           [("xt", s)], ["tz", ("stat", 0)])
                    ew("dve", lambda e: e.tensor_reduce(out=stat[:, 14:15], in_=stat[:, 10:14],
                                                        axis=mybir.AxisListType.X, op=ALU.add), [("stat", 0)], [("stat", 0)])
                    ew("act", lambda e: e.activation(out=stat[:, 15:16], in_=stat[:, 14:15], func=AF.Sqrt,
                                                      scale=1.0 / D, bias=epsc[:, 0:1]), [("stat", 0), "epsc"], [("stat", 0)])
                    ew("dve", lambda e: e.reciprocal(out=stat[:, 15:16], in_=stat[:, 15:16]), [("stat", 0)], [("stat", 0)])
                    for h in range(2):
                        ew("act", lambda e, h=h: e.activation(out=xt[:, h, :], in_=xt[:, h, :], func=AF.Copy,
                                                               scale=stat[:, 15:16]), [("xt", h), ("stat", 0)], [("xt", h)])
                        for r0 in range(0, 8, 4):
                            mi = next_misc()

                            def f(e, mi=mi, r0=r0, h=h):
                                ins = None
                                for kk in range(4):
                                    ins = e.transpose(out=misc[mi][:, kk * P:(kk + 1) * P],
                                                      in_=xt[:, h, (r0 + kk) * P:(r0 + kk + 1) * P],
                                                      identity=ident_f[:, :])
                                return ins
                            S.op("pe", f, reads=[("xt", h), "ident_f"], writes=[("misc", mi)])
                            k0 = h * 8 + r0
                            ew("dve", lambda e, mi=mi, k0=k0, mt=mt: e.tensor_tensor(
                                out=mnT[:, k0:k0 + 4, mt * P:(mt + 1) * P],
                                in0=misc[mi][:, :].rearrange("p (k t) -> p k t", k=4),
                                in1=gn[:, 4, k0:k0 + 4].unsqueeze(2).to_broadcast([P, 4, P]), op=ALU.mult),
                               [("misc", mi), "gn"], ["M"])


            tiles = [(t * P, P) for t in range(4)] + [(NPB, NSB)]
            hn = 0
            for (r0, rows) in tiles:
                for h in range(2):
                    sl = hn % len(STG)
                    hn += 1
                    S.dma("sp", lambda e, r0=r0, rows=rows, h=h, sl=sl:
                          e.dma_start(out=STG[sl][0:rows, :], in_=xg[g, r0:r0 + rows, h * 1024:(h + 1) * 1024]),
                          writes=[STG_KEYS[sl]])
                    to_feature_major(STG[sl], rows, R, "R", r0, h * 8, 8, STG_KEYS[sl])
                sdst = stat_pb[:, r0:r0 + rows] if r0 < NPB else stat_sb[:, 0:rows]
                for k in range(KC):
                    sl2 = st.get("sqn", 0) % 2
                    st["sqn"] = st.get("sqn", 0) + 1
                    ew("act", lambda e, k=k, sl2=sl2, r0=r0, rows=rows: e.activation(
                        out=sq[:, sl2, 0:rows], in_=R[:, k, r0:r0 + rows], func=AF.Square), ["R"], [("sq", sl2)])
                    S.op("pe", lambda e, k=k, sl2=sl2, rows=rows, sdst=sdst: e.matmul(
                        sdst, lhsT=ones_b[:, :], rhs=sq[:, sl2, 0:rows], start=(k == 0), stop=(k == KC - 1)),
                        reads=[("sq", sl2), "ones_b"], writes=STATKEYS)
            S.alias(PHA_KEYS + VR_KEYS, STG_KEYS[2:] + ATT_KEYS)
            if g == 0:
                mnT = M
                def st_kv(which, s):
                    def fn(wb, wkey):
                        outd = mk_o if which == "k" else mv_o
                        for mt in range(2):
                            i = next_acc()

                            def f(e, i=i, mt=mt):
                                ins = None
                                for k in range(KC):
                                    ins = e.matmul(accP[i][:, :], lhsT=mnT[:, k, mt * P:(mt + 1) * P], rhs=wb[:, k, :],
                                                   start=(k == 0), stop=(k == KC - 1))
                                return ins
                            S.op("pe", f, reads=["M"] + wk(wkey), writes=acc_reads(i))
                            if mt == 0:
                                ks = st["ev"] % 2
                                ew("act", lambda e, i=i, ks=ks: e.copy(out=kst[:, ks, :], in_=accP[i][:, :]),
                                   acc_reads(i), [KSTKEY[ks]])
                                S.dma("sp", lambda e, ks=ks, s=s: e.dma_start(out=outd[:, s * 512:(s + 1) * 512],
                                                                              in_=kst[:, ks, :]),
                                      reads=[KSTKEY[ks]])
                                st["ev"] += 1
                            if which == "v":
                                ew("dve", lambda e, i=i, mt=mt, s=s: e.tensor_copy(
                                    out=Vbf[:, mt, s * 512:(s + 1) * 512], in_=accP[i][:, :]), acc_reads(i), ["Vbf"])
                        if which == "k":
                            for cc in range(4):
                                i = next_acc()

                                def f2(e, i=i, cc=cc):
                                    ins = None
                                    for k in range(KC):
                                        ins = e.matmul(accP[i][:, 0:256], lhsT=wb[:, k, cc * P:(cc + 1) * P],
                                                       rhs=mnT[:, k, 0:256], start=(k == 0), stop=(k == KC - 1))
                                    return ins
                                S.op("pe", f2, reads=["M"] + wk(wkey, slice(cc * P, (cc + 1) * P)), writes=acc_reads(i))
                                ew("dve", lambda e, i=i, cc=cc, s=s: e.tensor_copy(out=KT[:, 4 * s + cc, :],
                                                                                    in_=accP[i][:, 0:256]),
                                   acc_reads(i), ["KT"])
                    return fn
                for s in range(4):
                    add_stage(*([(0, KC, 0, 512, wview(w_k, 0, KC, s * 512, 512))], st_kv("k", s)))
                for s in range(4):
                    add_stage(*([(0, KC, 0, 512, wview(w_v, 0, KC, s * 512, 512))], st_kv("v", s)))

            rmsnorm_fm(0, X, "X", fused=True)
            S.dma("sp", lambda e: e.dma_start(out=lng_bc.unsqueeze(1), in_=lnv[0:1, :].partition_broadcast(P)),
                  writes=LNG_KEYS)
            S.dma("sp", lambda e: e.dma_start(out=lnb_bc.unsqueeze(1), in_=lnv[1:2, :].partition_broadcast(P)),
                  writes=LNB_KEYS)

            vstate = {}

            def st_v0(wb, wkey):
                vstate["wb0"] = wb
                vstate["k0"] = wkey

            def st_v1(wb, wkey):
                wbs = [vstate["wb0"], wb]
                wkeys = [vstate["k0"], wkey]
                for ti, (r0, rows) in enumerate(tiles):
                    vpar = ti % 2
                    vraw = VRAW[vpar]
                    vkey = VR_KEYS[vpar]
                    stat = STATV[vpar]
                    skey = (("stat", 0), vpar)
                    banks = []
                    for h in range(2):
                        i = next_acc()
                        banks.append(i)

                        def f(e, i=i, h=h, r0=r0, rows=rows, ks=range(KC)):
                            ins = None
                            for k in ks:
                                ins = e.matmul(accP[i][0:rows, :], lhsT=X[:, k, r0:r0 + rows], rhs=wbs[h][:, k, :],
                                               start=(k == 0), stop=(k == KC - 1))
                            return ins
                        if st.get("split"):
                            st["split"] = False
                            for k in range(KC):
                                S.op("pe", lambda e, f=f, k=k: f(e, ks=[k]), reads=[("X", k)] + wk(wkeys[h]),
                                     writes=acc_reads(i))
                        else:
                            S.op("pe", f, reads=XKEYS + wk(wkeys[h]), writes=acc_reads(i))
                    for h in range(2):
                        i = banks[h]
                        ew("act", lambda e, i=i, h=h, rows=rows, vraw=vraw, stat=stat: e.activation(
                            out=vraw[0:rows, h * 512:(h + 1) * 512], in_=accP[i][0:rows, :], func=AF.Copy,
                            accum_out=stat[0:rows, h:h + 1]),
                           acc_reads(i), [vkey, skey])
                        ew("act", lambda e, i=i, h=h, rows=rows, vraw=vraw, stat=stat: e.activation(
                            out=tz[0:rows, 0:512], in_=accP[i][0:rows, :], func=AF.Square,
                            accum_out=stat[0:rows, 2 + h:3 + h]),
                           acc_reads(i), ["tz", skey])
                    rs = slice(0, rows)
                    ew("dve", lambda e, rs=rs, vraw=vraw, stat=stat: e.tensor_tensor(out=stat[rs, 4:5], in0=stat[rs, 0:1], in1=stat[rs, 1:2],
                                                                op=ALU.add), [skey], [skey])
                    ew("dve", lambda e, rs=rs, vraw=vraw, stat=stat: e.tensor_tensor(out=stat[rs, 5:6], in0=stat[rs, 2:3], in1=stat[rs, 3:4],
                                                                op=ALU.add), [skey], [skey])
                    ew("dve", lambda e, rs=rs, vraw=vraw, stat=stat: e.tensor_scalar(out=stat[rs, 4:6], in0=stat[rs, 4:6], scalar1=1.0 / AW,
                                                                scalar2=None, op0=ALU.mult), [skey], [skey])
                    ew("dve", lambda e, rs=rs, vraw=vraw, stat=stat: e.tensor_tensor(out=stat[rs, 6:7], in0=stat[rs, 4:5], in1=stat[rs, 4:5],
                                                                op=ALU.mult), [skey], [skey])
                    ew("dve", lambda e, rs=rs, vraw=vraw, stat=stat: e.tensor_tensor(out=stat[rs, 7:8], in0=stat[rs, 5:6], in1=stat[rs, 6:7],
                                                                op=ALU.subtract), [skey], [skey])
                    ew("act", lambda e, rs=rs, vraw=vraw, stat=stat: e.activation(out=stat[rs, 8:9], in_=stat[rs, 7:8], func=AF.Sqrt,
                                                             scale=1.0, bias=epsc[rs, 0:1]),
                       [skey, "epsc"], [skey])
                    ew("dve", lambda e, rs=rs, vraw=vraw, stat=stat: e.reciprocal(out=stat[rs, 8:9], in_=stat[rs, 8:9]), [skey], [skey])
                    ew("dve", lambda e, rs=rs, vraw=vraw, stat=stat: e.scalar_tensor_tensor(out=stat[rs, 9:10], in0=stat[rs, 4:5],
                                                                       scalar=-1.0, in1=stat[rs, 8:9], op0=ALU.mult,
                                                                       op1=ALU.mult), [skey], [skey])
                    ew("dve", lambda e, rs=rs, vraw=vraw, stat=stat: e.tensor_scalar(
                        out=vraw[rs, :], in0=vraw[rs, :], scalar1=stat[rs, 8:9], scalar2=stat[rs, 9:10],
                        op0=ALU.mult, op1=ALU.add), [vkey, skey], [vkey])
                    ew("dve", lambda e, rs=rs, vraw=vraw: e.tensor_tensor(out=vraw[rs, :], in0=vraw[rs, :],
                                                                           in1=lng_bc[rs, :], op=ALU.mult),
                       [vkey] + LNG_KEYS, [vkey])
                    if ti != 4:
                        ew("dve", lambda e, rs=rs, ti=ti, vraw=vraw: e.tensor_tensor(
                            out=vbf[rs, ti, :], in0=vraw[rs, :], in1=lnb_bc[rs, :], op=ALU.add),
                           [vkey] + LNB_KEYS, ["M"])
                    else:
                        ew("dve", lambda e, rs=rs, vraw=vraw: e.tensor_tensor(out=vraw[rs, :], in0=vraw[rs, :],
                                                                               in1=lnb_bc[rs, :], op=ALU.add),
                           [vkey] + LNB_KEYS, [vkey])
                        ew("act", lambda e, rs=rs, ti=ti, vraw=vraw: e.copy(out=vbf[rs, ti, :], in_=vraw[rs, :]),
                           [vkey], ["M"])
                    if ti == 4:
                        S.dma("sp", lambda e, vraw=vraw: e.dma_start(out=chunkv_o[g, :, :], in_=vraw[0:NS, :]), reads=[vkey])

            add_stage(*([(0, KC, 0, 512, wview(w_in, 0, KC, OFF_V, 512))], st_v0, "defer"))
            add_stage(*([(0, KC, 0, 512, wview(w_in, 0, KC, OFF_V + 512, 512))], st_v1))

            def st_alpha(q):
                def fn(wb, wkey):
                    if q == 0:
                        S.alias(PHA_KEYS, VR_KEYS)
                    for cc in range(2):
                        c = 2 * q + cc
                        i = lin_chunk_b(X, XKEYS, wb, wkey, KC, slice(cc * P, (cc + 1) * P))
                        evac_copy(i, Csb, cc, ("Csb", cc))
                        i = lin_chunk_b(X, XKEYS, wb, wkey, KC, slice(256 + cc * P, 256 + (cc + 1) * P))
                        rd = acc_reads(i) + [("Csb", cc)]
                        samp = pbuf[:, 2 + NPB:2 + NPB + 80].rearrange("p (b t) -> p b t", t=10)
                        ew("dve", lambda e, i=i, cc=cc: e.tensor_tensor(out=pbuf[:, 2:2 + NPB], in0=accP[i][:, :],
                                                                         in1=Csb[:, cc, 0:NPB], op=ALU.mult),
                           rd, ["pbuf"])
                        ew("dve", lambda e, i=i, cc=cc: e.tensor_tensor(out=pbuf[:, 0:2], in0=aS(i)[:, NS:NSB],
                                                                         in1=Csb[:, cc, NPB + NS:T], op=ALU.mult),
                           rd, ["pbuf"])
                        ew("dve", lambda e, i=i, cc=cc, samp=samp: e.tensor_tensor(
                            out=samp[:, :, 2:10], in0=aS(i)[:, 0:NS].rearrange("p (b t) -> p b t", t=8),
                            in1=Csb[:, cc, NPB:NPB + NS].rearrange("p (b t) -> p b t", t=8), op=ALU.mult),
                           rd, ["pbuf"])
                        ew("act", lambda e, c=c, samp=samp: e.copy(
                            out=samp[:, :, 0:2],
                            in_=sconvT[:, c, :].rearrange("p (b k) -> p b k", k=2)[:, 8 * g:8 * g + 8, :]),
                           ["sconvT"], ["pbuf"])
                        cvp = convb[:, cc, 0:NPB]
                        cvs = convb[:, cc, NPB:NPB + NS].rearrange("p (b t) -> p b t", t=8)
                        ew("act", lambda e, c=c, cvp=cvp: e.activation(out=cvp, in_=pbuf[:, 0:NPB], func=AF.Copy,
                                                                        scale=cw[:, c, 0:1]),
                           ["pbuf", "cw"], [("convb", cc)])
                        ew("act", lambda e, c=c, cvs=cvs, samp=samp: e.activation(out=cvs, in_=samp[:, :, 0:8],
                                                                                   func=AF.Copy, scale=cw[:, c, 0:1]),
                           ["pbuf", "cw"], [("convb", cc)])
                        for kk in (1, 2):
                            ew("dve", lambda e, c=c, cvp=cvp, kk=kk: e.scalar_tensor_tensor(
                                out=cvp, in0=pbuf[:, kk:kk + NPB], scalar=cw[:, c, kk:kk + 1], in1=cvp,
                                op0=ALU.mult, op1=ALU.add), ["pbuf", "cw", ("convb", cc)], [("convb", cc)])
                            ew("dve", lambda e, c=c, cvs=cvs, kk=kk, samp=samp: e.scalar_tensor_tensor(
                                out=cvs, in0=samp[:, :, kk:kk + 8], scalar=cw[:, c, kk:kk + 1], in1=cvs,
                                op0=ALU.mult, op1=ALU.add), ["pbuf", "cw", ("convb", cc)], [("convb", cc)])
                        if g == NGROUPS - 1:
                            ew("act", lambda e, c=c: e.copy(out=pl[:, c, :], in_=pbuf[:, NPB:NPB + 2]),
                               ["pbuf"], ["pl"])
                        ew("act", lambda e, c=c, samp=samp: e.copy(out=pss[:, c, 8 * g:8 * g + 8, :],
                                                                    in_=samp[:, :, 8:10]), ["pbuf"], ["pss"])
                return fn

            def st_beta(q):
                def fn(wb, wkey):
                    for cc in range(2):
                        c = 2 * q + cc
                        gi = c // 2
                        i = lin_chunk_b(X, XKEYS, wb, wkey, KC, slice(cc * P, (cc + 1) * P))
                        ew("dve", lambda e, i=i, cc=cc, c=c: e.tensor_tensor(out=Y[:, 8 + c, 0:NPB], in0=accP[i][:, :],
                                                                              in1=convb[:, cc, 0:NPB], op=ALU.mult),
                           acc_reads(i) + [("convb", cc)], ["Y"])
                        ew("dve", lambda e, i=i, cc=cc, c=c: e.tensor_tensor(out=Y[:, 8 + c, NPB:NPB + NS],
                                                                              in0=aS(i)[:, 0:NS],
                                                                              in1=convb[:, cc, NPB:NPB + NS],
                                                                              op=ALU.mult),
                           acc_reads(i) + [("convb", cc)], ["Y"])
                        i = lin_chunk_b(X, XKEYS, wb, wkey, KC, slice(256 + cc * P, 256 + (cc + 1) * P))
                        ew("act", lambda e, i=i: e.copy(out=usb[:, 0:NPB], in_=accP[i][:, :]), acc_reads(i), ["usb"])
                        ew("act", lambda e, i=i: e.copy(out=usb[:, NPB:T], in_=aS(i)[:, 0:NSB]), acc_reads(i),
                           ["usb"])
                        i = next_acc()

                        def fz(e, i=i, c=c, gi=gi):
                            for t4 in range(4):
                                e.matmul(accP[i][:, t4 * P:(t4 + 1) * P], lhsT=vbf[:, t4, c * P:(c + 1) * P],
                                         rhs=WsT[:, gi, :], start=True, stop=True)
                            return e.matmul(aS(i)[:, 0:NS], lhsT=vbf[0:NS, 4, c * P:(c + 1) * P],
                                            rhs=WsTs[:, gi, :], start=True, stop=True)
                        S.op("pe", fz, reads=["M", "WsT", "WsTs"], writes=acc_reads(i))
                        ew("dve", lambda e, i=i, gi=gi: e.tensor_tensor(
                            out=tz[:, 0:NPB].rearrange("p (a t) -> p a t", t=P),
                            in0=accP[i][:, :].rearrange("p (a t) -> p a t", t=P),
                            in1=bsp_bc[:, gi:gi + 1, :].to_broadcast([P, 4, P]), op=ALU.add),
                           acc_reads(i) + ["bsp_bc"], ["tz"])
                        ew("dve", lambda e, i=i, gi=gi: e.tensor_tensor(
                            out=tz[:, NPB:NPB + NS].rearrange("p (b t) -> p b t", t=8),
                            in0=aS(i)[:, 0:NS].rearrange("p (b t) -> p b t", t=8),
                            in1=bsp_bc[:, gi:gi + 1, 0:8].to_broadcast([P, 8, 8]), op=ALU.add),
                           acc_reads(i) + ["bsp_bc"], ["tz"])
                        ew("dve", lambda e, c=c: e.tensor_tensor(out=Y[:, c, 0:NPB + NS], in0=tz[:, 0:NPB + NS],
                                                                  in1=usb[:, 0:NPB + NS], op=ALU.mult),
                           ["tz", "usb"], ["Y"])
                return fn

            for q in range(4):
                add_stage(*([(0, KC, 0, 256, wview(w_in, 0, KC, OFF_C + q * 256, 256)),
                                (0, KC, 256, 256, wview(w_in, 0, KC, OFF_XIN + q * 256, 256))], st_alpha(q)))
                add_stage(*([(0, KC, 0, 256, wview(w_in, 0, KC, OFF_B + q * 256, 256)),
                                (0, KC, 256, 256, wview(w_in, 0, KC, OFF_U + q * 256, 256))], st_beta(q)))

            def st_gamma(jj):
                def fn(wb, wkey):
                    for s4 in range(4):
                        i = lin_chunk_b(X, XKEYS, wb, wkey, KC, slice(s4 * P, (s4 + 1) * P))
                        ew("act", lambda e, i=i, s4=s4: e.activation(out=G[:, s4, 0:NPB], in_=accP[i][:, :],
                                                                      func=AF.Sigmoid), acc_reads(i), [("G", s4)])
                        ew("act", lambda e, i=i, s4=s4: e.activation(out=G[:, s4, NPB:T], in_=aS(i)[:, 0:NSB],
                                                                      func=AF.Sigmoid), acc_reads(i), [("G", s4)])
                return fn

            def st_delta(jj):
                def fn(wb, wkey):
                    for cc in range(2):
                        j = 2 * jj + cc
                        i = next_acc()
                        tp = [(wb[:, k, cc * P:(cc + 1) * P], Y[:, k, 0:NPB]) for k in range(8)]
                        ts = [(wb[:, k, cc * P:(cc + 1) * P], Y[:, k, NPB:T]) for k in range(8)]
                        mm_group(i, tp, ts, reads=["Y"] + wk(wkey))
                        ew("dve", lambda e, i=i, cc=cc: e.tensor_tensor(out=G[:, cc, 0:NPB], in0=accP[i][:, :],
                                                                         in1=G[:, cc, 0:NPB], op=ALU.mult),
                           acc_reads(i) + [("G", cc)], [("G", cc)])
                        ew("dve", lambda e, i=i, cc=cc: e.tensor_tensor(out=G[:, cc, NPB:T], in0=aS(i)[:, 0:NSB],
                                                                         in1=G[:, cc, NPB:T], op=ALU.mult),
                           acc_reads(i) + [("G", cc)], [("G", cc)])
                        i = next_acc()
                        tp = [(wb[:, k, 256 + cc * P:256 + (cc + 1) * P], Y[:, 8 + k, 0:NPB]) for k in range(8)]
                        ts = [(wb[:, k, 256 + cc * P:256 + (cc + 1) * P], Y[:, 8 + k, NPB:T]) for k in range(8)]
                        mm_group(i, tp, ts, reads=["Y"] + wk(wkey))
                        ew("dve", lambda e, i=i, cc=cc: e.tensor_tensor(out=G[:, 2 + cc, 0:NPB], in0=accP[i][:, :],
                                                                         in1=G[:, 2 + cc, 0:NPB], op=ALU.mult),
                           acc_reads(i) + [("G", 2 + cc)], [("G", 2 + cc)])
                        ew("dve", lambda e, i=i, cc=cc: e.tensor_tensor(out=G[:, 2 + cc, NPB:T],
                                                                         in0=aS(i)[:, 0:NSB],
                                                                         in1=G[:, 2 + cc, NPB:T], op=ALU.mult),
                           acc_reads(i) + [("G", 2 + cc)], [("G", 2 + cc)])
                        ew("dve", lambda e, j=j, cc=cc: e.tensor_tensor(out=M[:, j, :], in0=G[:, cc, :],
                                                                          in1=G[:, 2 + cc, :], op=ALU.add),
                           [("G", cc), ("G", 2 + cc)], ["M"])
                return fn

            for jj in range(8):
                add_stage(*([(0, KC, 0, 256, wview(w_in, 0, KC, OFF_GA + jj * 256, 256)),
                                (0, KC, 256, 256, wview(w_in, 0, KC, OFF_GB + jj * 256, 256))], st_gamma(jj)))
                add_stage(*([(0, 8, 0, 256, wview(w_a, 0, 8, jj * 256, 256)),
                                (0, 8, 256, 256, wview(w_b, 0, 8, jj * 256, 256))], st_delta(jj)))

            def st_resid(src, srckey, stats=False):
                def mk(s):
                    def fn(wb, wkey):
                        for cc in range(4):
                            if srckey == "Y" and s == 0 and cc == 0:
                                i = next_acc()
                                colsl = slice(0, P)
                                for k in range(KC):
                                    def fk(e, k=k, i=i, colsl=colsl):
                                        e.matmul(accP[i][:, :], lhsT=wb[:, k, colsl], rhs=src[:, k, 0:NPB],
                                                 start=(k == 0), stop=(k == KC - 1))
                                        return e.matmul(aS(i)[:, 0:NSB], lhsT=wb[:, k, colsl], rhs=src[:, k, NPB:T],
                                                        start=(k == 0), stop=(k == KC - 1))
                                    S.op("pe", fk, reads=[("Ya", k)] + wk(wkey, colsl), writes=acc_reads(i))
                            else:
                                i = lin_chunk_b(src, [srckey], wb, wkey, KC, slice(cc * P, (cc + 1) * P))
                            stats_flush()
                            resid_add(i, 4 * s + cc)
                            if stats:
                                stats_push(4 * s + cc)
                    return fn
                return mk

            mkfn = st_resid(M, "M", stats=True)
            for s in range(4):
                add_stage(*([(0, KC, 0, 512, wview(w_mix, 0, KC, s * 512, 512))], mkfn(s)))
            run_stages()
            if DEBUG_DUMP == "h1" and g == 0:
                dump_R()

            rmsnorm_fm(1, X, "X", fused=True)
            Xflat = X[:, :, :].rearrange("p k t -> p (k t)")
            kvr = [scrB[:, 0:2048].bitcast(BF16).rearrange("p (m d) -> p m d", m=2)] + \
                  [Xflat[:, r * 4096:(r + 1) * 4096].rearrange("p (m d) -> p m d", m=2) for r in range(2)]
            S.alias([("kbX", 0)], PHA_KEYS + VR_KEYS + STG_KEYS[2:])
            S.dma("pool", lambda e: e.dma_start(out=kvr[0], in_=ck[8 * g, :, :].rearrange("(m p) d -> p m d", p=P)),
                  writes=[("kbX", 0)])

            def st_q(s):
                def fn(wb, wkey):
                    for cc in range(4):
                        i = lin_chunk_b(X, XKEYS, wb, wkey, KC, slice(cc * P, (cc + 1) * P))
                        evac_copy(i, Y, 4 * s + cc, "Y")
                return fn
            for s in range(4):
                add_stage(*([(0, KC, 0, 512, wview(w_q, 0, KC, s * 512, 512))], st_q(s)),
                          *(["noahead"] if s == 3 else []))

            run_stages()

            S.alias(ATT_KEYS, PHA_KEYS + VR_KEYS)
            S.alias([("kbX", 1), ("kbX", 2)], XKEYS)
            S.dma("pool", lambda e: e.dma_start(out=kvr[1],
                                                in_=ck[8 * g + 1, :, :].rearrange("(m p) d -> p m d", p=P)),
                  writes=[("kbX", 1)])
            S.dma("pool", lambda e: e.dma_start(out=kvr[2],
                                                in_=ck[8 * g + 2, :, :].rearrange("(m p) d -> p m d", p=P)),
                  writes=[("kbX", 2)])
            ABK = [(accP[j], [("accP", j)]) for j in range(4)] + [(accS2[j], [("accS", j)]) for j in range(2)]
            pTd = [pT, pT2]
            rinvd = [(scrA[:, 2, 0:NPB], ("G", 2)), (scrA[:, 3, 0:NPB], ("G", 3))]

            def nbk():
                st["abk"] = st.get("abk", 0) + 1
                return ABK[st["abk"] % len(ABK)]

            def att_scores(h):
                par = h % 2
                for mt in range(2):
                    bank, bkeys = nbk()

                    def f(e, bank=bank, mt=mt, h=h):
                        ins = None
                        for c in range(4):
                            ins = e.matmul(bank[:, :], lhsT=KT[:, 4 * h + c, mt * P:(mt + 1) * P],
                                           rhs=Y[:, 4 * h + c, 0:NPB], start=(c == 0), stop=(c == 3))
                        return ins
                    S.op("pe", f, reads=["KT", "Y"], writes=bkeys)
                    ew("act", lambda e, bank=bank, mt=mt, par=par: e.activation(
                        out=pTd[par][:, mt, :], in_=bank[:, :], func=AF.Exp, scale=SCALE), bkeys, [("pT", par)])

            def att_rest(h):
                par = h % 2
                pTh = pTd[par]
                rv, rkey = rinvd[par]
                bank, bkeys = nbk()

                def fden(e, bank=bank, pTh=pTh):
                    e.matmul(bank[:, :], lhsT=ones_b[:, :], rhs=pTh[:, 0, :], start=True, stop=False)
                    return e.matmul(bank[:, :], lhsT=ones_b[:, :], rhs=pTh[:, 1, :], start=False, stop=True)
                S.op("pe", fden, reads=[("pT", par), "ones_b"], writes=bkeys)
                ew("dve", lambda e, bank=bank, rv=rv: e.reciprocal(out=rv, in_=bank[:, :]), bkeys, [rkey])
                for c in range(4):
                    bank, bkeys = nbk()

                    def fo(e, bank=bank, c=c, h=h, pTh=pTh):
                        e.matmul(bank[:, :], lhsT=Vbf[:, 0, (4 * h + c) * P:(4 * h + c + 1) * P], rhs=pTh[:, 0, :],
                                 start=True, stop=False)
                        return e.matmul(bank[:, :], lhsT=Vbf[:, 1, (4 * h + c) * P:(4 * h + c + 1) * P],
                                        rhs=pTh[:, 1, :], start=False, stop=True)
                    S.op("pe", fo, reads=[("pT", par), "Vbf"], writes=bkeys)
                    ew("dve", lambda e, bank=bank, c=c, h=h, rv=rv: e.tensor_tensor(
                        out=M[:, 4 * h + c, 0:NPB], in0=bank[:, :], in1=rv, op=ALU.mult), bkeys + [rkey], ["M"])

            att_scores(0)
            for h in range(4):
                if h + 1 < 4:
                    att_scores(h + 1)
                att_rest(h)

            Sall = misc[0][:, :].rearrange("p (m h t) -> p m h t", m=2, h=4)
            trb = [(miscb[:, :], "miscb"), (accP[3][:, :].bitcast(BF16), ("accP", 3))]
            Oall = [accP[0][:, :].rearrange("p (j t) -> p j t", j=8), accP[1][:, :].rearrange("p (j t) -> p j t", j=8)]
            OKEYS = [("accP", 0), ("accP", 1)]

            def ld_kv(src, seq, r):
                S.dma("pool", lambda e, seq=seq, r=r: e.dma_start(
                    out=kvr[r], in_=src[seq, :, :].rearrange("(m p) d -> p m d", p=P)), writes=[("kbX", r)])

            bfree = (mode["si"] + NBUF - 1) % NBUF
            wfl = wbuf[:, bfree, :, :].rearrange("p k c -> p (k c)")
            vvr = [wfl[:, r * 4096:(r + 1) * 4096].rearrange("p (m d) -> p m d", m=2) for r in range(2)]
            VKEYS = [("vring", 0), ("vring", 1)]
            S.alias(VKEYS, [("wbuf", bfree, 0), ("wbuf", bfree, 1)])

            def ld_v(seq, r):
                S.dma("pool", lambda e, seq=seq, r=r: e.dma_start(
                    out=vvr[r], in_=cv[seq, :, :].rearrange("(m p) d -> p m d", p=P)), writes=[VKEYS[r]])

            items = [(b, h) for b in range(8) for h in range(4)]
            SB2 = [(misc[0], ("misc", 0)), (accP[2], ("accP", 2))]
            OB2 = [(accP[0], ("accP", 0)), (accP[1], ("accP", 1))]

            def Sreg(b):
                return SB2[b % 2][0][:, 0:64].rearrange("p (m h t) -> p m h t", m=2, h=4)

            def emit_T(n):
                b, h = items[n]
                r = b % 3
                tb, tkey = trb[n % 2]

                def ftr(e, r=r, h=h, tb=tb):
                    ins = None
                    for c in range(4):
                        for mt in range(2):
                            ins = e.transpose(out=tb[:, c * 256 + mt * P:c * 256 + (mt + 1) * P],
                                              in_=kvr[r][:, mt, (4 * h + c) * P:(4 * h + c + 1) * P],
                                              identity=ident_b[:, :])
                    return ins
                S.op("pe", ftr, reads=[("kbX", r), "ident_b"], writes=[tkey])
                tr = n % 2
                eng = ev_eng()
                if eng == "act":
                    ew("act", lambda e, tr=tr, tb=tb: e.copy(out=kbT[:, tr, :, :].rearrange("p c m -> p (c m)"),
                                                             in_=tb), [tkey], [("kbT", tr)])
                else:
                    ew("dve", lambda e, tr=tr, tb=tb: e.tensor_copy(
                        out=kbT[:, tr, :, :].rearrange("p c m -> p (c m)"), in_=tb), [tkey], [("kbT", tr)])

            def emit_S(n):
                b, h = items[n]
                tr = n % 2
                sreg = Sreg(b)

                def fsc(e, tr=tr, b=b, h=h, sreg=sreg):
                    ins = None
                    for mt in range(2):
                        for c in range(4):
                            ins = e.matmul(sreg[:, mt, h, :], lhsT=kbT[:, tr, c, mt * P:(mt + 1) * P],
                                           rhs=Y[:, 4 * h + c, NPB + 8 * b:NPB + 8 * b + 8], start=(c == 0),
                                           stop=(c == 3))
                    return ins
                S.op("pe", fsc, reads=[("kbT", tr), "Y"], writes=[SB2[b % 2][1]])

            def seq_exp(b):
                skey = SB2[b % 2][1]
                ew("act", lambda e, b=b: e.activation(out=pTs[:, :, :, 8 * b:8 * b + 8], in_=Sreg(b), func=AF.Exp,
                                                      scale=SCALE), [skey], [("pTs", b)])

            def seq_rest(b):
                r = b % 2
                sbank, skey = SB2[b % 2]
                obank, okey = OB2[b % 2]
                den = sbank[:, 64:96]
                oreg = obank[:, 0:128].rearrange("p (j t) -> p j t", j=16)

                def fden(e, b=b, den=den):
                    e.matmul(den, lhsT=ones_b[:, :], rhs=pTs[:, 0, :, 8 * b:8 * b + 8], start=True, stop=False)
                    return e.matmul(den, lhsT=ones_b[:, :], rhs=pTs[:, 1, :, 8 * b:8 * b + 8], start=False, stop=True)
                S.op("pe", fden, reads=[("pTs", b), "ones_b"], writes=[skey])
                ew("dve", lambda e, b=b, den=den: e.reciprocal(out=rinvs[:, :, 8 * b:8 * b + 8],
                                                               in_=den.rearrange("p (h t) -> p h t", h=4)),
                   [skey], [("rinvs", b)])

                def fpv(e, r=r, b=b, oreg=oreg):
                    ins = None
                    for j in range(16):
                        for mt in range(2):
                            ins = e.matmul(oreg[:, j, :], lhsT=vvr[r][:, mt, j * P:(j + 1) * P],
                                           rhs=pTs[:, mt, j // 4, 8 * b:8 * b + 8], start=(mt == 0), stop=(mt == 1))
                    return ins
                S.op("pe", fpv, reads=[VKEYS[r], ("pTs", b)], writes=[okey])
                if b + 2 < 8:
                    ld_v(8 * g + b + 2, r)
                ew("dve", lambda e, b=b, oreg=oreg: e.tensor_tensor(
                    out=M[:, :, NPB + 8 * b:NPB + 8 * b + 8].rearrange("p (h c) t -> p h c t", h=4),
                    in0=oreg.rearrange("p (h c) t -> p h c t", h=4),
                    in1=rinvs[:, :, 8 * b:8 * b + 8].unsqueeze(2).to_broadcast([P, 4, 4, 8]), op=ALU.mult),
                   [okey, ("rinvs", b)], ["M"])

            emit_T(0)
            ld_v(8 * g + 0, 0)
            ld_v(8 * g + 1, 1)
            pend_seq = []
            for n in range(len(items)):
                b0, h0 = items[n]
                if n + 1 < len(items):
                    b1, h1 = items[n + 1]
                    emit_T(n + 1)
                    if h1 == 3 and b1 + 3 < 8:
                        ld_kv(ck, 8 * g + b1 + 3, b1 % 3)
                emit_S(n)
                if h0 == 0 and pend_seq:
                    seq_rest(pend_seq.pop(0))
                if h0 == 3:
                    seq_exp(b0)
                    pend_seq.append(b0)
            while pend_seq:
                seq_rest(pend_seq.pop(0))

            S.alias(XKEYS, [("kbX", 1), ("kbX", 2)])
            S.alias([("wbuf", bfree, 0), ("wbuf", bfree, 1)], VKEYS)


            mkfn = st_resid(M, "M", stats=True)
            for s in range(4):
                add_stage(*([(0, KC, 0, 512, wview(w_xo, 0, KC, s * 512, 512))], mkfn(s)))
            run_stages()
            if DEBUG_DUMP == "h2" and g == 0:
                dump_R()

            rmsnorm_fm(2, X, "X", fused=True)

            def st_up(fg, s):
                def fn(wb, wkey):
                    for cc in range(4):
                        j = 4 * s + cc
                        i = lin_chunk_b(X, XKEYS, wb, wkey, KC, slice(cc * P, (cc + 1) * P))
                        ew("act", lambda e, i=i: e.activation(out=usb[:, 0:NPB], in_=accP[i][:, :], func=AF.Square),
                           acc_reads(i), ["usb"])
                        ew("act", lambda e, i=i: e.activation(out=usb[:, NPB:T], in_=aS(i)[:, 0:NSB],
                                                               func=AF.Square), acc_reads(i), ["usb"])
                        ew("dve", lambda e, i=i, j=j: e.scalar_tensor_tensor(out=Y[:, j, 0:NPB], in0=accP[i][:, :],
                                                                              scalar=0.0, in1=usb[:, 0:NPB],
                                                                              op0=ALU.is_gt, op1=ALU.mult),
                           acc_reads(i) + ["usb"], ["Y", ("Ya", j)])
                        ew("dve", lambda e, i=i, j=j: e.scalar_tensor_tensor(out=Y[:, j, NPB:T],
                                                                              in0=aS(i)[:, 0:NSB], scalar=0.0,
                                                                              in1=usb[:, NPB:T], op0=ALU.is_gt,
                                                                              op1=ALU.mult),
                           acc_reads(i) + ["usb"], ["Y", ("Ya", j)])
                return fn

            for fg in range(4):
                mkfn = st_resid(Y, "Y", stats=(fg == 3))
                for s in range(4):
                    add_stage(*([(0, KC, 0, 512, wview(w_up, 0, KC, fg * 2048 + s * 512, 512))], st_up(fg, s)))
                for s in range(4):
                    add_stage(*([(0, KC, 0, 512, wview(w_down, fg * 2048, KC, s * 512, 512))], mkfn(s)))
            run_stages()
            if DEBUG_DUMP == "h3" and g == 0:
                dump_R()

            rms_rbc()
            for k in range(KC):
                ew("dve", lambda e, k=k: e.scalar_tensor_tensor(out=R[:, k, :], in0=R[:, k, :],
                                                                 scalar=gn[:, 3, k:k + 1], in1=rbc[:, :],
                                                                 op0=ALU.mult, op1=ALU.mult),
                   ["R", "usb", "gn"], ["R", ("Rf", k)])
            S.alias(STG_KEYS[2:], ATT_KEYS + PHA_KEYS + VR_KEYS)
            otiles = [(t * P, P) for t in range(4)] + [(NPB, NS)]
            hn2 = 0
            for (r0, rows) in otiles:
                for h in range(2):
                    s = hn2 % len(STG)
                    hn2 += 1
                    for r4 in range(0, 8, 4):
                        tb, tkey = next_tr()

                        def f(e, tb=tb, r0=r0, rows=rows, h=h, r4=r4):
                            ins = None
                            for kk in range(4):
                                ins = e.transpose(out=tb[0:rows, kk * P:(kk + 1) * P],
                                                  in_=R[:, h * 8 + r4 + kk, r0:r0 + rows], identity=ident_f[:, :])
                            return ins
                        S.op("pe", f, reads=[("Rf", h * 8 + r4 + kk) for kk in range(4)] + ["ident_f"], writes=[tkey])
                        eng = ev_eng()
                        o_ = STG[s][0:rows, r4 * P:(r4 + 4) * P]
                        i_ = tb[0:rows, :]
                        if eng == "act":
                            ew("act", lambda e, o_=o_, i_=i_: e.copy(out=o_, in_=i_), [tkey], [STG_KEYS[s]])
                        else:
                            ew("dve", lambda e, o_=o_, i_=i_: e.tensor_copy(out=o_, in_=i_), [tkey],
                               [STG_KEYS[s]])
                    S.dma("sp", lambda e, r0=r0, rows=rows, h=h, s=s: e.dma_start(
                        out=y_o[g, r0:r0 + rows, h * 1024:(h + 1) * 1024], in_=STG[s][0:rows, :]),
                        reads=[STG_KEYS[s]])

        def dump_R():
            for k in range(KC):
                S.dma("sp", lambda e, k=k: e.dma_start(out=dbg_o[k, :, :], in_=R[:, k, :]), reads=["R"])

        S.dry = True
        for g in range(NGROUPS):
            group(g)
        S.dry = False
        st.clear()
        st.update({"acc": 0, "misc": 0, "ev": 0})
        for g in range(NGROUPS):
            group(g)

        for hh in range(2):
            def fcs(e, hh=hh):
                ins = None
                for c4 in range(4):
                    c = hh * 4 + c4
                    ins = e.transpose(out=misc[0][0:32, c4 * P:(c4 + 1) * P],
                                      in_=pss[:, c, :, :].rearrange("p b k -> p (b k)"), identity=ident_f[:, :])
                return ins
            S.op("pe", fcs, reads=["pss", "ident_f"], writes=[("misc", 0)])
            ew("dve", lambda e, hh=hh: e.tensor_copy(out=cst[:, hh * 512:(hh + 1) * 512], in_=misc[0][0:32, :]),
               [("misc", 0)], [("xt", 1)])
        S.dma("sp", lambda e: e.dma_start(out=convs_o[:, :], in_=cst), reads=[("xt", 1)])
        for hh in range(2):
            def fcp(e, hh=hh):
                ins = None
                for c4 in range(4):
                    c = hh * 4 + c4
                    ins = e.transpose(out=misc[0][0:2, c4 * P:(c4 + 1) * P], in_=pl[:, c, :], identity=ident_f[:, :])
                return ins
            S.op("pe", fcp, reads=["pl", "ident_f"], writes=[("misc", 0)])
            ew("dve", lambda e, hh=hh: e.tensor_copy(out=kst[0:2, hh, :], in_=misc[0][0:2, :]), [("misc", 0)],
               [KSTKEY[hh]])
        S.dma("sp", lambda e: e.dma_start(out=convp_o[:, :].rearrange("r (h n) -> r h n", h=2), in_=kst[0:2, :, :]),
              reads=list(KSTKEY))

        final_waits = {}
        for slot, uses in S.dma_uses.items():
            final_waits[slot] = 16 * uses

        def emit(engname, e):
            for (waits, fn, semname, inc) in S.streams[engname]:
                for (sn, v) in waits:
                    e.wait_ge(sems[sn], v)
                ins = fn(e)
                ins.then_inc(sems[semname], inc)

        @block.tensor
        def _(e):
            emit("pe", e)

        @block.scalar
        def _(e):
            emit("act", e)

        @block.vector
        def _(e):
            emit("dve", e)

        @block.gpsimd
        def _(e):
            emit("pool", e)
            for slot, v in final_waits.items():
                if slot.startswith("pool"):
                    e.wait_ge(sems[slot], v)

        @block.sync
        def _(e):
            emit("sp", e)
            for slot, v in final_waits.items():
                if slot.startswith("sp"):
                    e.wait_ge(sems[slot], v)
    return nc


_CACHE = {}


def _program():
    if "nc" not in _CACHE:
        _CACHE["nc"] = build_program()
    return _CACHE["nc"]


def _make_in_maps(x_prompt, x_sample, state_conv, cache_mem_k, cache_mem_v, mem_prompt,
           norm_mix_g, w_in, ln_v_g, ln_v_b, w_spatial, b_spatial, conv_w,
           w_branch_a, w_branch_b, w_mix_out, norm_x_g, norm_mem_g, w_q, w_k, w_v,
           w_x_out, norm_mlp_g, w_up, w_down, norm_final_g):
    f = np.float32
    A = lambda a: np.ascontiguousarray(np.asarray(a, dtype=f))
    x_prompt, x_sample = A(x_prompt), A(x_sample)
    state_conv, cache_mem_k, cache_mem_v, mem_prompt = A(state_conv), A(cache_mem_k), A(cache_mem_v), A(mem_prompt)

    def fm(gv):
        return np.asarray(gv, dtype=f).reshape(KC, P).T

    gains = np.ascontiguousarray(np.stack([fm(norm_mix_g[0]), fm(norm_x_g[0]), fm(norm_mlp_g[0]),
                                           fm(norm_final_g), fm(norm_mem_g[0])], axis=1))
    cwt = np.ascontiguousarray(np.asarray(conv_w[0], dtype=f).reshape(3, 8, P).transpose(2, 1, 0))
    lnv = np.ascontiguousarray(np.stack([np.asarray(ln_v_g[0], dtype=f), np.asarray(ln_v_b[0], dtype=f)]))
    ws = np.asarray(w_spatial[0], dtype=f)
    wsp = np.ascontiguousarray(ws.transpose(1, 0, 2))
    wsb = np.zeros((NS, 4, NS), dtype=f)
    for b in range(8):
        wsb[8 * b:8 * b + 8, :, 8 * b:8 * b + 8] = ws[:, :8, :8].transpose(1, 0, 2)
    bsp = np.ascontiguousarray(np.asarray(b_spatial[0], dtype=f).reshape(1, 4 * P))
    ident = np.eye(P, dtype=f)
    tril = np.tril(np.ones((P, P), dtype=f))
    shared = dict(w_in=A(w_in[0]), w_a=A(w_branch_a[0]), w_b=A(w_branch_b[0]), w_mix=A(w_mix_out[0]),
                  w_q=A(w_q[0]), w_k=A(w_k[0]), w_v=A(w_v[0]), w_xo=A(w_x_out[0]), w_up=A(w_up[0]),
                  w_down=A(w_down[0]), gains=gains, cwt=cwt, lnv=lnv, wsp=wsp, wsb=wsb, bsp=bsp, ident=ident,
                  tril=tril)
    in_maps = []
    for c in range(NCORES):
        b, half = c // 2, c % 2
        xg = np.zeros((NGROUPS, T, D), dtype=f)
        for g in range(NGROUPS):
            p0 = half * 1024 + g * 512
            xg[g, 0:NPB] = x_prompt[b, p0:p0 + NPB]
            xg[g, NPB:NPB + NS] = x_sample[16 * c + 8 * g:16 * c + 8 * g + 8].reshape(NS, D)
            if p0 >= 2:
                xg[g, NPB + NS:T] = x_prompt[b, p0 - 2:p0]
        m = dict(shared)
        m["xg"] = xg
        m["sconv"] = np.ascontiguousarray(state_conv[0, 16 * c:16 * c + 16].reshape(32, AW))
        m["ck"] = np.ascontiguousarray(cache_mem_k[0, 16 * c:16 * c + 16].reshape(16, 256, D))
        m["cv"] = np.ascontiguousarray(cache_mem_v[0, 16 * c:16 * c + 16].reshape(16, 256, D))
        m["mem"] = np.ascontiguousarray(np.concatenate(
            [mem_prompt[b, half * P:(half + 1) * P], mem_prompt[b, (1 - half) * P:(2 - half) * P]], axis=0))
        in_maps.append(m)

    return in_maps


def _assemble(outs):
    f = np.float32

    y_prompt = np.zeros((4, 2048, D), dtype=f)
    y_sample = np.zeros((128, 8, D), dtype=f)
    mem_k = np.zeros((1, 4, 256, 4, 512), dtype=f)
    mem_v = np.zeros((1, 4, 256, 4, 512), dtype=f)
    conv_p = np.zeros((1, 4, 2, AW), dtype=f)
    conv_s = np.zeros((1, 128, 2, AW), dtype=f)
    chunk_v = np.zeros((1, 128, 8, AW), dtype=f)
    for c in range(NCORES):
        b, half = c // 2, c % 2
        o = outs[c]
        for g in range(NGROUPS):
            p0 = half * 1024 + g * 512
            y_prompt[b, p0:p0 + NPB] = o["y"][g, 0:NPB]
            y_sample[16 * c + 8 * g:16 * c + 8 * g + 8] = o["y"][g, NPB:NPB + NS].reshape(8, 8, D)
            chunk_v[0, 16 * c + 8 * g:16 * c + 8 * g + 8] = o["chunkv"][g].reshape(8, 8, AW)
        mem_k[0, b, half * P:(half + 1) * P] = o["mk"].reshape(P, 4, 512)
        mem_v[0, b, half * P:(half + 1) * P] = o["mv"].reshape(P, 4, 512)
        if half == 1:
            conv_p[0, b] = o["convp"]
        conv_s[0, 16 * c:16 * c + 16] = o["convs"].reshape(16, 2, AW)
    return (y_prompt, y_sample, mem_k, mem_v, conv_p, conv_s, chunk_v)


def kernel(**inputs):
    in_maps = _make_in_maps(**inputs)
    nc = _program()
    res = run_bass_kernel_spmd(nc, in_maps, core_ids=list(range(NCORES)))
    return _assemble(res.results)
```
